# Optimizing a Trainium2 kernel written in Bass

```python
import jax, jax.numpy as jnp
from jax import lax
import numpy as np

D_MODEL = 1024
BATCH = 8
SEQ = 4096
DEPTH = 4

GRID_W = 64
CTX_LEN = 256
N_MIXERS = 2
EXPAND = 2
D_INNER = EXPAND * D_MODEL
HG_HEAD_DIM = 128
HG_HEADS = D_INNER // HG_HEAD_DIM
HG_CHUNK = 32
N_HG_LAYERS = (DEPTH + 1) // 2
MLA_NOPE_DIM = 128
MLA_ROPE_DIM = 64
MLA_V_DIM = 128
MLA_HEADS = D_INNER // MLA_V_DIM
MLA_Q_RANK = D_MODEL // 4
MLA_KV_RANK = D_MODEL // 8
MLA_SCALE = (MLA_NOPE_DIM + MLA_ROPE_DIM) ** -0.5
N_MLA_LAYERS = DEPTH // 2
ROPE_THETA = 10000.0
ROPE_AXIS_DIM = MLA_ROPE_DIM // 2
Q_BLOCK = 128
NORM_EPS = 1e-6

kernel_name = "hybrid_hgrn2_mla_dit_trunk"


def _rmsnorm(x, g):
    xf = x.astype(jnp.float32)
    y = xf * lax.rsqrt(jnp.mean(xf * xf, axis=-1, keepdims=True) + NORM_EPS)
    return (y * g.astype(jnp.float32)).astype(x.dtype)


def _axial_rope(n_tokens):
    rows = n_tokens // GRID_W
    row = jnp.repeat(jnp.arange(rows, dtype=jnp.float32), GRID_W)
    col = jnp.tile(jnp.arange(GRID_W, dtype=jnp.float32), rows)
    inv = ROPE_THETA ** (-jnp.arange(0, ROPE_AXIS_DIM, 2, dtype=jnp.float32) / ROPE_AXIS_DIM)
    ang = jnp.concatenate([row[:, None] * inv, col[:, None] * inv], axis=-1)
    return jnp.cos(ang), jnp.sin(ang)


def _apply_rope(x, cos, sin):
    half = x.shape[-1] // 2
    x1 = x[..., :half].astype(jnp.float32)
    x2 = x[..., half:].astype(jnp.float32)
    return jnp.concatenate([x1 * cos - x2 * sin, x1 * sin + x2 * cos], axis=-1).astype(x.dtype)


def _hgrn2_scan(q, k, v, log_f, s0):
    B, L, H, _ = q.shape
    DV = v.shape[-1]
    n = L // HG_CHUNK

    def chunks(t):
        return jnp.moveaxis(t.astype(jnp.float32).reshape(B, n, HG_CHUNK, H, t.shape[-1]), 1, 0)

    tri = jnp.tril(jnp.ones((HG_CHUNK, HG_CHUNK), dtype=bool))[None, :, :, None, None]

    def step(S, xs):
        qc, kc, vc, lfc = xs
        b = jnp.cumsum(lfc, axis=1)
        b_last = b[:, -1]
        o_inter = jnp.einsum('bthk,bhkv->bthv', qc * jnp.exp(b), S)
        decay = jnp.exp(jnp.where(tri, b[:, :, None] - b[:, None, :], -jnp.inf))
        scores = jnp.einsum('bthk,btshk,bshk->bhts', qc, decay, kc)
        o_intra = jnp.einsum('bhts,bshv->bthv', scores, vc)
        S_new = jnp.exp(b_last)[..., None] * S + jnp.einsum(
            'bshk,bshv->bhkv', kc * jnp.exp(b_last[:, None] - b), vc)
        return S_new, o_inter + o_intra

    S, o = lax.scan(step, s0, (chunks(q), chunks(k), chunks(v), chunks(log_f)))
    return jnp.moveaxis(o, 0, 1).reshape(B, L, H, DV), S


def _hgrn2_bidir(q, v, gates, init_states):
    (k_f, lf_f), (k_b, lf_b) = gates
    rev = lambda t: jnp.flip(t, axis=1)
    o_f, s_f = _hgrn2_scan(q, k_f, v, lf_f, init_states[0])
    o_b, s_b = _hgrn2_scan(rev(q), rev(k_b), rev(v), rev(lf_b), init_states[1])
    return o_f + rev(o_b), (s_f, s_b)


def _hgrn2_mixer(h_lat, h_ctx, w_in, lb, o_gain, w_out):
    def project(h):
        B, L, _ = h.shape
        heads = lambda t: t.reshape(B, L, HG_HEADS, HG_HEAD_DIM)
        q, i_in, f_fw, f_bw, z = jnp.split(h @ w_in, 5, axis=-1)
        gates = []
        for d, g in enumerate((f_fw, f_bw)):
            lbd = lb[d]
            gf = g.astype(jnp.float32)
            log_f = jnp.logaddexp(jnp.log(lbd), jnp.log1p(-lbd) + jax.nn.log_sigmoid(gf))
            k = (1.0 - lbd) * jax.nn.sigmoid(-gf)
            gates.append((heads(k), heads(log_f)))
        return heads(jax.nn.silu(q)), heads(i_in), gates, z

    def readout(o, z):
        B, L = o.shape[:2]
        o = _rmsnorm(o, o_gain).reshape(B, L, D_INNER).astype(z.dtype)
        return (o * jax.nn.silu(z)) @ w_out

    Bsz = h_lat.shape[0]
    s0 = jnp.zeros((Bsz, HG_HEADS, HG_HEAD_DIM, HG_HEAD_DIM), jnp.float32)
    q_c, v_c, gates_c, z_c = project(h_ctx)
    o_c, ctx_states = _hgrn2_bidir(q_c, v_c, gates_c, (s0, s0))
    q_l, v_l, gates_l, z_l = project(h_lat)
    o_l, _ = _hgrn2_bidir(q_l, v_l, gates_l, ctx_states)
    return readout(o_l, z_l), readout(o_c, z_c)


def _attend(q_nope, q_pe, k_nope, k_pe, v):
    s = (jnp.einsum('bqhd,bkhd->bhqk', q_nope, k_nope, preferred_element_type=jnp.float32)
         + jnp.einsum('bqhr,bkr->bhqk', q_pe, k_pe, preferred_element_type=jnp.float32))
    p = jax.nn.softmax(s * MLA_SCALE, axis=-1)
    return jnp.einsum('bhqk,bkhd->bqhd', p.astype(v.dtype), v)


def _blocked_attend(q_nope, q_pe, k_nope, k_pe, v):
    B, L, H, _ = q_nope.shape
    n = L // Q_BLOCK
    blk = lambda t: jnp.moveaxis(t.reshape(B, n, Q_BLOCK, H, t.shape[-1]), 1, 0)
    o = lax.map(lambda qs: _attend(qs[0], qs[1], k_nope, k_pe, v), (blk(q_nope), blk(q_pe)))
    return jnp.moveaxis(o, 0, 1).reshape(B, L, H, v.shape[-1])


def _mla_mixer(h_lat, h_ctx, w_in, qa_norm, w_qb, kva_norm, w_kvb, w_out, cos, sin, need_ctx):
    splits = [MLA_Q_RANK, MLA_Q_RANK + MLA_KV_RANK, MLA_Q_RANK + MLA_KV_RANK + MLA_ROPE_DIM]

    def queries(q_a):
        B, L, _ = q_a.shape
        q = (_rmsnorm(q_a, qa_norm) @ w_qb).reshape(B, L, MLA_HEADS, MLA_NOPE_DIM + MLA_ROPE_DIM)
        return q[..., :MLA_NOPE_DIM], q[..., MLA_NOPE_DIM:]

    def keys_values(kv_a):
        B, L, _ = kv_a.shape
        kv = (_rmsnorm(kv_a, kva_norm) @ w_kvb).reshape(B, L, MLA_HEADS, MLA_NOPE_DIM + MLA_V_DIM)
        return kv[..., :MLA_NOPE_DIM], kv[..., MLA_NOPE_DIM:]

    def readout(o, z):
        B, L = o.shape[:2]
        return (o.reshape(B, L, D_INNER) * jax.nn.silu(z)) @ w_out

    q_a, kv_a, k_pe, z = jnp.split(h_lat @ w_in, splits, axis=-1)
    q_nope, q_pe = queries(q_a)
    q_pe = _apply_rope(q_pe, cos[None, :, None, :], sin[None, :, None, :])
    k_pe = _apply_rope(k_pe, cos[None], sin[None])
    k_nope, v = keys_values(kv_a)

    if need_ctx:
        qc_a, kvc_a, kc_pe, z_c = jnp.split(h_ctx @ w_in, splits, axis=-1)
    else:
        kvc_a, kc_pe = jnp.split(h_ctx @ w_in[:, splits[0]:splits[2]], [MLA_KV_RANK], axis=-1)
    kc_nope, v_c = keys_values(kvc_a)

    o_lat = _blocked_attend(q_nope, q_pe,
                            jnp.concatenate([kc_nope, k_nope], axis=1),
                            jnp.concatenate([kc_pe, k_pe], axis=1),
                            jnp.concatenate([v_c, v], axis=1))
    y_lat = readout(o_lat, z)
    y_ctx = None
    if need_ctx:
        qc_nope, qc_pe = queries(qc_a)
        y_ctx = readout(_attend(qc_nope, qc_pe, kc_nope, kc_pe, v_c), z_c)
    return y_lat, y_ctx


def setup_inputs(seed: int = 0) -> dict:
    key = jax.random.key(seed)
    ks = jax.random.split(key, 20)
    f32 = jnp.float32

    def w(k, shape, fan_in):
        return jax.random.normal(k, shape, f32) * fan_in ** -0.5

    def gain(k, shape):
        return 1.0 + 0.1 * jax.random.normal(k, shape, f32)

    mla_in = MLA_Q_RANK + MLA_KV_RANK + MLA_ROPE_DIM + D_INNER
    return {
        "x": jax.random.normal(ks[0], (BATCH, SEQ, D_MODEL), f32),
        "c": jax.random.normal(ks[1], (BATCH, D_MODEL), f32),
        "ctx": jax.random.normal(ks[2], (BATCH, CTX_LEN, D_MODEL), f32),
        "c_ctx": jax.random.normal(ks[3], (D_MODEL,), f32),
        "ada_w": w(ks[4], (DEPTH, D_MODEL, 3 * D_MODEL), D_MODEL),
        "ada_b": 0.02 * jax.random.normal(ks[5], (DEPTH, 3 * D_MODEL), f32),
        "norm_pre": gain(ks[6], (DEPTH, D_MODEL)),
        "norm_post": gain(ks[7], (DEPTH, D_MODEL)),
        "hg_w_in": w(ks[8], (N_HG_LAYERS, D_MODEL, 5 * D_INNER), D_MODEL),
        "hg_lb_logits": jax.random.normal(ks[9], (N_HG_LAYERS, 2, D_INNER), f32),
        "hg_o_norm": gain(ks[10], (N_HG_LAYERS, HG_HEAD_DIM)),
        "hg_w_out": w(ks[11], (N_HG_LAYERS, D_INNER, D_MODEL), D_INNER),
        "mla_w_in": w(ks[12], (N_MLA_LAYERS, D_MODEL, mla_in), D_MODEL),
        "mla_qa_norm": gain(ks[13], (N_MLA_LAYERS, MLA_Q_RANK)),
        "mla_w_qb": w(ks[14], (N_MLA_LAYERS, MLA_Q_RANK, MLA_HEADS * (MLA_NOPE_DIM + MLA_ROPE_DIM)), MLA_Q_RANK),
        "mla_kva_norm": gain(ks[15], (N_MLA_LAYERS, MLA_KV_RANK)),
        "mla_w_kvb": w(ks[16], (N_MLA_LAYERS, MLA_KV_RANK, MLA_HEADS * (MLA_NOPE_DIM + MLA_V_DIM)), MLA_KV_RANK),
        "mla_w_out": w(ks[17], (N_MLA_LAYERS, D_INNER, D_MODEL), D_INNER),
    }


def reference(x, c, ctx, c_ctx, ada_w, ada_b, norm_pre, norm_post,
              hg_w_in, hg_lb_logits, hg_o_norm, hg_w_out,
              mla_w_in, mla_qa_norm, mla_w_qb, mla_kva_norm, mla_w_kvb, mla_w_out):
    cos, sin = _axial_rope(x.shape[1])
    p = jax.nn.softmax(hg_lb_logits.astype(jnp.float32), axis=0)
    lower_bounds = jnp.cumsum(p, axis=0) - p[0:1]
    silu_c = jax.nn.silu(c)
    silu_cc = jax.nn.silu(c_ctx)
    x_lat, x_ctx = x, ctx
    for i in range(DEPTH):
        need_ctx = i < DEPTH - 1
        j = i // N_MIXERS
        mod_lat = (silu_c @ ada_w[i] + ada_b[i])[:, None, :]
        mod_ctx = silu_cc @ ada_w[i] + ada_b[i]
        sh_l, sc_l, gt_l = jnp.split(mod_lat, 3, axis=-1)
        sh_c, sc_c, gt_c = jnp.split(mod_ctx, 3, axis=-1)
        h_lat = _rmsnorm(x_lat, norm_pre[i]) * (1.0 + sc_l) + sh_l
        h_ctx = _rmsnorm(x_ctx, norm_pre[i]) * (1.0 + sc_c) + sh_c
        if i % N_MIXERS == 0:
            y_lat, y_ctx = _hgrn2_mixer(h_lat, h_ctx, hg_w_in[j], lower_bounds[j],
                                        hg_o_norm[j], hg_w_out[j])
        else:
            y_lat, y_ctx = _mla_mixer(h_lat, h_ctx, mla_w_in[j], mla_qa_norm[j], mla_w_qb[j],
                                      mla_kva_norm[j], mla_w_kvb[j], mla_w_out[j],
                                      cos, sin, need_ctx)
        x_lat = x_lat + gt_l * _rmsnorm(y_lat, norm_post[i])
        if need_ctx:
            x_ctx = x_ctx + gt_c * _rmsnorm(y_ctx, norm_post[i])
    return x_lat
```

```python
import os
import numpy as np
from contextlib import ExitStack
import concourse.bass as bass
import concourse.mybir as mybir
from concourse.bass_utils import run_bass_kernel_spmd

F32 = mybir.dt.float32
BF16 = mybir.dt.bfloat16
AF = mybir.ActivationFunctionType
ALU = mybir.AluOpType

D = 1024
L = 4096
LC = 256
T = L + LC
NT = T // 128
DI = 2048
NH = 16
DEPTH = 4
EPS = 1e-6
CH = 16
NCH = T // CH
CPB = 128 // CH
PIECES = [(s, min(512, T - s)) for s in range(0, T, 512)]
MLA_SCALE = (128 + 64) ** -0.5
STRICT = True


_UID = [0]


def _sbt(nc, name, shape, dtype):
    _UID[0] += 1
    return nc.sbuf_tensor("%s_u%d" % (name, _UID[0]), shape, dtype)


class Res:
    __slots__ = ("name", "w", "r")

    def __init__(self, name=""):
        self.name = name
        self.w = None
        self.r = {}


class _Eng:
    def __init__(self, key, h, sem, self_sync):
        self.key = key
        self.h = h
        self.sem = sem
        self.self_sync = self_sync
        self.count = 0
        self.waited = {}


class FW:
    def __init__(self, nc, es, ndma=8):
        self.nc = nc
        self.sem = {}
        self.E = {}
        for key, h, ss in (("pe", nc.tensor, False), ("act", nc.scalar, True),
                           ("dve", nc.vector, True), ("pool", nc.gpsimd, True),
                           ("sp", nc.sync, False)):
            s = es.enter_context(nc.semaphore("sem_" + key))
            self.sem[key] = s
            self.E[key] = _Eng(key, h, s, ss)
        self.dq = {}
        for q in ("sp", "pool", "act"):
            keys = []
            for k in range(ndma):
                kk = "dq_%s%d" % (q, k)
                self.sem[kk] = es.enter_context(nc.semaphore(kk))
                keys.append(kk)
            self.dq[q] = [keys, 0]
        self.n_inst = 0

    @staticmethod
    def _deps(eng, r, w, dma=False):
        deps = {}

        def add(kv):
            k, v = kv
            if deps.get(k, 0) < v:
                deps[k] = v
        for t in r:
            if t.w is not None:
                add(t.w)
        strict = dma or (STRICT and eng.self_sync)
        for t in w:
            if t.w is not None and (strict or t.w[0] != eng.key):
                add(t.w)
            for k, v in t.r.items():
                if strict or k != eng.key:
                    add((k, v))
        return deps

    def _wait(self, eng, deps, dma=False):
        for k, v in deps.items():
            if k == eng.key and not (eng.self_sync or dma):
                continue
            if eng.waited.get(k, 0) >= v:
                continue
            eng.h.wait_ge(self.sem[k], v)
            eng.waited[k] = v

    @staticmethod
    def _mark(me, r, w):
        for t in r:
            if t.r.get(me[0], 0) < me[1]:
                t.r[me[0]] = me[1]
        for t in w:
            t.w = me
            t.r = {}

    def op(self, e, fn, r=(), w=(), inc=True):
        eng = self.E[e]
        self._wait(eng, self._deps(eng, r, w))
        inst = fn(eng.h)
        self.n_inst += 1
        if inc:
            eng.count += 1
            inst.then_inc(eng.sem, 1)
            me = (eng.key, eng.count)
        else:
            me = (eng.key, eng.count + 1)
        self._mark(me, r, w)
        return inst

    def dma(self, q, out, in_, r=(), w=(), **kw):
        eng = self.E[q]
        keys, i = self.dq[q]
        P = len(keys)
        key = keys[i % P]
        val = 16 * (i // P + 1)
        deps = self._deps(eng, r, w, dma=True)
        if i >= P and deps.get(key, 0) < val - 16:
            deps[key] = val - 16
        self._wait(eng, deps, dma=True)
        eng.h.dma_start(out=out, in_=in_, **kw).then_inc(self.sem[key], 16)
        self.n_inst += 1
        self.dq[q][1] = i + 1
        self._mark((key, val), r, w)

    def barrier(self):
        deps = {}
        for k, e in self.E.items():
            if e.count > 0:
                deps[k] = e.count
        for q, (keys, i) in self.dq.items():
            P = len(keys)
            for j in range(max(0, i - P), i):
                deps[keys[j % P]] = 16 * (j // P + 1)
        for k, e in self.E.items():
            d = {kk: v for kk, v in deps.items() if kk != k}
            self._wait(e, d, dma=True)


class Prog:
    def __init__(self, n_layers=DEPTH, dbg=False):
        self.n_layers = n_layers
        self.dbg = dbg
        self.nc = nc = bass.Bass("TRN2", target_bir_lowering=False)
        dt = lambda name, shape, dtype=F32, kind="ExternalInput": nc.dram_tensor(name, list(shape), dtype, kind=kind).ap()
        self.x_in = dt("x", [L, D])
        self.ctx_in = dt("ctx", [LC, D])
        self.c_in = dt("c", [D])
        self.cctx_in = dt("c_ctx", [D])
        self.ada_w = dt("ada_w", [DEPTH, D, 3 * D])
        self.ada_b = dt("ada_b", [DEPTH, 3 * D])
        self.norm_pre = dt("norm_pre", [DEPTH, D])
        self.norm_post = dt("norm_post", [DEPTH, D])
        self.hg_w_in = dt("hg_w_in", [2, D, 5 * DI])
        self.hg_lb = dt("hg_lb_logits", [2, 2, DI])
        self.hg_o_norm = dt("hg_o_norm", [2, 128])
        self.hg_w_out = dt("hg_w_out", [2, DI, D])
        self.mla_w_in = dt("mla_w_in", [2, D, 2496])
        self.mla_qa_norm = dt("mla_qa_norm", [2, 256])
        self.mla_w_qb = dt("mla_w_qb", [2, 256, 3072])
        self.mla_kva_norm = dt("mla_kva_norm", [2, 128])
        self.mla_w_kvb = dt("mla_w_kvb", [2, 128, 4096])
        self.mla_w_out = dt("mla_w_out", [2, DI, D])
        self.k_ident = dt("k_ident", [128, 128])
        self.k_maskf = dt("k_maskf", [128, 128])
        self.k_maskb = dt("k_maskb", [128, 128])
        self.k_cm3 = dt("k_cm3", [128, CPB * 128])
        self.k_cos = dt("k_cos", [64, T])
        self.k_sin = dt("k_sin", [64, T])
        self.out = dt("out", [L, D], F32, "ExternalOutput")
        okind = "ExternalOutput" if dbg else "Internal"
        self.xres = dt("xres", [T, D], F32, okind)
        self.ozT = dt("ozT", [NH, 128, T], BF16, "Internal")
        self.zs_d = dt("zs_d", [T, DI], BF16, "Internal")
        self.hT_d = dt("hT_d", [128, 8, T], BF16, "Internal")
        self.build()

    def build(self):
        nc = self.nc
        with ExitStack() as es:
            self.fw = fw = FW(nc, es)
            self.ps = [es.enter_context(nc.psum_tensor("ps%d" % i, [128, 512], F32)) for i in range(8)]
            self.psr = [Res("ps%d" % i) for i in range(8)]
            self.ident = es.enter_context(_sbt(nc, "ident", [128, 128], BF16))
            self.ones_bf = es.enter_context(_sbt(nc, "ones_bf", [128, 128], BF16))
            self.maskf = es.enter_context(_sbt(nc, "maskf", [128, 128], F32))
            self.maskb = es.enter_context(_sbt(nc, "maskb", [128, 128], F32))
            self.epsc = es.enter_context(_sbt(nc, "epsc", [128, 1], F32))
            self.sc2 = es.enter_context(_sbt(nc, "sc2", [128, 8, 2], F32))
            self.AB = es.enter_context(_sbt(nc, "AB", [128, 2, 2, 8], F32))
            self.Gbc = es.enter_context(_sbt(nc, "Gbc", [128, 2, D], F32))
            self.r_const = Res("const")
            self.r_sc2 = Res("sc2")
            self.r_AB = Res("AB")
            self.r_Gbc = Res("Gbc")
            self.r_xres = [Res("xres%d" % i) for i in range(NT)]
            self.r_ozT = [[Res("oz%d_%d" % (h, p)) for p in range(len(PIECES))] for h in range(NH)]
            self.setup_consts()
            for i in range(self.n_layers):
                self.layer(i)
            if self.n_layers < DEPTH:
                self.copy_out()
            fw.barrier()

    def setup_consts(self):
        nc, fw = self.nc, self.fw
        with ExitStack() as es:
            st = es.enter_context(_sbt(nc, "cst_stage", [128, 128], F32))
            cst = es.enter_context(_sbt(nc, "cst_c", [128, 8, 2], F32))
            r_st = Res("st")
            r_c = Res("c")
            fw.dma("sp", st[:], self.k_ident[:, :], w=[r_st])
            fw.op("dve", lambda e: e.tensor_copy(out=self.ident[:], in_=st[:]), r=[r_st], w=[self.r_const])
            fw.op("dve", lambda e: e.memset(self.ones_bf[:], 1.0), w=[self.r_const])
            fw.op("dve", lambda e: e.memset(self.epsc[:], EPS), w=[self.r_const])
            fw.dma("sp", self.maskf[:], self.k_maskf[:, :], w=[self.r_const])
            fw.dma("sp", self.maskb[:], self.k_maskb[:, :], w=[self.r_const])
            fw.dma("sp", cst[:, :, 0], self.c_in.rearrange("(kc p) -> p kc", p=128), w=[r_c],
                   allow_slow_non_contiguous=True)
            fw.dma("sp", cst[:, :, 1], self.cctx_in.rearrange("(kc p) -> p kc", p=128), w=[r_c],
                   allow_slow_non_contiguous=True)
            fw.op("act", lambda e: e.activation(out=self.sc2[:], in_=cst[:], func=AF.Silu), r=[r_c], w=[self.r_sc2])
            for tt in range(NT):
                src = self.ctx_in[tt * 128:(tt + 1) * 128, :] if tt < 2 else self.x_in[(tt - 2) * 128:(tt - 1) * 128, :]
                fw.dma("sp", self.xres[tt * 128:(tt + 1) * 128, :], src, w=[self.r_xres[tt]])
            if os.environ.get("DBG_ZERO_OZ"):
                zt_ = es.enter_context(_sbt(nc, "dbgz", [128, T], BF16))
                rz = Res()
                fw.op("dve", lambda e: e.memset(zt_[:], 0.0), w=[rz])
                for h in range(NH):
                    fw.dma("sp", self.ozT[h], zt_[:], r=[rz], w=self.r_ozT[h])
            fw.barrier()

    def copy_out(self):
        fw = self.fw
        for tt in range(2, NT):
            fw.dma("sp", self.out[(tt - 2) * 128:(tt - 1) * 128, :], self.xres[tt * 128:(tt + 1) * 128, :],
                   r=[self.r_xres[tt]])

    def layer(self, i):
        fw = self.fw
        last = (i == DEPTH - 1)
        self.phase0_mod(i)
        fw.barrier()
        hT = self.hT_d
        r_hT = [Res("hT%d" % t) for t in range(NT)]
        self.phase1_prenorm(i, hT, r_hT)
        if i % 2 == 0:
            self.hgrn_mixer(i // 2, hT, r_hT)
        else:
            self.mla_mixer(i // 2, hT, r_hT, need_ctx=not last)
        fw.barrier()
        w_out = self.hg_w_out[i // 2] if i % 2 == 0 else self.mla_w_out[i // 2]
        self.phase3_out(i, w_out, last)
        fw.barrier()

    def phase0_mod(self, i):
        nc, fw = self.nc, self.fw
        with ExitStack() as es:
            sb = lambda n, s, d=F32: es.enter_context(_sbt(nc, n, s, d))
            wblk = [sb("adaw%d" % k, [128, 8, 512]) for k in range(2)]
            r_wblk = [Res("adaw%d" % k) for k in range(2)]
            screp = sb("screp", [128, 2, 8, 128])
            r_screp = Res("screp")
            adabT = sb("adabT", [128, 24])
            npreT = sb("npreT", [128, 8])
            adab_bc = sb("adab_bc", [128, D])
            npost_bc = sb("npost_bc", [128, D])
            tmpg = sb("tmpg", [128, 512])
            r_small = Res("small")
            r_bc = Res("bc")
            r_tmpg = Res("tmpg")
            fw.dma("sp", adabT[:], self.ada_b[i].rearrange("(g p) -> p g", p=128), w=[r_small],
                   allow_slow_non_contiguous=True)
            fw.dma("sp", npreT[:], self.norm_pre[i].rearrange("(g p) -> p g", p=128), w=[r_small],
                   allow_slow_non_contiguous=True)
            fw.dma("sp", adab_bc[:], self.ada_b[i, 2 * D:3 * D].partition_broadcast(128), w=[r_bc])
            fw.dma("sp", npost_bc[:], self.norm_post[i].partition_broadcast(128), w=[r_bc])
            for wh in range(2):
                for kc in range(8):
                    fw.op("dve", lambda e, wh=wh, kc=kc: e.tensor_copy(
                        out=screp[:, wh, kc, :], in_=self.sc2[:, kc, wh:wh + 1].to_broadcast([128, 128])),
                        r=[self.r_sc2], w=[r_screp])
            psM, r_psM = self.ps[0], self.psr[0]
            for jb in range(6):
                wb, r_wb = wblk[jb % 2], r_wblk[jb % 2]
                fw.dma("sp", wb[:], self.ada_w[i, :, jb * 512:(jb + 1) * 512].rearrange("(kc p) n -> p kc n", p=128),
                       w=[r_wb])
                if jb < 4:
                    for g in range(4):
                        gi = jb * 4 + g
                        for kc in range(8):
                            fw.op("pe", lambda e, gi=gi, g=g, kc=kc, wb=wb: e.matmul(
                                psM[:, 2 * gi:2 * gi + 2], wb[:, kc, g * 128:(g + 1) * 128], self.sc2[:, kc, :],
                                start=(kc == 0), stop=(kc == 7)),
                                r=[r_wb, self.r_sc2], w=[r_psM], inc=(kc == 7))
                else:
                    for wh in range(2):
                        pg, r_pg = self.ps[1 + wh], self.psr[1 + wh]
                        for kc in range(8):
                            fw.op("pe", lambda e, wh=wh, kc=kc, wb=wb, pg=pg: e.matmul(
                                pg[:, :], screp[:, wh, kc, :], wb[:, kc, :], start=(kc == 0), stop=(kc == 7)),
                                r=[r_wb, r_screp], w=[r_pg], inc=(kc == 7))
                        cs = slice((jb - 4) * 512, (jb - 3) * 512)
                        fw.op("dve", lambda e, pg=pg, cs=cs: e.tensor_tensor(
                            out=tmpg[:], in0=pg[:, :], in1=adab_bc[:, cs], op=ALU.add),
                            r=[r_pg, r_bc], w=[r_tmpg])
                        fw.op("dve", lambda e, wh=wh, cs=cs: e.tensor_tensor(
                            out=self.Gbc[:, wh, cs], in0=tmpg[:], in1=npost_bc[:, cs], op=ALU.mult),
                            r=[r_tmpg, r_bc], w=[self.r_Gbc])
            pv = psM[:, 0:32].rearrange("p (g w) -> p g w", w=2)
            for wh in range(2):
                fw.op("dve", lambda e, wh=wh: e.tensor_tensor(
                    out=self.AB[:, wh, 1, :], in0=pv[:, 0:8, wh], in1=adabT[:, 0:8], op=ALU.add),
                    r=[r_psM, r_small], w=[self.r_AB])
                fw.op("dve", lambda e, wh=wh: e.scalar_tensor_tensor(
                    out=self.AB[:, wh, 0, :], in0=pv[:, 8:16, wh], scalar=1.0, in1=adabT[:, 8:16],
                    op0=ALU.add, op1=ALU.add),
                    r=[r_psM, r_small], w=[self.r_AB])
                fw.op("dve", lambda e, wh=wh: e.tensor_tensor(
                    out=self.AB[:, wh, 0, :], in0=self.AB[:, wh, 0, :], in1=npreT[:], op=ALU.mult),
                    r=[self.r_AB, r_small], w=[self.r_AB])
            fw.barrier()

    def phase1_prenorm(self, i, hT, r_hT):
        nc, fw = self.nc, self.fw
        with ExitStack() as es:
            sb = lambda n, s, d=F32: es.enter_context(_sbt(nc, n, s, d))
            NB = 3
            xt = [sb("p1x%d" % k, [128, D]) for k in range(NB)]
            r_xt = [Res() for k in range(NB)]
            xn = [sb("p1xn%d" % k, [128, D], BF16) for k in range(2)]
            r_xn = [Res() for k in range(2)]
            junk = sb("p1junk", [128, D], BF16)
            r_junk = Res()
            st = [sb("p1st%d" % k, [128, 4]) for k in range(2)]
            r_st = [Res() for k in range(2)]
            hst = [sb("p1hs%d" % k, [128, 8, 128], BF16) for k in range(2)]
            r_hst = [Res() for k in range(2)]
            def load_x(tt):
                fw.dma("sp", xt[tt % NB][:], self.xres[tt * 128:(tt + 1) * 128, :], r=[self.r_xres[tt]], w=[r_xt[tt % NB]])
            load_x(0)
            load_x(1)
            for tt in range(NT):
                wh = 1 if tt < 2 else 0
                x_, rx = xt[tt % NB], r_xt[tt % NB]
                xn_, rxn = xn[tt % 2], r_xn[tt % 2]
                s_, rs = st[tt % 2], r_st[tt % 2]
                pT, rpT = self.ps[tt % 2], self.psr[tt % 2]
                pTb = pT[:, :].bitcast(BF16)
                fw.op("act", lambda e, x_=x_, s_=s_: e.activation(
                    out=junk[:], in_=x_[:], func=AF.Square, accum_out=s_[:, 0:1]), r=[rx], w=[r_junk, rs])
                fw.op("act", lambda e, s_=s_: e.activation(
                    out=s_[:, 1:2], in_=s_[:, 0:1], func=AF.Sqrt, scale=1.0 / D, bias=self.epsc[:, 0:1]),
                    r=[rs, self.r_const], w=[rs])
                fw.op("dve", lambda e, s_=s_: e.reciprocal(out=s_[:, 2:3], in_=s_[:, 1:2]), r=[rs], w=[rs])
                fw.op("dve", lambda e, x_=x_, xn_=xn_, s_=s_: e.tensor_scalar(
                    out=xn_[:], in0=x_[:], scalar1=s_[:, 2:3], scalar2=None, op0=ALU.mult), r=[rx, rs], w=[rxn])
                for kc in range(8):
                    fw.op("pe", lambda e, kc=kc, xn_=xn_, pTb=pTb: e.transpose(
                        out=pTb[:, kc * 128:(kc + 1) * 128], in_=xn_[:, kc * 128:(kc + 1) * 128], identity=self.ident[:]),
                        r=[rxn, self.r_const], w=[rpT], inc=(kc == 7))
                rh = r_hT[tt]
                hs, rhs_ = hst[tt % 2], r_hst[tt % 2]
                for kc in range(8):
                    dst = hs[:, kc, :]
                    src = pTb[:, kc * 128:(kc + 1) * 128]
                    if kc % 2 == 0:
                        fw.op("dve", lambda e, dst=dst, src=src, kc=kc, wh=wh: e.tensor_scalar(
                            out=dst, in0=src, scalar1=self.AB[:, wh, 0, kc:kc + 1], scalar2=self.AB[:, wh, 1, kc:kc + 1],
                            op0=ALU.mult, op1=ALU.add), r=[rpT, self.r_AB], w=[rhs_])
                    else:
                        fw.op("act", lambda e, dst=dst, src=src, kc=kc, wh=wh: e.activation(
                            out=dst, in_=src, func=AF.Identity, scale=self.AB[:, wh, 0, kc:kc + 1],
                            bias=self.AB[:, wh, 1, kc:kc + 1]), r=[rpT, self.r_AB], w=[rhs_])
                fw.dma("pool", hT[:, :, tt * 128:(tt + 1) * 128], hs[:], r=[rhs_], w=[rh])
                if tt + 2 < NT:
                    load_x(tt + 2)
            fw.barrier()

    def phase3_out(self, i, w_out, last):
        nc, fw = self.nc, self.fw
        with ExitStack() as es:
            sb = lambda n, s, d=F32: es.enter_context(_sbt(nc, n, s, d))
            wo = sb("p3wo", [128, NH, D], BF16)
            r_wo = Res()
            wst = [sb("p3wst%d" % k, [128, 2, D]) for k in range(2)]
            r_wst = [Res() for k in range(2)]
            for g in range(8):
                s_, rs = wst[g % 2], r_wst[g % 2]
                fw.dma("sp", s_[:], w_out[g * 256:(g + 1) * 256, :].rearrange("(h p) n -> p h n", p=128), w=[rs])
                fw.op("pool", lambda e, s_=s_, g=g: e.tensor_copy(out=wo[:, 2 * g:2 * g + 2, :], in_=s_[:]),
                      r=[rs], w=[r_wo])
            NB = 3
            oz = [sb("p3oz%d" % k, [128, NH, 128], BF16) for k in range(NB)]
            r_oz = [Res() for k in range(NB)]
            xt = [sb("p3x%d" % k, [128, D]) for k in range(NB)]
            r_xt = [Res() for k in range(NB)]
            tmp = [sb("p3t%d" % k, [128, D]) for k in range(NB)]
            r_tmp = [Res() for k in range(NB)]
            junk = sb("p3junk", [128, 512], BF16)
            r_junk = Res()
            st = [sb("p3st%d" % k, [128, 8]) for k in range(NB)]
            r_st = [Res() for k in range(NB)]
            tiles = list(range(2, NT)) if last else list(range(NT))
            NB = 3

            def load_t(n):
                tt = tiles[n]
                k = n % NB
                pc = (tt * 128) // 512
                fw.dma("sp", oz[k][:], self.ozT[:, :, tt * 128:(tt + 1) * 128].rearrange("h v t -> v h t"),
                       r=[self.r_ozT[h][pc] for h in range(NH)], w=[r_oz[k]])
                fw.dma("sp", xt[k][:], self.xres[tt * 128:(tt + 1) * 128, :], r=[self.r_xres[tt]], w=[r_xt[k]])
            load_t(0)
            load_t(1)
            for n, tt in enumerate(tiles):
                wh = 1 if tt < 2 else 0
                k = n % NB
                pb = [(self.ps[2 * (n % 2)], self.psr[2 * (n % 2)]), (self.ps[2 * (n % 2) + 1], self.psr[2 * (n % 2) + 1])]
                for nb in range(2):
                    py, rpy = pb[nb]
                    for h in range(NH):
                        fw.op("pe", lambda e, py=py, h=h, nb=nb, k=k: e.matmul(
                            py[:, :], oz[k][:, h, :], wo[:, h, nb * 512:(nb + 1) * 512], start=(h == 0), stop=(h == NH - 1)),
                            r=[r_oz[k], r_wo], w=[rpy], inc=(h == NH - 1))
                    fw.op("act", lambda e, py=py, nb=nb, k=k: e.activation(
                        out=junk[:], in_=py[:, :], func=AF.Square, accum_out=st[k][:, nb:nb + 1]),
                        r=[rpy], w=[r_junk, r_st[k]])
                fw.op("dve", lambda e, k=k: e.tensor_tensor(
                    out=st[k][:, 2:3], in0=st[k][:, 0:1], in1=st[k][:, 1:2], op=ALU.add), r=[r_st[k]], w=[r_st[k]])
                fw.op("act", lambda e, k=k: e.activation(
                    out=st[k][:, 3:4], in_=st[k][:, 2:3], func=AF.Sqrt, scale=1.0 / D, bias=self.epsc[:, 0:1]),
                    r=[r_st[k], self.r_const], w=[r_st[k]])
                fw.op("dve", lambda e, k=k: e.reciprocal(out=st[k][:, 4:5], in_=st[k][:, 3:4]), r=[r_st[k]], w=[r_st[k]])
                for nb in range(2):
                    py, rpy = pb[nb]
                    cs = slice(nb * 512, (nb + 1) * 512)
                    fw.op("dve", lambda e, py=py, cs=cs, k=k, wh=wh: e.scalar_tensor_tensor(
                        out=tmp[k][:, cs], in0=py[:, :], scalar=st[k][:, 4:5], in1=self.Gbc[:, wh, cs],
                        op0=ALU.mult, op1=ALU.mult), r=[rpy, r_st[k], self.r_Gbc], w=[r_tmp[k]])
                fw.op("pool", lambda e, k=k: e.tensor_tensor(
                    out=tmp[k][:], in0=tmp[k][:], in1=xt[k][:], op=ALU.add), r=[r_tmp[k], r_xt[k]], w=[r_tmp[k]])
                if last:
                    fw.dma("pool", self.out[(tt - 2) * 128:(tt - 1) * 128, :], tmp[k][:], r=[r_tmp[k]])
                else:
                    fw.dma("pool", self.xres[tt * 128:(tt + 1) * 128, :], tmp[k][:], r=[r_tmp[k]], w=[self.r_xres[tt]])
                if n + 2 < len(tiles):
                    load_t(n + 2)

    def hgrn_mixer(self, j, hT, r_hT):
        from_hgrn(self, j, hT, r_hT)

    def mla_mixer(self, j, hT, r_hT, need_ctx):
        from_mla(self, j, hT, r_hT, need_ctx)


HG_COL = {"q": 0, "v": DI, "ff": 2 * DI, "fb": 3 * DI, "z": 4 * DI}
DCLAMP = 40.0


def from_hgrn(P, j, hT, r_hT):
    nc, fw = P.nc, P.fw
    NP = len(PIECES)
    REFI = CH // 2 - 1
    with ExitStack() as es:
        sb = lambda n, s, d=F32: es.enter_context(_sbt(nc, n, s, d))
        cm3 = sb("hg_cm3", [128, CPB, 128], BF16)
        gain = sb("hg_gain", [128, 1])
        lbT = sb("hg_lb", [128, 3, 2, NH])
        ones32 = sb("hg_ones", [128, 512])
        r_lc = Res("hg_lc")
        with ExitStack() as es2:
            st = es2.enter_context(_sbt(nc, "hg_cst", [128, CPB * 128], F32))
            lg = es2.enter_context(_sbt(nc, "hg_lg", [128, 2, 2, NH], F32))
            r_st = Res()
            fw.dma("sp", st[:], P.k_cm3[:, :], w=[r_st])
            fw.op("dve", lambda e: e.tensor_copy(out=cm3[:].rearrange("p c v -> p (c v)"), in_=st[:]), r=[r_st], w=[r_lc])
            fw.op("dve", lambda e: e.memset(ones32[:], 1.0), w=[r_lc])
            fw.dma("sp", gain[:], P.hg_o_norm[j].rearrange("(p o) -> p o", o=1), w=[r_lc])
            if j > 0:
                for jj in range(2):
                    for d in range(2):
                        fw.dma("sp", lg[:, jj, d, :], P.hg_lb[jj, d].rearrange("(h p) -> p h", p=128), w=[r_st],
                               allow_slow_non_contiguous=True)
                fw.op("dve", lambda e: e.tensor_tensor(out=lg[:, 0], in0=lg[:, 1], in1=lg[:, 0], op=ALU.subtract), r=[r_st], w=[r_st])
                fw.op("act", lambda e: e.activation(out=lbT[:, 0], in_=lg[:, 0], func=AF.Sigmoid), r=[r_st], w=[r_lc])
                fw.op("dve", lambda e: e.tensor_scalar(out=lbT[:, 1], in0=lbT[:, 0], scalar1=-1.0, scalar2=1.0,
                                                       op0=ALU.mult, op1=ALU.add), r=[r_lc], w=[r_lc])
                fw.op("dve", lambda e: e.tensor_scalar(out=lbT[:, 2], in0=lbT[:, 0], scalar1=-1.0, scalar2=None,
                                                       op0=ALU.add), r=[r_lc], w=[r_lc])
            fw.barrier()
        Q = [sb("hg_Q%d" % d, [128, T], BF16) for d in range(2)]
        Kt = [sb("hg_K%d" % d, [128, T], BF16) for d in range(2)]
        V = sb("hg_V", [128, NT, 128], BF16)
        zs = sb("hg_zs", [128, T], BF16)
        oacc = sb("hg_oacc", [128, T], BF16)
        Gref = sb("hg_Gref", [128, 2, NCH])
        Wt = sb("hg_W", [128, 2, NCH])
        Gl = sb("hg_Gl", [128, 2])
        Wj = sb("hg_Wj", [128, 2])
        r_Q = [[Res() for p in range(NP)] for d in range(2)]
        r_K = [[Res() for p in range(NP)] for d in range(2)]
        r_V = [Res() for p in range(NP)]
        r_zs = [Res() for p in range(NP)]
        r_oacc = [Res() for p in range(NT)]
        r_Gref, r_W, r_Gl, r_Wj = Res(), Res(), Res(), Res()
        wbf = [sb("hg_wbf%d" % k, [128, 5, 8, 128], BF16) for k in range(2)]
        r_wbf = [Res() for k in range(2)]
        wst = [sb("hg_wst%d" % k, [128, 8, 128]) for k in range(2)]
        r_wst = [Res() for k in range(2)]
        hp = [sb("hg_hp%d" % k, [128, 8, 512], BF16) for k in range(2)]
        r_hp = [Res() for k in range(2)]
        eq = sb("hg_eq", [128, 512])
        ez = sb("hg_ez", [128, 512])
        Ld = [sb("hg_L%d" % d, [128, 512]) for d in range(2)]
        Fd = [sb("hg_F%d" % d, [128, 512]) for d in range(2)]
        E1 = [sb("hg_E1%d" % d, [128, 512], BF16) for d in range(2)]
        E1b = [sb("hg_E1b%d" % d, [128, 512], BF16) for d in range(2)]
        k1 = [sb("hg_k1%d" % d, [128, 512], BF16) for d in range(2)]
        wtmp = [sb("hg_wtmp%d" % d, [128, 512 // CH + 1]) for d in range(2)]
        r_eq, r_ez = Res(), Res()
        r_L, r_F, r_E1, r_E2, r_k1, r_wtmp = [[Res(), Res()] for _ in range(6)]
        AT = [[sb("hg_AT%d%d" % (d, k), [128, 128], BF16) for k in range(3)] for d in range(2)]
        KT = [[sb("hg_KT%d%d" % (d, k), [128, 128], BF16) for k in range(2)] for d in range(2)]
        Vx = [[sb("hg_Vx%d%d" % (d, k), [128, CPB, 128], BF16) for k in range(3)] for d in range(2)]
        Tb = [[sb("hg_Tb%d%d" % (d, k), [128, CPB, 128]) for k in range(2)] for d in range(2)]
        Sp = [[sb("hg_Sp%d%d" % (d, k), [128, CPB, 128], BF16) for k in range(2)] for d in range(2)]
        Qw = [sb("hg_Qw%d" % d, [128, T], BF16) for d in range(2)]
        WtQ = sb("hg_WtQ", [128, 2, NCH + 1])
        r_AT = [[Res() for k in range(3)] for d in range(2)]
        r_KT = [[Res() for k in range(2)] for d in range(2)]
        r_Vx = [[Res() for k in range(3)] for d in range(2)]
        r_Tb = [[Res() for k in range(2)] for d in range(2)]
        r_Sp = [[Res() for k in range(2)] for d in range(2)]
        r_Qw = [[Res() for p in range(NP)] for d in range(2)]
        r_WtQ = Res()
        NRO = 6
        osum = [sb("hg_osum%d" % k, [128, 128]) for k in range(NRO)]
        sq = [sb("hg_sq%d" % k, [128, 128], BF16) for k in range(NRO)]
        rt = [sb("hg_rt%d" % k, [128, 128]) for k in range(NRO)]
        ozp = [sb("hg_ozp%d" % k, [128, 128], BF16) for k in range(NRO)]
        r_osum = [Res() for k in range(NRO)]
        r_sq = [Res() for k in range(NRO)]
        r_rt = [Res() for k in range(NRO)]
        r_ozp = [Res() for k in range(NRO)]
        ps, psr = P.ps, P.psr
        SC_Bs, KT_B, O_B, SS_B = [5, 0], 4, 3, 4
        x_banks = [[6, 7], [1, 2]]

        def load_weights(hd):
            k = hd % 2
            for m, nm in enumerate(("q", "v", "ff", "fb", "z")):
                c0 = HG_COL[nm] + hd * 128
                s_, rs = wst[m % 2], r_wst[m % 2]
                fw.dma("sp", s_[:], P.hg_w_in[j, :, c0:c0 + 128].rearrange("(kc p) n -> p kc n", p=128), w=[rs])
                fw.op("pool", lambda e, s_=s_, m=m, k=k: e.tensor_copy(out=wbf[k][:, m], in_=s_[:]), r=[rs], w=[r_wbf[k]])

        ld_count = [0]

        def load_piece(p):
            s0, n = PIECES[p]
            k = ld_count[0] % 2
            ld_count[0] += 1
            fw.dma("sp", hp[k][:, :, 0:n], hT[:, :, s0:s0 + n], r=[r_hT[t] for t in range(s0 // 128, (s0 + n) // 128)], w=[r_hp[k]])
            return k

        def prep(hd, p, hk):
            k = hd % 2
            s0, n = PIECES[p]
            ntile = n // 128
            nch = n // CH
            c0 = s0 // CH
            wb, rwb = wbf[k], r_wbf[k]
            hpk, rhp = hp[hk], r_hp[hk]
            tsl = slice(s0, s0 + n)
            one = ones32[:, 0:1]
            for m, bank in ((0, 0), (2, 1), (3, 2), (4, 3)):
                for kc in range(8):
                    fw.op("pe", lambda e, m=m, bank=bank, kc=kc: e.matmul(
                        ps[bank][:, 0:n], wb[:, m, kc, :], hpk[:, kc, 0:n], start=(kc == 0), stop=(kc == 7)),
                        r=[rwb, rhp], w=[psr[bank]], inc=(kc == 7))
            for ti in range(ntile):
                for kc in range(8):
                    fw.op("pe", lambda e, ti=ti, kc=kc: e.matmul(
                        ps[4][:, ti * 128:(ti + 1) * 128], hpk[:, kc, ti * 128:(ti + 1) * 128], wb[:, 1, kc, :],
                        start=(kc == 0), stop=(kc == 7)),
                        r=[rwb, rhp], w=[psr[4]], inc=(kc == 7))
            fw.op("act", lambda e: e.activation(out=eq[:, 0:n], in_=ps[0][:, 0:n], func=AF.Exp, scale=-1.0), r=[psr[0]], w=[r_eq])
            for d in range(2):
                fw.op("act", lambda e, d=d: e.activation(out=Ld[d][:, 0:n], in_=ps[1 + d][:, 0:n], func=AF.Exp, scale=-1.0),
                      r=[psr[1 + d]], w=[r_L[d]])
            fw.op("act", lambda e: e.activation(out=ez[:, 0:n], in_=ps[3][:, 0:n], func=AF.Exp, scale=-1.0), r=[psr[3]], w=[r_ez])
            fw.op("dve", lambda e: e.tensor_copy(out=V[:, s0 // 128:s0 // 128 + ntile, :].rearrange("p a v -> p (a v)"),
                                                 in_=ps[4][:, 0:n]), r=[psr[4]], w=[r_V[p]])
            fw.op("act", lambda e: e.activation(out=eq[:, 0:n], in_=eq[:, 0:n], func=AF.Ln, bias=one), r=[r_eq, r_lc], w=[r_eq])
            for d in range(2):
                fw.op("act", lambda e, d=d: e.activation(out=Ld[d][:, 0:n], in_=Ld[d][:, 0:n], func=AF.Ln, bias=one),
                      r=[r_L[d], r_lc], w=[r_L[d]])
            fw.op("act", lambda e: e.activation(out=ez[:, 0:n], in_=ez[:, 0:n], func=AF.Ln, bias=one), r=[r_ez, r_lc], w=[r_ez])
            fw.op("act", lambda e: e.activation(out=eq[:, 0:n], in_=eq[:, 0:n], func=AF.Exp, scale=-1.0), r=[r_eq], w=[r_eq])
            for d in range(2):
                fw.op("act", lambda e, d=d: e.activation(out=Fd[d][:, 0:n], in_=Ld[d][:, 0:n], func=AF.Exp, scale=-1.0),
                      r=[r_L[d]], w=[r_F[d]])
            fw.op("act", lambda e: e.activation(out=ez[:, 0:n], in_=ez[:, 0:n], func=AF.Exp, scale=-1.0), r=[r_ez], w=[r_ez])
            fw.op("dve", lambda e: e.tensor_tensor(out=eq[:, 0:n], in0=ps[0][:, 0:n], in1=eq[:, 0:n], op=ALU.mult),
                  r=[psr[0], r_eq], w=[r_eq])
            fw.op("dve", lambda e: e.tensor_tensor(out=zs[:, tsl], in0=ps[3][:, 0:n], in1=ez[:, 0:n], op=ALU.mult),
                  r=[psr[3], r_ez], w=[r_zs[p]])
            sops = [None, None]
            for d in range(2):
                L_, F_ = Ld[d], Fd[d]
                if j == 0:
                    fw.op("dve", lambda e, d=d, F_=F_: e.tensor_scalar(out=k1[d][:, 0:n], in0=F_[:, 0:n], scalar1=-1.0, scalar2=1.0,
                                                                      op0=ALU.mult, op1=ALU.add), r=[r_F[d]], w=[r_k1[d]])
                    sops[d] = ALU.subtract
                else:
                    fw.op("dve", lambda e, d=d, F_=F_: e.tensor_scalar(out=k1[d][:, 0:n], in0=F_[:, 0:n], scalar1=-1.0,
                                                                      scalar2=lbT[:, 2, d, hd:hd + 1],
                                                                      op0=ALU.add, op1=ALU.mult), r=[r_F[d], r_lc], w=[r_k1[d]])
                    fw.op("dve", lambda e, d=d, F_=F_: e.tensor_scalar(out=F_[:, 0:n], in0=F_[:, 0:n],
                                                                      scalar1=lbT[:, 1, d, hd:hd + 1], scalar2=lbT[:, 0, d, hd:hd + 1],
                                                                      op0=ALU.mult, op1=ALU.add), r=[r_F[d], r_lc], w=[r_F[d]])
                    fw.op("act", lambda e, d=d, F_=F_, L_=L_: e.activation(out=L_[:, 0:n], in_=F_[:, 0:n], func=AF.Ln),
                          r=[r_F[d]], w=[r_L[d]])
                    sops[d] = ALU.add
            for d in range(2):
                L_, F_ = Ld[d], Fd[d]
                init = 0.0 if p == 0 else Gl[:, d:d + 1]
                fw.op("dve", lambda e, init=init, sop=sops[d], L_=L_, F_=F_: e.tensor_tensor_scan(
                    out=F_[:, 0:n], data0=ones32[:, 0:n], data1=L_[:, 0:n], initial=init, op0=ALU.mult, op1=sop),
                    r=[r_L[d], r_lc, r_Gl, r_k1[d]], w=[r_F[d]])
            for d in range(2):
                L_, F_ = Ld[d], Fd[d]
                gv = F_[:, 0:n].rearrange("p (c i) -> p c i", i=CH)
                fw.op("pool", lambda e, d=d, F_=F_: e.tensor_copy(out=Gl[:, d:d + 1], in_=F_[:, n - 1:n]), r=[r_F[d]], w=[r_Gl])
                fw.op("pool", lambda e, d=d, gv=gv: e.tensor_copy(out=Gref[:, d, c0:c0 + nch], in_=gv[:, :, REFI]), r=[r_F[d]], w=[r_Gref])
                if d == 1:
                    fw.op("dve", lambda e, sop=sops[d], L_=L_, F_=F_: e.tensor_tensor(
                        out=F_[:, 0:n], in0=F_[:, 0:n], in1=L_[:, 0:n], op=(ALU.add if sop == ALU.subtract else ALU.subtract)),
                        r=[r_F[d], r_L[d]], w=[r_F[d]])
                fw.op("pool", lambda e, d=d, gv=gv: e.tensor_tensor(
                    out=gv, in0=gv, in1=Gref[:, d, c0:c0 + nch].unsqueeze(2).to_broadcast([128, nch, CH]), op=ALU.subtract),
                    r=[r_F[d], r_Gref], w=[r_F[d]])
            for d in range(2):
                F_ = Fd[d]
                sq_ = 1.0 if d == 0 else -1.0
                fw.op("act", lambda e, d=d, sq_=sq_, F_=F_: e.activation(out=E1[d][:, 0:n], in_=F_[:, 0:n], func=AF.Exp, scale=sq_),
                      r=[r_F[d]], w=[r_E1[d]])
                fw.op("act", lambda e, d=d, sq_=sq_, F_=F_: e.activation(out=E1b[d][:, 0:n], in_=F_[:, 0:n], func=AF.Exp, scale=-sq_),
                      r=[r_F[d]], w=[r_E2[d]])
            for d in range(2):
                fw.op("dve", lambda e, d=d: e.tensor_tensor(out=Q[d][:, tsl], in0=eq[:, 0:n], in1=E1[d][:, 0:n], op=ALU.mult),
                      r=[r_eq, r_E1[d]], w=[r_Q[d][p]])
                fw.op("pool", lambda e, d=d: e.tensor_tensor(out=Kt[d][:, tsl], in0=k1[d][:, 0:n], in1=E1b[d][:, 0:n], op=ALU.mult),
                      r=[r_k1[d], r_E2[d]], w=[r_K[d][p]])
                lo = max(c0 - 1, 0)
                hi = c0 + nch - 1
                fw.op("pool", lambda e, d=d, lo=lo, hi=hi: e.tensor_tensor(
                    out=wtmp[d][:, 0:hi - lo], in0=Gref[:, d, lo + 1:hi + 1], in1=Gref[:, d, lo:hi], op=ALU.subtract),
                    r=[r_Gref], w=[r_wtmp[d]])
                fw.op("act", lambda e, d=d, lo=lo, hi=hi: e.activation(out=Wt[:, d, lo:hi], in_=wtmp[d][:, 0:hi - lo], func=AF.Exp),
                      r=[r_wtmp[d]], w=[r_W])

        def junction():
            fw.op("dve", lambda e: e.tensor_tensor(out=Wj[:, 1:2], in0=Gref[:, 1, 0:1], in1=Gref[:, 1, NCH - 1:NCH], op=ALU.subtract),
                  r=[r_Gref], w=[r_Wj])
            fw.op("act", lambda e: e.activation(out=Wj[:, 0:1], in_=Wj[:, 1:2], func=AF.Exp, bias=Gl[:, 1:2]),
                  r=[r_Wj, r_Gl], w=[r_Wj])
            fw.op("dve", lambda e: e.memset(WtQ[:, 0, 0:1], 1.0), w=[r_WtQ])
            fw.op("dve", lambda e: e.tensor_copy(out=WtQ[:, 0, 1:NCH], in_=Wt[:, 0, 0:NCH - 1]), r=[r_W], w=[r_WtQ])
            fw.op("dve", lambda e: e.tensor_copy(out=WtQ[:, 1, 0:NCH - 1], in_=Wt[:, 1, 0:NCH - 1]), r=[r_W], w=[r_WtQ])
            fw.op("dve", lambda e: e.tensor_copy(out=WtQ[:, 1, NCH - 1:NCH], in_=Wj[:, 0:1]), r=[r_Wj], w=[r_WtQ])
            for p in range(NP):
                s0, n = PIECES[p]
                nch = n // CH
                c0 = s0 // CH
                for d in range(2):
                    fw.op("pool", lambda e, d=d, s0=s0, n=n, nch=nch, c0=c0: e.tensor_tensor(
                        out=Qw[d][:, s0:s0 + n].rearrange("p (c i) -> p c i", i=CH),
                        in0=Q[d][:, s0:s0 + n].rearrange("p (c i) -> p c i", i=CH),
                        in1=WtQ[:, d, c0:c0 + nch].unsqueeze(2).to_broadcast([128, nch, CH]), op=ALU.mult),
                        r=[r_Q[d][p], r_WtQ], w=[r_Qw[d][p]])

        ncc = LC // CH
        orders = [list(range(NCH)), list(range(ncc - 1, -1, -1)) + list(range(NCH - 1, ncc - 1, -1))]

        def blocks_of(order):
            out = []
            for gc in order:
                if not out or out[-1] != gc // CPB:
                    out.append(gc // CPB)
            return out

        def build_vx(d, step, blk):
            sl = step % 3
            p = blk // 4
            fw.op("pool", lambda e: e.tensor_tensor(out=Vx[d][sl][:], in0=V[:, blk:blk + 1, :].to_broadcast([128, CPB, 128]),
                                                    in1=cm3[:], op=ALU.mult),
                  r=[r_V[p], r_lc], w=[r_Vx[d][sl]])

        def stageA_pre(hd, d, step, blk):
            sl = step % 2
            p = blk // 4
            tb = slice(blk * 128, (blk + 1) * 128)
            SC_B = SC_Bs[d]
            scv = ps[SC_B][:, 0:128]
            ktv = ps[KT_B][:, :].bitcast(BF16)[:, 128 * d:128 * d + 128]
            fw.op("pe", lambda e: e.matmul(scv, Kt[d][:, tb], Q[d][:, tb], start=True, stop=True),
                  r=[r_K[d][p], r_Q[d][p]], w=[psr[SC_B]])
            mk = P.maskf if d == 0 else P.maskb
            fw.op("dve", lambda e: e.tensor_tensor(out=AT[d][step % 3][:], in0=scv, in1=mk[:], op=ALU.mult),
                  r=[psr[SC_B], P.r_const], w=[r_AT[d][step % 3]])
            fw.op("pe", lambda e: e.transpose(out=ktv, in_=Kt[d][:, tb], identity=P.ident[:]),
                  r=[r_K[d][p], P.r_const], w=[psr[KT_B]])
            fw.op("act", lambda e: e.copy(out=KT[d][sl][:], in_=ktv), r=[psr[KT_B]], w=[r_KT[d][sl]])

        def stageX(hd, d, step, blk):
            sl = step % 2
            vs = step % 3
            for h in range(CPB // 4):
                xb = x_banks[d][h]
                fw.op("pe", lambda e, xb=xb, h=h: e.matmul(ps[xb][:, :], KT[d][sl][:],
                                                          Vx[d][vs][:, 4 * h:4 * h + 4, :].rearrange("p c v -> p (c v)"),
                                                          start=True, stop=True),
                      r=[r_KT[d][sl], r_Vx[d][vs]], w=[psr[xb]])

        def chain_step(hd, d, pos, gc, state):
            c = gc % CPB
            xb = x_banks[d][c // 4]
            xv = ps[xb][:, (c % 4) * 128:(c % 4 + 1) * 128]
            bb, sl = (pos // CPB) % 2, pos % CPB
            out_ = Tb[d][bb][:, sl, :]
            if pos == 0:
                fw.op("dve", lambda e: e.tensor_copy(out=out_, in_=xv), r=[psr[xb]], w=[r_Tb[d][bb]])
                return
            if d == 0:
                wcol, rw = Wt[:, 0, gc - 1:gc], r_W
            elif gc == NCH - 1:
                wcol, rw = Wj[:, 0:1], r_Wj
            else:
                wcol, rw = Wt[:, 1, gc:gc + 1], r_W
            pb, psl = ((pos - 1) // CPB) % 2, (pos - 1) % CPB
            prev = Tb[d][pb][:, psl, :]
            fw.op("dve", lambda e: e.scalar_tensor_tensor(out=out_, in0=prev, scalar=wcol, in1=xv, op0=ALU.mult, op1=ALU.add),
                  r=[r_Tb[d][pb], rw, psr[xb]], w=[r_Tb[d][bb]])

        def cast_block(d, bidx):
            bb = bidx % 2
            fw.op("act", lambda e: e.copy(out=Sp[d][bb][:].rearrange("p c v -> p (c v)"),
                                          in_=Tb[d][bb][:].rearrange("p c v -> p (c v)")),
                  r=[r_Tb[d][bb]], w=[r_Sp[d][bb]])

        def stageB(hd, d, step, blk):
            sl = step % 2
            p = blk // 4
            ov = ps[O_B][:, 128 * d:128 * d + 128]
            mm = [(ov, V[:, blk, :], AT[d][step % 3][:], [r_V[p], r_AT[d][step % 3]])]
            cord = corder[d]
            for ci in range(CPB):
                pos = step * CPB + ci
                if pos == 0:
                    continue
                c = cord[ci]
                pb, psl = ((pos - 1) // CPB) % 2, (pos - 1) % CPB
                tq = slice(blk * 128 + c * CH, blk * 128 + (c + 1) * CH)
                mm.append((ov[:, c * CH:(c + 1) * CH], Sp[d][pb][:, psl, :], Qw[d][:, tq], [r_Sp[d][pb], r_Qw[d][p]]))
            for n_, (o_, l_, r_, rr) in enumerate(mm):
                fw.op("pe", lambda e, o_=o_, l_=l_, r_=r_, n_=n_: e.matmul(o_, l_, r_, start=(n_ == 0), stop=(n_ == len(mm) - 1)),
                      r=rr, w=[psr[O_B]], inc=(n_ == len(mm) - 1))
            return ov

        def readout1(hd, blk, k):
            fw.op("act", lambda e: e.activation(out=sq[k][:], in_=osum[k][:], func=AF.Square), r=[r_osum[k]], w=[r_sq[k]])
            fw.op("pe", lambda e: e.matmul(ps[SS_B][:, 256:384], P.ones_bf[:], sq[k][:], start=True, stop=True),
                  r=[r_sq[k], P.r_const], w=[psr[SS_B]])
            fw.op("act", lambda e: e.activation(out=rt[k][:], in_=ps[SS_B][:, 256:384], func=AF.Ln, scale=1.0 / 128, bias=P.epsc[:, 0:1]),
                  r=[psr[SS_B], P.r_const], w=[r_rt[k]])
            fw.op("act", lambda e: e.activation(out=rt[k][:], in_=rt[k][:], func=AF.Exp, scale=-0.5), r=[r_rt[k]], w=[r_rt[k]])

        def readout2(hd, blk, k):
            p = blk // 4
            tb = slice(blk * 128, (blk + 1) * 128)
            fw.op("dve", lambda e: e.scalar_tensor_tensor(out=osum[k][:], in0=osum[k][:], scalar=gain[:, 0:1], in1=rt[k][:],
                                                          op0=ALU.mult, op1=ALU.mult), r=[r_osum[k], r_rt[k], r_lc], w=[r_osum[k]])
            fw.op("pool", lambda e: e.tensor_tensor(out=ozp[k][:], in0=osum[k][:], in1=zs[:, tb], op=ALU.mult),
                  r=[r_osum[k], r_zs[p]], w=[r_ozp[k]])
            fw.dma("sp", P.ozT[hd, :, tb], ozp[k][:], r=[r_ozp[k]], w=[P.r_ozT[hd][p]])

        blks = [blocks_of(o) for o in orders]
        nsteps = len(blks[0])
        bpos = [{b: n for n, b in enumerate(bl)} for bl in blks]
        corder = [list(range(CPB)), list(range(CPB - 1, -1, -1))]
        _NHD = int(os.environ.get("HG_NH", NH))
        load_weights(0)
        ridx = 0
        pre_hk = [None]
        for hd in range(_NHD):
            if hd + 1 < NH:
                load_weights(hd + 1)
            hk = pre_hk[0] if pre_hk[0] is not None else load_piece(0)
            pre_hk[0] = None
            for p in range(NP):
                hk_next = load_piece(p + 1) if p + 1 < NP else None
                prep(hd, p, hk)
                hk = hk_next
            if hd + 1 < _NHD:
                pre_hk[0] = load_piece(0)
            junction()
            for d in range(2):
                build_vx(d, 0, blks[d][0])
                build_vx(d, 1, blks[d][1])
                stageA_pre(hd, d, 0, blks[d][0])
            L1, L2 = [], []
            for step in range(nsteps + 1):
                if step < nsteps:
                    for d in range(2):
                        stageX(hd, d, step, blks[d][step])
                    for ci in range(CPB):
                        for d in range(2):
                            blk = blks[d][step]
                            chain_step(hd, d, step * CPB + ci, blk * CPB + corder[d][ci], None)
                    if step + 2 < nsteps:
                        for d in range(2):
                            build_vx(d, step + 2, blks[d][step + 2])
                    if step + 1 < nsteps:
                        for d in range(2):
                            stageA_pre(hd, d, step + 1, blks[d][step + 1])
                run2, L2 = L2, []
                for fn in run2:
                    fn()
                run1, L1 = L1, []
                for (hd_, blk_, k_) in run1:
                    readout1(hd_, blk_, k_)
                    L2.append(lambda hd_=hd_, blk_=blk_, k_=k_: readout2(hd_, blk_, k_))
                if step >= 1:
                    pstep = step - 1
                    for d in range(2):
                        blk = blks[d][pstep]
                        ov = stageB(hd, d, pstep, blk)
                        tb = slice(blk * 128, (blk + 1) * 128)
                        if bpos[d][blk] < bpos[1 - d][blk]:
                            fw.op("act", lambda e, ov=ov, tb=tb: e.copy(out=oacc[:, tb], in_=ov), r=[psr[O_B]], w=[r_oacc[blk]])
                        else:
                            k = ridx % NRO
                            ridx += 1
                            fw.op("dve", lambda e, ov=ov, tb=tb, k=k: e.tensor_tensor(
                                out=osum[k][:], in0=ov, in1=oacc[:, tb], op=ALU.add),
                                r=[psr[O_B], r_oacc[blk]], w=[r_osum[k]])
                            L1.append((hd, blk, k))
                if step < nsteps:
                    for d in range(2):
                        cast_block(d, step)
            for (hd_, blk_, k_) in L1:
                readout1(hd_, blk_, k_)
                L2.append(lambda hd_=hd_, blk_=blk_, k_=k_: readout2(hd_, blk_, k_))
            for fn in L2:
                fn()
        fw.barrier()


def from_mla(P, j, hT, r_hT, need_ctx):
    nc, fw = P.nc, P.fw
    NP = len(PIECES)
    ps, psr = P.ps, P.psr
    W = P.mla_w_in[j]
    QR, KVR, RD = 256, 128, 64
    with ExitStack() as es:
        sb = lambda n, s, d=F32: es.enter_context(_sbt(nc, n, s, d))
        qn = sb("ml_qn", [128, 2, T], BF16)
        kvn = sb("ml_kvn", [128, T], BF16)
        kpeT = sb("ml_kpe", [128, T], BF16)
        r_qn = [Res() for p in range(NP)]
        r_kvn = [Res() for p in range(NP)]
        r_kpe = [Res() for p in range(NP)]
        fw.op("pool", lambda e: e.memset(kpeT[64:128, :], 0.0), w=r_kpe)
        wqb = sb("ml_wqb", [128, 2, NH * 192], BF16)
        wqs = sb("ml_wqs", [128, 2, NH, 64], BF16)
        wkvb = sb("ml_wkvb", [128, NH * 256], BF16)
        qanT = sb("ml_qan", [128, 2])
        kvanT = sb("ml_kvan", [128, 1])
        r_w2 = Res("ml_w2")
        stg = [sb("ml_stg%d" % k, [128, 2048]) for k in range(2)]
        r_stg = [Res(), Res()]
        stg_n = [0]

        def load_cast(dst, src, shape):
            k = stg_n[0] % 2
            stg_n[0] += 1
            n = int(np.prod(shape[1:]))
            sv = stg[k][:, 0:n]
            if len(shape) == 3:
                sv = sv.rearrange("p (a b) -> p a b", b=shape[2])
            fw.dma("sp", sv, src, w=[r_stg[k]])
            return sv, k

        with ExitStack() as es1:
            sb1 = lambda n, s, d=F32: es1.enter_context(_sbt(nc, n, s, d))
            wqa = sb1("ml_wqa", [128, 8, QR], BF16)
            wkva = sb1("ml_wkva", [128, 8, KVR], BF16)
            wkpe = sb1("ml_wkpe", [128, 8, RD], BF16)
            wkpes = sb1("ml_wkpes", [128, 8, RD], BF16)
            wz = sb1("ml_wz", [128, 8, DI], BF16)
            r_w1 = Res("ml_w1")
            for (dst, c0, n) in ((wqa, 0, QR), (wkva, QR, KVR), (wkpe, QR + KVR, RD)):
                sv, k = load_cast(dst, W[:, c0:c0 + n].rearrange("(kc p) n -> p kc n", p=128), [128, 8, n])
                fw.op("pool", lambda e, dst=dst, sv=sv: e.tensor_copy(out=dst[:], in_=sv), r=[r_stg[k]], w=[r_w1])
            fw.op("pool", lambda e: e.tensor_copy(out=wkpes[:, :, 0:32], in_=wkpe[:, :, 32:64]), r=[r_w1], w=[r_w1])
            fw.op("pool", lambda e: e.tensor_copy(out=wkpes[:, :, 32:64], in_=wkpe[:, :, 0:32]), r=[r_w1], w=[r_w1])
            zc0 = QR + KVR + RD
            for cb in range(8):
                sv, k = load_cast(wz, W[:, zc0 + cb * 256:zc0 + (cb + 1) * 256].rearrange("(kc p) n -> p kc n", p=128), [128, 8, 256])
                fw.op("pool", lambda e, sv=sv, cb=cb: e.tensor_copy(out=wz[:, :, cb * 256:(cb + 1) * 256], in_=sv), r=[r_stg[k]], w=[r_w1])
            for cb in range(3):
                sv, k = load_cast(wqb, P.mla_w_qb[j, :, cb * 1024:(cb + 1) * 1024].rearrange("(g p) n -> p g n", p=128), [128, 2, 1024])
                fw.op("pool", lambda e, sv=sv, cb=cb: e.tensor_copy(out=wqb[:, :, cb * 1024:(cb + 1) * 1024], in_=sv), r=[r_stg[k]], w=[r_w2])
            wqbv = wqb[:].rearrange("p g (h c) -> p g h c", c=192)
            fw.op("pool", lambda e: e.tensor_copy(out=wqs[:, :, :, 0:32], in_=wqbv[:, :, :, 160:192]), r=[r_w2], w=[r_w2])
            fw.op("pool", lambda e: e.tensor_copy(out=wqs[:, :, :, 32:64], in_=wqbv[:, :, :, 128:160]), r=[r_w2], w=[r_w2])
            for cb in range(2):
                sv, k = load_cast(wkvb, P.mla_w_kvb[j, :, cb * 2048:(cb + 1) * 2048], [128, 2048])
                fw.op("pool", lambda e, sv=sv, cb=cb: e.tensor_copy(out=wkvb[:, cb * 2048:(cb + 1) * 2048], in_=sv), r=[r_stg[k]], w=[r_w2])
            fw.dma("sp", qanT[:], P.mla_qa_norm[j].rearrange("(g p) -> p g", p=128), w=[r_w2], allow_slow_non_contiguous=True)
            fw.dma("sp", kvanT[:], P.mla_kva_norm[j].rearrange("(p o) -> p o", o=1), w=[r_w2])
            hp = [sb1("ml_hp%d" % k, [128, 8, 512], BF16) for k in range(2)]
            r_hp = [Res(), Res()]
            cosp = [sb1("ml_cos%d" % k, [64, 512]) for k in range(2)]
            sinp = [sb1("ml_sin%d" % k, [64, 512]) for k in range(2)]
            r_cs = [Res(), Res()]
            sqq = sb1("ml_sqq", [128, 2, 512], BF16)
            sqk = sb1("ml_sqk", [128, 512], BF16)
            rq = sb1("ml_rq", [128, 512])
            rk = sb1("ml_rk", [128, 512])
            t1 = sb1("ml_t1", [64, 512])
            t2 = sb1("ml_t2", [64, 512])
            zst = [sb1("ml_zst%d" % k, [128, DI], BF16) for k in range(2)]
            r_sqq, r_sqk, r_rq, r_rk, r_t1, r_t2 = [Res() for _ in range(6)]
            r_zst = [Res(), Res()]
            r_zs_d = [Res() for t in range(NT)]
            P.r_zs_d = r_zs_d
            zcount = 0
            def m1_load(p):
                s0, n = PIECES[p]
                k = p % 2
                fw.dma("sp", hp[k][:, :, 0:n], hT[:, :, s0:s0 + n], r=[r_hT[t] for t in range(s0 // 128, (s0 + n) // 128)], w=[r_hp[k]])
                fw.dma("sp", cosp[k][:, 0:n], P.k_cos[:, s0:s0 + n], w=[r_cs[k]])
                fw.dma("sp", sinp[k][:, 0:n], P.k_sin[:, s0:s0 + n], w=[r_cs[k]])
            m1_load(0)
            for p in range(NP):
                s0, n = PIECES[p]
                k = p % 2
                tsl = slice(s0, s0 + n)
                if p + 1 < NP:
                    m1_load(p + 1)
                hpk = hp[k]
                for g in range(2):
                    for kc in range(8):
                        fw.op("pe", lambda e, g=g, kc=kc: e.matmul(ps[g][:, 0:n], wqa[:, kc, g * 128:(g + 1) * 128], hpk[:, kc, 0:n],
                                                                  start=(kc == 0), stop=(kc == 7)),
                              r=[r_w1, r_hp[k]], w=[psr[g]], inc=(kc == 7))
                for kc in range(8):
                    fw.op("pe", lambda e, kc=kc: e.matmul(ps[2][:, 0:n], wkva[:, kc, :], hpk[:, kc, 0:n], start=(kc == 0), stop=(kc == 7)),
                          r=[r_w1, r_hp[k]], w=[psr[2]], inc=(kc == 7))
                for (wt_, bank) in ((wkpe, 3), (wkpes, 4)):
                    for kc in range(8):
                        fw.op("pe", lambda e, kc=kc, wt_=wt_, bank=bank: e.matmul(ps[bank][0:64, 0:n], wt_[:, kc, :], hpk[:, kc, 0:n],
                                                                                start=(kc == 0), stop=(kc == 7)),
                              r=[r_w1, r_hp[k]], w=[psr[bank]], inc=(kc == 7))
                for g in range(2):
                    fw.op("act", lambda e, g=g: e.activation(out=sqq[:, g, 0:n], in_=ps[g][:, 0:n], func=AF.Square), r=[psr[g]], w=[r_sqq])
                fw.op("act", lambda e: e.activation(out=sqk[:, 0:n], in_=ps[2][:, 0:n], func=AF.Square), r=[psr[2]], w=[r_sqk])
                for g in range(2):
                    fw.op("pe", lambda e, g=g: e.matmul(ps[5][:, 0:n], P.ones_bf[:], sqq[:, g, 0:n], start=(g == 0), stop=(g == 1)),
                          r=[r_sqq, P.r_const], w=[psr[5]], inc=(g == 1))
                fw.op("pe", lambda e: e.matmul(ps[6][:, 0:n], P.ones_bf[:], sqk[:, 0:n], start=True, stop=True),
                      r=[r_sqk, P.r_const], w=[psr[6]])
                fw.op("act", lambda e: e.activation(out=rq[:, 0:n], in_=ps[5][:, 0:n], func=AF.Sqrt, scale=1.0 / QR, bias=P.epsc[:, 0:1]),
                      r=[psr[5], P.r_const], w=[r_rq])
                fw.op("act", lambda e: e.activation(out=rk[:, 0:n], in_=ps[6][:, 0:n], func=AF.Sqrt, scale=1.0 / KVR, bias=P.epsc[:, 0:1]),
                      r=[psr[6], P.r_const], w=[r_rk])
                fw.op("dve", lambda e: e.reciprocal(out=rq[:, 0:n], in_=rq[:, 0:n]), r=[r_rq], w=[r_rq])
                fw.op("dve", lambda e: e.reciprocal(out=rk[:, 0:n], in_=rk[:, 0:n]), r=[r_rk], w=[r_rk])
                for g in range(2):
                    fw.op("dve", lambda e, g=g: e.scalar_tensor_tensor(out=qn[:, g, tsl], in0=ps[g][:, 0:n], scalar=qanT[:, g:g + 1],
                                                                      in1=rq[:, 0:n], op0=ALU.mult, op1=ALU.mult),
                          r=[psr[g], r_rq, r_w2], w=[r_qn[p]])
                fw.op("dve", lambda e: e.scalar_tensor_tensor(out=kvn[:, tsl], in0=ps[2][:, 0:n], scalar=kvanT[:, 0:1],
                                                              in1=rk[:, 0:n], op0=ALU.mult, op1=ALU.mult),
                      r=[psr[2], r_rk, r_w2], w=[r_kvn[p]])
                fw.op("dve", lambda e: e.tensor_tensor(out=t1[:, 0:n], in0=ps[3][0:64, 0:n], in1=cosp[k][:, 0:n], op=ALU.mult),
                      r=[psr[3], r_cs[k]], w=[r_t1])
                fw.op("dve", lambda e: e.tensor_tensor(out=t2[:, 0:n], in0=ps[4][0:64, 0:n], in1=sinp[k][:, 0:n], op=ALU.mult),
                      r=[psr[4], r_cs[k]], w=[r_t2])
                fw.op("pool", lambda e: e.tensor_tensor(out=kpeT[0:64, tsl], in0=t1[:, 0:n], in1=t2[:, 0:n], op=ALU.add),
                      r=[r_t1, r_t2], w=[r_kpe[p]])
                for ti in range(n // 128):
                    tt = s0 // 128 + ti
                    if tt < 2 and not need_ctx:
                        continue
                    zk = zcount % 2
                    zcount += 1
                    for cb in range(4):
                        bank = 4 + cb
                        for kc in range(8):
                            fw.op("pe", lambda e, kc=kc, cb=cb, bank=bank, ti=ti: e.matmul(
                                ps[bank][:, :], hpk[:, kc, ti * 128:(ti + 1) * 128], wz[:, kc, cb * 512:(cb + 1) * 512],
                                start=(kc == 0), stop=(kc == 7)), r=[r_w1, r_hp[k]], w=[psr[bank]], inc=(kc == 7))
                        fw.op("act", lambda e, cb=cb, bank=bank, zk=zk: e.activation(
                            out=zst[zk][:, cb * 512:(cb + 1) * 512], in_=ps[bank][:, :], func=AF.Silu), r=[psr[bank]], w=[r_zst[zk]])
                    fw.dma("pool", P.zs_d[tt * 128:(tt + 1) * 128, :], zst[zk][:], r=[r_zst[zk]], w=[r_zs_d[tt]])
            fw.barrier()

        knT = [sb("ml_knT%d" % k, [128, T], BF16) for k in range(2)]
        V1 = [sb("ml_V1%d" % k, [128, NT, 132], BF16) for k in range(2)]
        qnT = [sb("ml_qnT%d" % k, [128, T], BF16) for k in range(2)]
        qpT = [sb("ml_qpT%d" % k, [128, T], BF16) for k in range(2)]
        r_knT, r_V1, r_qnT, r_qpT = [[Res(), Res()] for _ in range(4)]
        cosq = [sb("ml_cq%d" % k, [64, 512]) for k in range(2)]
        sinq = [sb("ml_sq%d" % k, [64, 512]) for k in range(2)]
        r_csq = [Res(), Res()]
        u1 = sb("ml_u1", [64, 512])
        u2 = sb("ml_u2", [64, 512])
        r_u1, r_u2 = Res(), Res()
        NPT = 3
        PT = [sb("ml_PT%d" % k, [128, 512], BF16) for k in range(NPT)]
        r_PT = [Res() for k in range(NPT)]
        zt = [sb("ml_zt%d" % k, [128, 4, 128], BF16) for k in range(3)]
        r_zt = [Res(), Res(), Res()]
        rinv = [sb("ml_ri%d" % k, [128, 4]) for k in range(2)]
        r_rinv = [Res(), Res()]
        ozt = [[sb("ml_ozt%d%d" % (a, k), [128, 128], BF16) for k in range(2)] for a in range(2)]
        r_ozt = [[Res(), Res()], [Res(), Res()]]
        ozs = [sb("ml_ozs%d" % k, [128, 512], BF16) for k in range(2)]
        r_ozs = [Res(), Res()]
        for k in range(2):
            fw.op("dve", lambda e, k=k: e.memset(V1[k][:, :, 128:132], 1.0), w=[r_V1[k]])
            fw.op("pool", lambda e, k=k: e.memset(qpT[k][64:128, :], 0.0), w=[r_qpT[k]])
        SCB = [0, 1, 2]
        OAB = [3, 4, 5, 6]
        PJ = 7
        qtiles = [(LC + i * 512, 512, list(range(NT))) for i in range(L // 512)]
        if need_ctx:
            qtiles = [(0, LC, [0, 1])] + qtiles
        tok_lo = 0 if need_ctx else LC
        csn = [0]
        n_oz = [0]
        n_it = [0]
        nheads = int(os.environ.get("ML_NH", NH))

        def project_steps(hd):
            hb = hd % 2
            b = PJ
            steps = []
            for p in range(NP):
                s0, n = PIECES[p]
                tsl = slice(s0, s0 + n)

                def st_k(s0=s0, n=n, tsl=tsl, p=p):
                    fw.op("pe", lambda e: e.matmul(ps[b][:, 0:n], wkvb[:, hd * 256:hd * 256 + 128], kvn[:, tsl], start=True, stop=True),
                          r=[r_w2, r_kvn[p]], w=[psr[b]])
                    fw.op("dve", lambda e: e.tensor_copy(out=knT[hb][:, tsl], in_=ps[b][:, 0:n]), r=[psr[b]], w=[r_knT[hb]])

                def st_v(s0=s0, n=n, tsl=tsl, p=p):
                    for ti in range(n // 128):
                        fw.op("pe", lambda e, ti=ti: e.matmul(ps[b][:, ti * 128:(ti + 1) * 128], kvn[:, s0 + ti * 128:s0 + (ti + 1) * 128],
                                                             wkvb[:, hd * 256 + 128:hd * 256 + 256], start=True, stop=True),
                              r=[r_w2, r_kvn[p]], w=[psr[b]], inc=(ti == n // 128 - 1))
                    fw.op("dve", lambda e: e.tensor_copy(out=V1[hb][:, s0 // 128:(s0 + n) // 128, 0:128],
                                                         in_=ps[b][:, 0:n].rearrange("p (a v) -> p a v", v=128)), r=[psr[b]], w=[r_V1[hb]])
                steps += [st_k, st_v]
                if s0 + n <= tok_lo:
                    continue

                def st_qn(s0=s0, n=n, tsl=tsl, p=p):
                    for g in range(2):
                        fw.op("pe", lambda e, g=g: e.matmul(ps[b][:, 0:n], wqb[:, g, hd * 192:hd * 192 + 128], qn[:, g, tsl],
                                                           start=(g == 0), stop=(g == 1)),
                              r=[r_w2, r_qn[p]], w=[psr[b]], inc=(g == 1))
                    fw.op("dve", lambda e: e.tensor_copy(out=qnT[hb][:, tsl], in_=ps[b][:, 0:n]), r=[psr[b]], w=[r_qnT[hb]])

                def st_qp1(s0=s0, n=n, tsl=tsl, p=p):
                    ck = csn[0] % 2
                    fw.dma("sp", cosq[ck][:, 0:n], P.k_cos[:, tsl], w=[r_csq[ck]])
                    fw.dma("sp", sinq[ck][:, 0:n], P.k_sin[:, tsl], w=[r_csq[ck]])
                    for g in range(2):
                        fw.op("pe", lambda e, g=g: e.matmul(ps[b][0:64, 0:n], wqb[:, g, hd * 192 + 128:hd * 192 + 192], qn[:, g, tsl],
                                                           start=(g == 0), stop=(g == 1)),
                              r=[r_w2, r_qn[p]], w=[psr[b]], inc=(g == 1))
                    fw.op("dve", lambda e, ck=ck: e.tensor_tensor(out=u1[:, 0:n], in0=ps[b][0:64, 0:n], in1=cosq[ck][:, 0:n], op=ALU.mult),
                          r=[psr[b], r_csq[ck]], w=[r_u1])

                def st_qp2(s0=s0, n=n, tsl=tsl, p=p):
                    ck = csn[0] % 2
                    csn[0] += 1
                    for g in range(2):
                        fw.op("pe", lambda e, g=g: e.matmul(ps[b][0:64, 0:n], wqs[:, g, hd, :], qn[:, g, tsl],
                                                           start=(g == 0), stop=(g == 1)),
                              r=[r_w2, r_qn[p]], w=[psr[b]], inc=(g == 1))
                    fw.op("dve", lambda e, ck=ck: e.tensor_tensor(out=u2[:, 0:n], in0=ps[b][0:64, 0:n], in1=sinq[ck][:, 0:n], op=ALU.mult),
                          r=[psr[b], r_csq[ck]], w=[r_u2])
                    fw.op("pool", lambda e: e.tensor_tensor(out=qpT[hb][0:64, tsl], in0=u1[:, 0:n], in1=u2[:, 0:n], op=ALU.add),
                          r=[r_u1, r_u2], w=[r_qpT[hb]])
                steps += [st_qn, st_qp1, st_qp2]
            return steps

        zk_of = {}

        def scores(it):
            hd, q0, nq, kt, ki, nk, idx = it
            hb = hd % 2
            if ki == 0:
                zk, zq = n_oz[0] % 2, n_oz[0] % 3
                n_oz[0] += 1
                zk_of[(hd, q0)] = (zk, zq)
                fw.dma("sp", zt[zq][:, 0:nq // 128, :],
                       P.zs_d[q0:q0 + nq, hd * 128:(hd + 1) * 128].rearrange("(a p) v -> p a v", p=128),
                       r=[P.r_zs_d[t] for t in range(q0 // 128, (q0 + nq) // 128)], w=[r_zt[zq]])
            ks = slice(kt * 128, (kt + 1) * 128)
            sb_ = SCB[idx % 3]
            pk = idx % NPT
            fw.op("pe", lambda e: e.matmul(ps[sb_][:, 0:nq], knT[hb][:, ks], qnT[hb][:, q0:q0 + nq], start=True, stop=False),
                  r=[r_knT[hb], r_qnT[hb]], w=[psr[sb_]], inc=False)
            fw.op("pe", lambda e: e.matmul(ps[sb_][:, 0:nq], kpeT[:, ks], qpT[hb][:, q0:q0 + nq], start=False, stop=True),
                  r=[r_kpe[kt // 4], r_qpT[hb]], w=[psr[sb_]])
            fw.op("act", lambda e: e.activation(out=PT[pk][:, 0:nq], in_=ps[sb_][:, 0:nq], func=AF.Exp, scale=MLA_SCALE),
                  r=[psr[sb_]], w=[r_PT[pk]])

        def pv(it, i):
            hd, q0, nq, kt, ki, nk, idx = it
            hb = hd % 2
            pk = idx % NPT
            nqs = nq // 128
            for qi in range(nqs):
                ob = OAB[qi]
                fw.op("pe", lambda e, ob=ob, qi=qi: e.matmul(
                    ps[ob][:, 0:129], PT[pk][:, qi * 128:(qi + 1) * 128], V1[hb][:, kt, 0:129],
                    start=(ki == 0), stop=(ki == nk - 1)),
                    r=[r_PT[pk], r_V1[hb]], w=[psr[ob]], inc=(ki == nk - 1 or qi == nqs - 1))
            if ki == nk - 1:
                finalize(hd, q0, nq, i)

        oev = [sb("ml_oev%d" % k, [128, 4, 132]) for k in range(2)]
        r_oev = [Res(), Res()]
        later = {}

        def finalize(hd, q0, nq, i):
            nqs = nq // 128
            zk, zq = zk_of[(hd, q0)]
            for qi in range(nqs):
                ob = OAB[qi]
                fw.op("dve", lambda e, ob=ob, qi=qi: e.tensor_copy(out=oev[zk][:, qi, 0:129], in_=ps[ob][:, 0:129]),
                      r=[psr[ob]], w=[r_oev[zk]])

            def f2():
                fw.op("dve", lambda e: e.reciprocal(out=rinv[zk][:, 0:nqs], in_=oev[zk][:, 0:nqs, 128]),
                      r=[r_oev[zk]], w=[r_rinv[zk]])

            def f3(qi):
                def run():
                    tk = qi % 2
                    fw.op("dve", lambda e: e.scalar_tensor_tensor(
                        out=ozt[zk][tk][:], in0=oev[zk][:, qi, 0:128], scalar=rinv[zk][:, qi:qi + 1], in1=zt[zq][:, qi, :],
                        op0=ALU.mult, op1=ALU.mult), r=[r_oev[zk], r_rinv[zk], r_zt[zq]], w=[r_ozt[zk][tk]])
                return run

            def f4(qi):
                def run():
                    tk = qi % 2
                    tv = ps[PJ][:, :].bitcast(BF16)[:, 0:128]
                    fw.op("pe", lambda e: e.transpose(out=tv, in_=ozt[zk][tk][:], identity=P.ident[:]),
                          r=[r_ozt[zk][tk], P.r_const], w=[psr[PJ]])
                    fw.op("dve", lambda e: e.tensor_copy(out=ozs[zk][:, qi * 128:(qi + 1) * 128], in_=tv),
                          r=[psr[PJ]], w=[r_ozs[zk]])
                    if qi == nqs - 1:
                        fw.dma("pool", P.ozT[hd, :, q0:q0 + nq], ozs[zk][:, 0:nq], r=[r_ozs[zk]],
                               w=[P.r_ozT[hd][pp] for pp in range(q0 // 512, (q0 + nq - 1) // 512 + 1)])
                return run
            later.setdefault(i + 1, []).append(f2)
            for qi in range(nqs):
                later.setdefault(i + 2 + 2 * qi, []).append(f3(qi))
                later.setdefault(i + 4 + 2 * qi, []).append(f4(qi))

        its = []
        for hd in range(nheads):
            for (q0, nq, ktiles) in qtiles:
                for ki, kt in enumerate(ktiles):
                    its.append((hd, q0, nq, kt, ki, len(ktiles), len(its)))
        DEPTHP = 2
        for st_ in project_steps(0):
            st_()
        nxt_steps = []
        per_head = len(its) // nheads
        total = len(its) + DEPTHP + 16
        for i in range(total):
            if i < len(its):
                hd_i = its[i][0]
                if i % per_head == 0 and hd_i + 1 < nheads:
                    nxt_steps = project_steps(hd_i + 1)
                scores(its[i])
            if 0 <= i - DEPTHP < len(its):
                pv(its[i - DEPTHP], i)
            for fn in later.pop(i, []):
                fn()
            if nxt_steps and i % per_head >= 8 and i % 2 == 0:
                nxt_steps.pop(0)()
        assert not later and not nxt_steps
        fw.barrier()


def _consts():
    ident = np.eye(128, dtype=np.float32)
    s = np.arange(128)[:, None]
    t = np.arange(128)[None, :]
    same = (s // CH) == (t // CH)
    maskf = (same & (s <= t)).astype(np.float32)
    maskb = (same & (s >= t)).astype(np.float32)
    GRID_W = 64
    rows = L // GRID_W
    row = np.repeat(np.arange(rows, dtype=np.float32), GRID_W)
    col = np.tile(np.arange(GRID_W, dtype=np.float32), rows)
    inv = (10000.0 ** (-np.arange(0, 32, 2, dtype=np.float32) / 32)).astype(np.float32)
    ang = np.concatenate([row[:, None] * inv, col[:, None] * inv], axis=-1)
    cos = np.cos(ang).astype(np.float32).T
    sin = np.sin(ang).astype(np.float32).T
    kcos = np.ones((64, T), np.float32)
    ksin = np.zeros((64, T), np.float32)
    kcos[0:32, LC:] = cos
    kcos[32:64, LC:] = cos
    ksin[0:32, LC:] = -sin
    ksin[32:64, LC:] = sin
    cm3 = np.zeros((128, CPB, 128), np.float32)
    for c in range(CPB):
        cm3[c * CH:(c + 1) * CH, c, :] = 1.0
    return {"k_cm3": cm3.reshape(128, CPB * 128), "k_ident": ident, "k_maskf": maskf, "k_maskb": maskb, "k_cos": kcos, "k_sin": ksin}


_PROG_CACHE = {}


def _get_prog(n_layers=DEPTH, dbg=False):
    key = (n_layers, dbg)
    if key not in _PROG_CACHE:
        _PROG_CACHE[key] = Prog(n_layers, dbg)
    return _PROG_CACHE[key]


def _in_maps(inputs):
    f = lambda a: np.ascontiguousarray(np.asarray(a, dtype=np.float32))
    shared = {k: f(inputs[k]) for k in ("c_ctx", "ada_w", "ada_b", "norm_pre", "norm_post", "hg_w_in",
                                         "hg_lb_logits", "hg_o_norm", "hg_w_out", "mla_w_in", "mla_qa_norm",
                                         "mla_w_qb", "mla_kva_norm", "mla_w_kvb", "mla_w_out")}
    shared.update(_consts())
    x = f(inputs["x"])
    c = f(inputs["c"])
    ctx = f(inputs["ctx"])
    maps = []
    for b in range(8):
        m = dict(shared)
        m["x"] = x[b]
        m["c"] = c[b]
        m["ctx"] = ctx[b]
        maps.append(m)
    return maps


def kernel(**inputs):
    prog = _get_prog()
    res = run_bass_kernel_spmd(prog.nc, _in_maps(inputs), core_ids=list(range(8)))
    return np.stack([np.asarray(r["out"], dtype=np.float32) for r in res.results], axis=0)
```

```python
import os
import numpy as np
from contextlib import ExitStack
import concourse.bass as bass
import concourse.mybir as mybir
from concourse.bass_utils import run_bass_kernel_spmd

F32 = mybir.dt.float32
BF16 = mybir.dt.bfloat16
AF = mybir.ActivationFunctionType
ALU = mybir.AluOpType

D = 1024
L = 4096
LC = 256
T = L + LC
NT = T // 128
DI = 2048
NH = 16
DEPTH = 4
EPS = 1e-6
CH = 16
NCH = T // CH
CPB = 128 // CH
PIECES = [(s, min(512, T - s)) for s in range(0, T, 512)]
MLA_SCALE = (128 + 64) ** -0.5
STRICT = True


_UID = [0]


def _sbt(nc, name, shape, dtype):
    _UID[0] += 1
    return nc.sbuf_tensor("%s_u%d" % (name, _UID[0]), shape, dtype)


class Res:
    __slots__ = ("name", "w", "r")

    def __init__(self, name=""):
        self.name = name
        self.w = None
        self.r = {}


class _Eng:
    def __init__(self, key, h, sem, self_sync):
        self.key = key
        self.h = h
        self.sem = sem
        self.self_sync = self_sync
        self.count = 0
        self.waited = {}


class FW:
    def __init__(self, nc, es, ndma=8):
        self.nc = nc
        self.sem = {}
        self.E = {}
        for key, h, ss in (("pe", nc.tensor, False), ("act", nc.scalar, True),
                           ("dve", nc.vector, True), ("pool", nc.gpsimd, True),
                           ("sp", nc.sync, False)):
            s = es.enter_context(nc.semaphore("sem_" + key))
            self.sem[key] = s
            self.E[key] = _Eng(key, h, s, ss)
        self.dq = {}
        for q in ("sp", "pool", "act"):
            keys = []
            for k in range(ndma):
                kk = "dq_%s%d" % (q, k)
                self.sem[kk] = es.enter_context(nc.semaphore(kk))
                keys.append(kk)
            self.dq[q] = [keys, 0]
        self.n_inst = 0

    @staticmethod
    def _deps(eng, r, w, dma=False):
        deps = {}

        def add(kv):
            k, v = kv
            if deps.get(k, 0) < v:
                deps[k] = v
        for t in r:
            if t.w is not None:
                add(t.w)
        strict = dma or (STRICT and eng.self_sync)
        for t in w:
            if t.w is not None and (strict or t.w[0] != eng.key):
                add(t.w)
            for k, v in t.r.items():
                if strict or k != eng.key:
                    add((k, v))
        return deps

    def _wait(self, eng, deps, dma=False):
        for k, v in deps.items():
            if k == eng.key and not (eng.self_sync or dma):
                continue
            if eng.waited.get(k, 0) >= v:
                continue
            eng.h.wait_ge(self.sem[k], v)
            eng.waited[k] = v

    @staticmethod
    def _mark(me, r, w):
        for t in r:
            if t.r.get(me[0], 0) < me[1]:
                t.r[me[0]] = me[1]
        for t in w:
            t.w = me
            t.r = {}

    def op(self, e, fn, r=(), w=(), inc=True):
        eng = self.E[e]
        self._wait(eng, self._deps(eng, r, w))
        inst = fn(eng.h)
        self.n_inst += 1
        if inc:
            eng.count += 1
            inst.then_inc(eng.sem, 1)
            me = (eng.key, eng.count)
        else:
            me = (eng.key, eng.count + 1)
        self._mark(me, r, w)
        return inst

    def dma(self, q, out, in_, r=(), w=(), **kw):
        eng = self.E[q]
        keys, i = self.dq[q]
        P = len(keys)
        key = keys[i % P]
        val = 16 * (i // P + 1)
        deps = self._deps(eng, r, w, dma=True)
        if i >= P and deps.get(key, 0) < val - 16:
            deps[key] = val - 16
        self._wait(eng, deps, dma=True)
        eng.h.dma_start(out=out, in_=in_, **kw).then_inc(self.sem[key], 16)
        self.n_inst += 1
        self.dq[q][1] = i + 1
        self._mark((key, val), r, w)

    def barrier(self):
        deps = {}
        for k, e in self.E.items():
            if e.count > 0:
                deps[k] = e.count
        for q, (keys, i) in self.dq.items():
            P = len(keys)
            for j in range(max(0, i - P), i):
                deps[keys[j % P]] = 16 * (j // P + 1)
        for k, e in self.E.items():
            d = {kk: v for kk, v in deps.items() if kk != k}
            self._wait(e, d, dma=True)


def _cast(fw, i, out, in_, r, w):
    e = ("pool", "dve", "act")[i % 3]
    if e == "act":
        fw.op(e, lambda h: h.copy(out=out, in_=in_), r=r, w=w)
    else:
        fw.op(e, lambda h: h.tensor_copy(out=out, in_=in_), r=r, w=w)


class Prog:
    def __init__(self, n_layers=DEPTH, dbg=False):
        self.n_layers = n_layers
        self.dbg = dbg
        self.nc = nc = bass.Bass("TRN2", target_bir_lowering=False)
        dt = lambda name, shape, dtype=F32, kind="ExternalInput": nc.dram_tensor(name, list(shape), dtype, kind=kind).ap()
        self.x_in = dt("x", [L, D])
        self.ctx_in = dt("ctx", [LC, D])
        self.c_in = dt("c", [D])
        self.cctx_in = dt("c_ctx", [D])
        self.ada_w = dt("ada_w", [DEPTH, D, 3 * D])
        self.ada_b = dt("ada_b", [DEPTH, 3 * D])
        self.norm_pre = dt("norm_pre", [DEPTH, D])
        self.norm_post = dt("norm_post", [DEPTH, D])
        self.hg_w_in = dt("hg_w_in", [2, D, 5 * DI])
        self.hg_lb = dt("hg_lb_logits", [2, 2, DI])
        self.hg_o_norm = dt("hg_o_norm", [2, 128])
        self.hg_w_out = dt("hg_w_out", [2, DI, D])
        self.mla_w_in = dt("mla_w_in", [2, D, 2496])
        self.mla_qa_norm = dt("mla_qa_norm", [2, 256])
        self.mla_w_qb = dt("mla_w_qb", [2, 256, 3072])
        self.mla_kva_norm = dt("mla_kva_norm", [2, 128])
        self.mla_w_kvb = dt("mla_w_kvb", [2, 128, 4096])
        self.mla_w_out = dt("mla_w_out", [2, DI, D])
        self.k_ident = dt("k_ident", [128, 128])
        self.k_maskf = dt("k_maskf", [128, 128])
        self.k_maskb = dt("k_maskb", [128, 128])
        self.k_cm3 = dt("k_cm3", [128, CPB * 128])
        self.k_cos = dt("k_cos", [64, T])
        self.k_sin = dt("k_sin", [64, T])
        self.out = dt("out", [L, D], F32, "ExternalOutput")
        okind = "ExternalOutput" if dbg else "Internal"
        self.xres = dt("xres", [T, D], F32, okind)
        self.ozT = dt("ozT", [NH, 128, T], BF16, "Internal")
        self.zs_d = dt("zs_d", [T, DI], BF16, "Internal")
        self.hT_d = dt("hT_d", [128, 8, T], BF16, "Internal")
        self.build()

    def build(self):
        nc = self.nc
        with ExitStack() as es:
            self.fw = fw = FW(nc, es)
            self.ps = [es.enter_context(nc.psum_tensor("ps%d" % i, [128, 512], F32)) for i in range(8)]
            self.psr = [Res("ps%d" % i) for i in range(8)]
            self.ident = es.enter_context(_sbt(nc, "ident", [128, 128], BF16))
            self.ones_bf = es.enter_context(_sbt(nc, "ones_bf", [128, 128], BF16))
            self.maskf = es.enter_context(_sbt(nc, "maskf", [128, 128], F32))
            self.maskb = es.enter_context(_sbt(nc, "maskb", [128, 128], F32))
            self.epsc = es.enter_context(_sbt(nc, "epsc", [128, 1], F32))
            self.sc2 = es.enter_context(_sbt(nc, "sc2", [128, 8, 2], F32))
            self.AB = es.enter_context(_sbt(nc, "AB", [128, 2, 2, 8], F32))
            self.Gbc = es.enter_context(_sbt(nc, "Gbc", [128, 2, D], F32))
            self.r_const = Res("const")
            self.r_sc2 = Res("sc2")
            self.r_AB = Res("AB")
            self.r_Gbc = Res("Gbc")
            self.r_xres = [Res("xres%d" % i) for i in range(NT)]
            self.r_ozT = [[Res("oz%d_%d" % (h, p)) for p in range(len(PIECES))] for h in range(NH)]
            self.setup_consts()
            for i in range(self.n_layers):
                self.layer(i)
            if self.n_layers < DEPTH:
                self.copy_out()
            fw.barrier()

    def setup_consts(self):
        nc, fw = self.nc, self.fw
        with ExitStack() as es:
            st = es.enter_context(_sbt(nc, "cst_stage", [128, 128], F32))
            cst = es.enter_context(_sbt(nc, "cst_c", [128, 8, 2], F32))
            r_st = Res("st")
            r_c = Res("c")
            fw.dma("sp", st[:], self.k_ident[:, :], w=[r_st])
            fw.op("dve", lambda e: e.tensor_copy(out=self.ident[:], in_=st[:]), r=[r_st], w=[self.r_const])
            fw.op("dve", lambda e: e.memset(self.ones_bf[:], 1.0), w=[self.r_const])
            fw.op("dve", lambda e: e.memset(self.epsc[:], EPS), w=[self.r_const])
            fw.dma("sp", self.maskf[:], self.k_maskf[:, :], w=[self.r_const])
            fw.dma("sp", self.maskb[:], self.k_maskb[:, :], w=[self.r_const])
            fw.dma("sp", cst[:, :, 0], self.c_in.rearrange("(kc p) -> p kc", p=128), w=[r_c],
                   allow_slow_non_contiguous=True)
            fw.dma("sp", cst[:, :, 1], self.cctx_in.rearrange("(kc p) -> p kc", p=128), w=[r_c],
                   allow_slow_non_contiguous=True)
            fw.op("act", lambda e: e.activation(out=self.sc2[:], in_=cst[:], func=AF.Silu), r=[r_c], w=[self.r_sc2])
            for tt in range(NT):
                src = self.ctx_in[tt * 128:(tt + 1) * 128, :] if tt < 2 else self.x_in[(tt - 2) * 128:(tt - 1) * 128, :]
                fw.dma("sp", self.xres[tt * 128:(tt + 1) * 128, :], src, w=[self.r_xres[tt]])
            if os.environ.get("DBG_ZERO_OZ"):
                zt_ = es.enter_context(_sbt(nc, "dbgz", [128, T], BF16))
                rz = Res()
                fw.op("dve", lambda e: e.memset(zt_[:], 0.0), w=[rz])
                for h in range(NH):
                    fw.dma("sp", self.ozT[h], zt_[:], r=[rz], w=self.r_ozT[h])
            fw.barrier()

    def copy_out(self):
        fw = self.fw
        for tt in range(2, NT):
            fw.dma("sp", self.out[(tt - 2) * 128:(tt - 1) * 128, :], self.xres[tt * 128:(tt + 1) * 128, :],
                   r=[self.r_xres[tt]])

    def layer(self, i):
        fw = self.fw
        last = (i == DEPTH - 1)
        self.phase0_mod(i)
        fw.barrier()
        hT = self.hT_d
        r_hT = [Res("hT%d" % t) for t in range(NT)]
        self.phase1_prenorm(i, hT, r_hT)
        if i % 2 == 0:
            self.hgrn_mixer(i // 2, hT, r_hT)
        else:
            self.mla_mixer(i // 2, hT, r_hT, need_ctx=not last)
        fw.barrier()
        w_out = self.hg_w_out[i // 2] if i % 2 == 0 else self.mla_w_out[i // 2]
        self.phase3_out(i, w_out, last)
        fw.barrier()

    def phase0_mod(self, i):
        nc, fw = self.nc, self.fw
        with ExitStack() as es:
            sb = lambda n, s, d=F32: es.enter_context(_sbt(nc, n, s, d))
            wblk = [sb("adaw%d" % k, [128, 8, 512]) for k in range(2)]
            r_wblk = [Res("adaw%d" % k) for k in range(2)]
            screp = sb("screp", [128, 2, 8, 128])
            r_screp = Res("screp")
            adabT = sb("adabT", [128, 24])
            npreT = sb("npreT", [128, 8])
            adab_bc = sb("adab_bc", [128, D])
            npost_bc = sb("npost_bc", [128, D])
            tmpg = sb("tmpg", [128, 512])
            r_small = Res("small")
            r_bc = Res("bc")
            r_tmpg = Res("tmpg")
            fw.dma("sp", adabT[:], self.ada_b[i].rearrange("(g p) -> p g", p=128), w=[r_small],
                   allow_slow_non_contiguous=True)
            fw.dma("sp", npreT[:], self.norm_pre[i].rearrange("(g p) -> p g", p=128), w=[r_small],
                   allow_slow_non_contiguous=True)
            fw.dma("sp", adab_bc[:], self.ada_b[i, 2 * D:3 * D].partition_broadcast(128), w=[r_bc])
            fw.dma("sp", npost_bc[:], self.norm_post[i].partition_broadcast(128), w=[r_bc])
            for wh in range(2):
                for kc in range(8):
                    fw.op("dve", lambda e, wh=wh, kc=kc: e.tensor_copy(
                        out=screp[:, wh, kc, :], in_=self.sc2[:, kc, wh:wh + 1].to_broadcast([128, 128])),
                        r=[self.r_sc2], w=[r_screp])
            psM, r_psM = self.ps[0], self.psr[0]
            for jb in range(6):
                wb, r_wb = wblk[jb % 2], r_wblk[jb % 2]
                fw.dma("sp", wb[:], self.ada_w[i, :, jb * 512:(jb + 1) * 512].rearrange("(kc p) n -> p kc n", p=128),
                       w=[r_wb])
                if jb < 4:
                    for g in range(4):
                        gi = jb * 4 + g
                        for kc in range(8):
                            fw.op("pe", lambda e, gi=gi, g=g, kc=kc, wb=wb: e.matmul(
                                psM[:, 2 * gi:2 * gi + 2], wb[:, kc, g * 128:(g + 1) * 128], self.sc2[:, kc, :],
                                start=(kc == 0), stop=(kc == 7)),
                                r=[r_wb, self.r_sc2], w=[r_psM], inc=(kc == 7))
                else:
                    for wh in range(2):
                        pg, r_pg = self.ps[1 + wh], self.psr[1 + wh]
                        for kc in range(8):
                            fw.op("pe", lambda e, wh=wh, kc=kc, wb=wb, pg=pg: e.matmul(
                                pg[:, :], screp[:, wh, kc, :], wb[:, kc, :], start=(kc == 0), stop=(kc == 7)),
                                r=[r_wb, r_screp], w=[r_pg], inc=(kc == 7))
                        cs = slice((jb - 4) * 512, (jb - 3) * 512)
                        fw.op("dve", lambda e, pg=pg, cs=cs: e.tensor_tensor(
                            out=tmpg[:], in0=pg[:, :], in1=adab_bc[:, cs], op=ALU.add),
                            r=[r_pg, r_bc], w=[r_tmpg])
                        fw.op("dve", lambda e, wh=wh, cs=cs: e.tensor_tensor(
                            out=self.Gbc[:, wh, cs], in0=tmpg[:], in1=npost_bc[:, cs], op=ALU.mult),
                            r=[r_tmpg, r_bc], w=[self.r_Gbc])
            pv = psM[:, 0:32].rearrange("p (g w) -> p g w", w=2)
            for wh in range(2):
                fw.op("dve", lambda e, wh=wh: e.tensor_tensor(
                    out=self.AB[:, wh, 1, :], in0=pv[:, 0:8, wh], in1=adabT[:, 0:8], op=ALU.add),
                    r=[r_psM, r_small], w=[self.r_AB])
                fw.op("dve", lambda e, wh=wh: e.scalar_tensor_tensor(
                    out=self.AB[:, wh, 0, :], in0=pv[:, 8:16, wh], scalar=1.0, in1=adabT[:, 8:16],
                    op0=ALU.add, op1=ALU.add),
                    r=[r_psM, r_small], w=[self.r_AB])
                fw.op("dve", lambda e, wh=wh: e.tensor_tensor(
                    out=self.AB[:, wh, 0, :], in0=self.AB[:, wh, 0, :], in1=npreT[:], op=ALU.mult),
                    r=[self.r_AB, r_small], w=[self.r_AB])
            fw.barrier()

    def phase1_prenorm(self, i, hT, r_hT):
        nc, fw = self.nc, self.fw
        with ExitStack() as es:
            sb = lambda n, s, d=F32: es.enter_context(_sbt(nc, n, s, d))
            NB = 3
            xt = [sb("p1x%d" % k, [128, D]) for k in range(NB)]
            r_xt = [Res() for k in range(NB)]
            xn = [sb("p1xn%d" % k, [128, D], BF16) for k in range(2)]
            r_xn = [Res() for k in range(2)]
            junk = sb("p1junk", [128, D], BF16)
            r_junk = Res()
            st = [sb("p1st%d" % k, [128, 4]) for k in range(2)]
            r_st = [Res() for k in range(2)]
            hst = [sb("p1hs%d" % k, [128, 8, 128], BF16) for k in range(2)]
            r_hst = [Res() for k in range(2)]
            def load_x(tt):
                fw.dma("sp", xt[tt % NB][:], self.xres[tt * 128:(tt + 1) * 128, :], r=[self.r_xres[tt]], w=[r_xt[tt % NB]])
            load_x(0)
            load_x(1)
            for tt in range(NT):
                wh = 1 if tt < 2 else 0
                x_, rx = xt[tt % NB], r_xt[tt % NB]
                xn_, rxn = xn[tt % 2], r_xn[tt % 2]
                s_, rs = st[tt % 2], r_st[tt % 2]
                pT, rpT = self.ps[tt % 2], self.psr[tt % 2]
                pTb = pT[:, :].bitcast(BF16)
                fw.op("act", lambda e, x_=x_, s_=s_: e.activation(
                    out=junk[:], in_=x_[:], func=AF.Square, accum_out=s_[:, 0:1]), r=[rx], w=[r_junk, rs])
                fw.op("act", lambda e, s_=s_: e.activation(
                    out=s_[:, 1:2], in_=s_[:, 0:1], func=AF.Sqrt, scale=1.0 / D, bias=self.epsc[:, 0:1]),
                    r=[rs, self.r_const], w=[rs])
                fw.op("dve", lambda e, s_=s_: e.reciprocal(out=s_[:, 2:3], in_=s_[:, 1:2]), r=[rs], w=[rs])
                fw.op("dve", lambda e, x_=x_, xn_=xn_, s_=s_: e.tensor_scalar(
                    out=xn_[:], in0=x_[:], scalar1=s_[:, 2:3], scalar2=None, op0=ALU.mult), r=[rx, rs], w=[rxn])
                for kc in range(8):
                    fw.op("pe", lambda e, kc=kc, xn_=xn_, pTb=pTb: e.transpose(
                        out=pTb[:, kc * 128:(kc + 1) * 128], in_=xn_[:, kc * 128:(kc + 1) * 128], identity=self.ident[:]),
                        r=[rxn, self.r_const], w=[rpT], inc=(kc == 7))
                rh = r_hT[tt]
                hs, rhs_ = hst[tt % 2], r_hst[tt % 2]
                for kc in range(8):
                    dst = hs[:, kc, :]
                    src = pTb[:, kc * 128:(kc + 1) * 128]
                    if kc % 2 == 0:
                        fw.op("dve", lambda e, dst=dst, src=src, kc=kc, wh=wh: e.tensor_scalar(
                            out=dst, in0=src, scalar1=self.AB[:, wh, 0, kc:kc + 1], scalar2=self.AB[:, wh, 1, kc:kc + 1],
                            op0=ALU.mult, op1=ALU.add), r=[rpT, self.r_AB], w=[rhs_])
                    else:
                        fw.op("act", lambda e, dst=dst, src=src, kc=kc, wh=wh: e.activation(
                            out=dst, in_=src, func=AF.Identity, scale=self.AB[:, wh, 0, kc:kc + 1],
                            bias=self.AB[:, wh, 1, kc:kc + 1]), r=[rpT, self.r_AB], w=[rhs_])
                fw.dma("pool", hT[:, :, tt * 128:(tt + 1) * 128], hs[:], r=[rhs_], w=[rh])
                if tt + 2 < NT:
                    load_x(tt + 2)
            fw.barrier()

    def phase3_out(self, i, w_out, last):
        nc, fw = self.nc, self.fw
        with ExitStack() as es:
            sb = lambda n, s, d=F32: es.enter_context(_sbt(nc, n, s, d))
            wo = sb("p3wo", [128, NH, D], BF16)
            r_wo = Res()
            wst = [sb("p3wst%d" % k, [128, 2, D]) for k in range(2)]
            r_wst = [Res() for k in range(2)]
            for g in range(8):
                s_, rs = wst[g % 2], r_wst[g % 2]
                fw.dma("sp", s_[:], w_out[g * 256:(g + 1) * 256, :].rearrange("(h p) n -> p h n", p=128), w=[rs])
                _cast(fw, g, wo[:, 2 * g:2 * g + 2, :], s_[:], [rs], [r_wo])
            NB = 3
            oz = [sb("p3oz%d" % k, [128, NH, 128], BF16) for k in range(NB)]
            r_oz = [Res() for k in range(NB)]
            xt = [sb("p3x%d" % k, [128, D]) for k in range(NB)]
            r_xt = [Res() for k in range(NB)]
            tmp = [sb("p3t%d" % k, [128, D]) for k in range(NB)]
            r_tmp = [Res() for k in range(NB)]
            junk = sb("p3junk", [128, 512], BF16)
            r_junk = Res()
            st = [sb("p3st%d" % k, [128, 8]) for k in range(NB)]
            r_st = [Res() for k in range(NB)]
            tiles = list(range(2, NT)) if last else list(range(NT))
            NB = 3

            def load_t(n):
                tt = tiles[n]
                k = n % NB
                pc = (tt * 128) // 512
                fw.dma("sp", oz[k][:], self.ozT[:, :, tt * 128:(tt + 1) * 128].rearrange("h v t -> v h t"),
                       r=[self.r_ozT[h][pc] for h in range(NH)], w=[r_oz[k]])
                fw.dma("sp", xt[k][:], self.xres[tt * 128:(tt + 1) * 128, :], r=[self.r_xres[tt]], w=[r_xt[k]])
            load_t(0)
            load_t(1)
            for n, tt in enumerate(tiles):
                wh = 1 if tt < 2 else 0
                k = n % NB
                pb = [(self.ps[2 * (n % 2)], self.psr[2 * (n % 2)]), (self.ps[2 * (n % 2) + 1], self.psr[2 * (n % 2) + 1])]
                for nb in range(2):
                    py, rpy = pb[nb]
                    for h in range(NH):
                        fw.op("pe", lambda e, py=py, h=h, nb=nb, k=k: e.matmul(
                            py[:, :], oz[k][:, h, :], wo[:, h, nb * 512:(nb + 1) * 512], start=(h == 0), stop=(h == NH - 1)),
                            r=[r_oz[k], r_wo], w=[rpy], inc=(h == NH - 1))
                    fw.op("act", lambda e, py=py, nb=nb, k=k: e.activation(
                        out=junk[:], in_=py[:, :], func=AF.Square, accum_out=st[k][:, nb:nb + 1]),
                        r=[rpy], w=[r_junk, r_st[k]])
                fw.op("dve", lambda e, k=k: e.tensor_tensor(
                    out=st[k][:, 2:3], in0=st[k][:, 0:1], in1=st[k][:, 1:2], op=ALU.add), r=[r_st[k]], w=[r_st[k]])
                fw.op("act", lambda e, k=k: e.activation(
                    out=st[k][:, 3:4], in_=st[k][:, 2:3], func=AF.Sqrt, scale=1.0 / D, bias=self.epsc[:, 0:1]),
                    r=[r_st[k], self.r_const], w=[r_st[k]])
                fw.op("dve", lambda e, k=k: e.reciprocal(out=st[k][:, 4:5], in_=st[k][:, 3:4]), r=[r_st[k]], w=[r_st[k]])
                for nb in range(2):
                    py, rpy = pb[nb]
                    cs = slice(nb * 512, (nb + 1) * 512)
                    fw.op("dve", lambda e, py=py, cs=cs, k=k, wh=wh: e.scalar_tensor_tensor(
                        out=tmp[k][:, cs], in0=py[:, :], scalar=st[k][:, 4:5], in1=self.Gbc[:, wh, cs],
                        op0=ALU.mult, op1=ALU.mult), r=[rpy, r_st[k], self.r_Gbc], w=[r_tmp[k]])
                fw.op("pool", lambda e, k=k: e.tensor_tensor(
                    out=tmp[k][:], in0=tmp[k][:], in1=xt[k][:], op=ALU.add), r=[r_tmp[k], r_xt[k]], w=[r_tmp[k]])
                if last:
                    fw.dma("pool", self.out[(tt - 2) * 128:(tt - 1) * 128, :], tmp[k][:], r=[r_tmp[k]])
                else:
                    fw.dma("pool", self.xres[tt * 128:(tt + 1) * 128, :], tmp[k][:], r=[r_tmp[k]], w=[self.r_xres[tt]])
                if n + 2 < len(tiles):
                    load_t(n + 2)

    def hgrn_mixer(self, j, hT, r_hT):
        from_hgrn(self, j, hT, r_hT)

    def mla_mixer(self, j, hT, r_hT, need_ctx):
        from_mla(self, j, hT, r_hT, need_ctx)


HG_COL = {"q": 0, "v": DI, "ff": 2 * DI, "fb": 3 * DI, "z": 4 * DI}
DCLAMP = 40.0


def from_hgrn(P, j, hT, r_hT):
    nc, fw = P.nc, P.fw
    NP = len(PIECES)
    REFI = CH // 2 - 1
    with ExitStack() as es:
        sb = lambda n, s, d=F32: es.enter_context(_sbt(nc, n, s, d))
        cm3 = sb("hg_cm3", [128, CPB, 128], BF16)
        gain = sb("hg_gain", [128, 1])
        lbT = sb("hg_lb", [128, 3, 2, NH])
        ones32 = sb("hg_ones", [128, 512])
        r_lc = Res("hg_lc")
        with ExitStack() as es2:
            st = es2.enter_context(_sbt(nc, "hg_cst", [128, CPB * 128], F32))
            lg = es2.enter_context(_sbt(nc, "hg_lg", [128, 2, 2, NH], F32))
            r_st = Res()
            fw.dma("sp", st[:], P.k_cm3[:, :], w=[r_st])
            fw.op("dve", lambda e: e.tensor_copy(out=cm3[:].rearrange("p c v -> p (c v)"), in_=st[:]), r=[r_st], w=[r_lc])
            fw.op("dve", lambda e: e.memset(ones32[:], 1.0), w=[r_lc])
            fw.dma("sp", gain[:], P.hg_o_norm[j].rearrange("(p o) -> p o", o=1), w=[r_lc])
            if j > 0:
                for jj in range(2):
                    for d in range(2):
                        fw.dma("sp", lg[:, jj, d, :], P.hg_lb[jj, d].rearrange("(h p) -> p h", p=128), w=[r_st],
                               allow_slow_non_contiguous=True)
                fw.op("dve", lambda e: e.tensor_tensor(out=lg[:, 0], in0=lg[:, 1], in1=lg[:, 0], op=ALU.subtract), r=[r_st], w=[r_st])
                fw.op("act", lambda e: e.activation(out=lbT[:, 0], in_=lg[:, 0], func=AF.Sigmoid), r=[r_st], w=[r_lc])
                fw.op("dve", lambda e: e.tensor_scalar(out=lbT[:, 1], in0=lbT[:, 0], scalar1=-1.0, scalar2=1.0,
                                                       op0=ALU.mult, op1=ALU.add), r=[r_lc], w=[r_lc])
                fw.op("dve", lambda e: e.tensor_scalar(out=lbT[:, 2], in0=lbT[:, 0], scalar1=-1.0, scalar2=None,
                                                       op0=ALU.add), r=[r_lc], w=[r_lc])
            fw.barrier()
        Q = [sb("hg_Q%d" % d, [128, T], BF16) for d in range(2)]
        Kt = [sb("hg_K%d" % d, [128, T], BF16) for d in range(2)]
        V = sb("hg_V", [128, NT, 128], BF16)
        zs = sb("hg_zs", [128, T], BF16)
        oacc = sb("hg_oacc", [128, T], BF16)
        Gref = sb("hg_Gref", [128, 2, NCH])
        Wt = sb("hg_W", [128, 2, NCH])
        Gl = sb("hg_Gl", [128, 2])
        Wj = sb("hg_Wj", [128, 2])
        r_Q = [[Res() for p in range(NP)] for d in range(2)]
        r_K = [[Res() for p in range(NP)] for d in range(2)]
        r_V = [Res() for p in range(NP)]
        r_zs = [Res() for p in range(NP)]
        r_oacc = [Res() for p in range(NT)]
        r_Gref, r_W, r_Gl, r_Wj = Res(), Res(), Res(), Res()
        wbf = [sb("hg_wbf%d" % k, [128, 5, 8, 128], BF16) for k in range(2)]
        r_wbf = [Res() for k in range(2)]
        wst = [sb("hg_wst%d" % k, [128, 8, 128]) for k in range(2)]
        r_wst = [Res() for k in range(2)]
        hp = [sb("hg_hp%d" % k, [128, 8, 512], BF16) for k in range(2)]
        r_hp = [Res() for k in range(2)]
        eq = sb("hg_eq", [128, 512])
        ez = sb("hg_ez", [128, 512])
        Ld = [sb("hg_L%d" % d, [128, 512]) for d in range(2)]
        Fd = [sb("hg_F%d" % d, [128, 512]) for d in range(2)]
        E1 = [sb("hg_E1%d" % d, [128, 512], BF16) for d in range(2)]
        E1b = [sb("hg_E1b%d" % d, [128, 512], BF16) for d in range(2)]
        k1 = [sb("hg_k1%d" % d, [128, 512], BF16) for d in range(2)]
        wtmp = [sb("hg_wtmp%d" % d, [128, 512 // CH + 1]) for d in range(2)]
        r_eq, r_ez = Res(), Res()
        r_L, r_F, r_E1, r_E2, r_k1, r_wtmp = [[Res(), Res()] for _ in range(6)]
        AT = [[sb("hg_AT%d%d" % (d, k), [128, 128], BF16) for k in range(3)] for d in range(2)]
        KT = [[sb("hg_KT%d%d" % (d, k), [128, 128], BF16) for k in range(2)] for d in range(2)]
        Vx = [[sb("hg_Vx%d%d" % (d, k), [128, CPB, 128], BF16) for k in range(3)] for d in range(2)]
        Tb = [[sb("hg_Tb%d%d" % (d, k), [128, CPB, 128]) for k in range(2)] for d in range(2)]
        Sp = [[sb("hg_Sp%d%d" % (d, k), [128, CPB, 128], BF16) for k in range(2)] for d in range(2)]
        Qw = [sb("hg_Qw%d" % d, [128, T], BF16) for d in range(2)]
        WtQ = sb("hg_WtQ", [128, 2, NCH + 1])
        r_AT = [[Res() for k in range(3)] for d in range(2)]
        r_KT = [[Res() for k in range(2)] for d in range(2)]
        r_Vx = [[Res() for k in range(3)] for d in range(2)]
        r_Tb = [[Res() for k in range(2)] for d in range(2)]
        r_Sp = [[Res() for k in range(2)] for d in range(2)]
        r_Qw = [[Res() for p in range(NP)] for d in range(2)]
        r_WtQ = Res()
        NRO = 6
        osum = [sb("hg_osum%d" % k, [128, 128]) for k in range(NRO)]
        sq = [sb("hg_sq%d" % k, [128, 128], BF16) for k in range(NRO)]
        rt = [sb("hg_rt%d" % k, [128, 128]) for k in range(NRO)]
        ozp = [sb("hg_ozp%d" % k, [128, 128], BF16) for k in range(NRO)]
        r_osum = [Res() for k in range(NRO)]
        r_sq = [Res() for k in range(NRO)]
        r_rt = [Res() for k in range(NRO)]
        r_ozp = [Res() for k in range(NRO)]
        ps, psr = P.ps, P.psr
        SC_Bs, KT_B, O_B, SS_B = [5, 0], 4, 3, 4
        x_banks = [[6, 7], [1, 2]]

        def load_weights(hd):
            k = hd % 2
            for m, nm in enumerate(("q", "v", "ff", "fb", "z")):
                c0 = HG_COL[nm] + hd * 128
                s_, rs = wst[m % 2], r_wst[m % 2]
                fw.dma("sp", s_[:], P.hg_w_in[j, :, c0:c0 + 128].rearrange("(kc p) n -> p kc n", p=128), w=[rs])
                fw.op("pool", lambda e, s_=s_, m=m, k=k: e.tensor_copy(out=wbf[k][:, m], in_=s_[:]), r=[rs], w=[r_wbf[k]])

        ld_count = [0]

        def load_piece(p):
            s0, n = PIECES[p]
            k = ld_count[0] % 2
            ld_count[0] += 1
            fw.dma("sp", hp[k][:, :, 0:n], hT[:, :, s0:s0 + n], r=[r_hT[t] for t in range(s0 // 128, (s0 + n) // 128)], w=[r_hp[k]])
            return k

        def prep(hd, p, hk):
            k = hd % 2
            s0, n = PIECES[p]
            ntile = n // 128
            nch = n // CH
            c0 = s0 // CH
            wb, rwb = wbf[k], r_wbf[k]
            hpk, rhp = hp[hk], r_hp[hk]
            tsl = slice(s0, s0 + n)
            one = ones32[:, 0:1]
            for m, bank in ((0, 0), (2, 1), (3, 2), (4, 3)):
                for kc in range(8):
                    fw.op("pe", lambda e, m=m, bank=bank, kc=kc: e.matmul(
                        ps[bank][:, 0:n], wb[:, m, kc, :], hpk[:, kc, 0:n], start=(kc == 0), stop=(kc == 7)),
                        r=[rwb, rhp], w=[psr[bank]], inc=(kc == 7))
            for ti in range(ntile):
                for kc in range(8):
                    fw.op("pe", lambda e, ti=ti, kc=kc: e.matmul(
                        ps[4][:, ti * 128:(ti + 1) * 128], hpk[:, kc, ti * 128:(ti + 1) * 128], wb[:, 1, kc, :],
                        start=(kc == 0), stop=(kc == 7)),
                        r=[rwb, rhp], w=[psr[4]], inc=(kc == 7))
            fw.op("act", lambda e: e.activation(out=eq[:, 0:n], in_=ps[0][:, 0:n], func=AF.Exp, scale=-1.0), r=[psr[0]], w=[r_eq])
            for d in range(2):
                fw.op("act", lambda e, d=d: e.activation(out=Ld[d][:, 0:n], in_=ps[1 + d][:, 0:n], func=AF.Exp, scale=-1.0),
                      r=[psr[1 + d]], w=[r_L[d]])
            fw.op("act", lambda e: e.activation(out=ez[:, 0:n], in_=ps[3][:, 0:n], func=AF.Exp, scale=-1.0), r=[psr[3]], w=[r_ez])
            fw.op("dve", lambda e: e.tensor_copy(out=V[:, s0 // 128:s0 // 128 + ntile, :].rearrange("p a v -> p (a v)"),
                                                 in_=ps[4][:, 0:n]), r=[psr[4]], w=[r_V[p]])
            fw.op("act", lambda e: e.activation(out=eq[:, 0:n], in_=eq[:, 0:n], func=AF.Ln, bias=one), r=[r_eq, r_lc], w=[r_eq])
            for d in range(2):
                fw.op("act", lambda e, d=d: e.activation(out=Ld[d][:, 0:n], in_=Ld[d][:, 0:n], func=AF.Ln, bias=one),
                      r=[r_L[d], r_lc], w=[r_L[d]])
            fw.op("act", lambda e: e.activation(out=ez[:, 0:n], in_=ez[:, 0:n], func=AF.Ln, bias=one), r=[r_ez, r_lc], w=[r_ez])
            fw.op("act", lambda e: e.activation(out=eq[:, 0:n], in_=eq[:, 0:n], func=AF.Exp, scale=-1.0), r=[r_eq], w=[r_eq])
            for d in range(2):
                fw.op("act", lambda e, d=d: e.activation(out=Fd[d][:, 0:n], in_=Ld[d][:, 0:n], func=AF.Exp, scale=-1.0),
                      r=[r_L[d]], w=[r_F[d]])
            fw.op("act", lambda e: e.activation(out=ez[:, 0:n], in_=ez[:, 0:n], func=AF.Exp, scale=-1.0), r=[r_ez], w=[r_ez])
            fw.op("dve", lambda e: e.tensor_tensor(out=eq[:, 0:n], in0=ps[0][:, 0:n], in1=eq[:, 0:n], op=ALU.mult),
                  r=[psr[0], r_eq], w=[r_eq])
            fw.op("dve", lambda e: e.tensor_tensor(out=zs[:, tsl], in0=ps[3][:, 0:n], in1=ez[:, 0:n], op=ALU.mult),
                  r=[psr[3], r_ez], w=[r_zs[p]])
            sops = [None, None]
            for d in range(2):
                L_, F_ = Ld[d], Fd[d]
                if j == 0:
                    fw.op("dve", lambda e, d=d, F_=F_: e.tensor_scalar(out=k1[d][:, 0:n], in0=F_[:, 0:n], scalar1=-1.0, scalar2=1.0,
                                                                      op0=ALU.mult, op1=ALU.add), r=[r_F[d]], w=[r_k1[d]])
                    sops[d] = ALU.subtract
                else:
                    fw.op("dve", lambda e, d=d, F_=F_: e.tensor_scalar(out=k1[d][:, 0:n], in0=F_[:, 0:n], scalar1=-1.0,
                                                                      scalar2=lbT[:, 2, d, hd:hd + 1],
                                                                      op0=ALU.add, op1=ALU.mult), r=[r_F[d], r_lc], w=[r_k1[d]])
                    fw.op("dve", lambda e, d=d, F_=F_: e.tensor_scalar(out=F_[:, 0:n], in0=F_[:, 0:n],
                                                                      scalar1=lbT[:, 1, d, hd:hd + 1], scalar2=lbT[:, 0, d, hd:hd + 1],
                                                                      op0=ALU.mult, op1=ALU.add), r=[r_F[d], r_lc], w=[r_F[d]])
                    fw.op("act", lambda e, d=d, F_=F_, L_=L_: e.activation(out=L_[:, 0:n], in_=F_[:, 0:n], func=AF.Ln),
                          r=[r_F[d]], w=[r_L[d]])
                    sops[d] = ALU.add
            for d in range(2):
                L_, F_ = Ld[d], Fd[d]
                init = 0.0 if p == 0 else Gl[:, d:d + 1]
                fw.op("dve", lambda e, init=init, sop=sops[d], L_=L_, F_=F_: e.tensor_tensor_scan(
                    out=F_[:, 0:n], data0=ones32[:, 0:n], data1=L_[:, 0:n], initial=init, op0=ALU.mult, op1=sop),
                    r=[r_L[d], r_lc, r_Gl, r_k1[d]], w=[r_F[d]])
            for d in range(2):
                L_, F_ = Ld[d], Fd[d]
                gv = F_[:, 0:n].rearrange("p (c i) -> p c i", i=CH)
                fw.op("pool", lambda e, d=d, F_=F_: e.tensor_copy(out=Gl[:, d:d + 1], in_=F_[:, n - 1:n]), r=[r_F[d]], w=[r_Gl])
                fw.op("pool", lambda e, d=d, gv=gv: e.tensor_copy(out=Gref[:, d, c0:c0 + nch], in_=gv[:, :, REFI]), r=[r_F[d]], w=[r_Gref])
                if d == 1:
                    fw.op("dve", lambda e, sop=sops[d], L_=L_, F_=F_: e.tensor_tensor(
                        out=F_[:, 0:n], in0=F_[:, 0:n], in1=L_[:, 0:n], op=(ALU.add if sop == ALU.subtract else ALU.subtract)),
                        r=[r_F[d], r_L[d]], w=[r_F[d]])
                fw.op("pool", lambda e, d=d, gv=gv: e.tensor_tensor(
                    out=gv, in0=gv, in1=Gref[:, d, c0:c0 + nch].unsqueeze(2).to_broadcast([128, nch, CH]), op=ALU.subtract),
                    r=[r_F[d], r_Gref], w=[r_F[d]])
            for d in range(2):
                F_ = Fd[d]
                sq_ = 1.0 if d == 0 else -1.0
                fw.op("act", lambda e, d=d, sq_=sq_, F_=F_: e.activation(out=E1[d][:, 0:n], in_=F_[:, 0:n], func=AF.Exp, scale=sq_),
                      r=[r_F[d]], w=[r_E1[d]])
                fw.op("act", lambda e, d=d, sq_=sq_, F_=F_: e.activation(out=E1b[d][:, 0:n], in_=F_[:, 0:n], func=AF.Exp, scale=-sq_),
                      r=[r_F[d]], w=[r_E2[d]])
            for d in range(2):
                fw.op("dve", lambda e, d=d: e.tensor_tensor(out=Q[d][:, tsl], in0=eq[:, 0:n], in1=E1[d][:, 0:n], op=ALU.mult),
                      r=[r_eq, r_E1[d]], w=[r_Q[d][p]])
                fw.op("pool", lambda e, d=d: e.tensor_tensor(out=Kt[d][:, tsl], in0=k1[d][:, 0:n], in1=E1b[d][:, 0:n], op=ALU.mult),
                      r=[r_k1[d], r_E2[d]], w=[r_K[d][p]])
                lo = max(c0 - 1, 0)
                hi = c0 + nch - 1
                fw.op("pool", lambda e, d=d, lo=lo, hi=hi: e.tensor_tensor(
                    out=wtmp[d][:, 0:hi - lo], in0=Gref[:, d, lo + 1:hi + 1], in1=Gref[:, d, lo:hi], op=ALU.subtract),
                    r=[r_Gref], w=[r_wtmp[d]])
                fw.op("act", lambda e, d=d, lo=lo, hi=hi: e.activation(out=Wt[:, d, lo:hi], in_=wtmp[d][:, 0:hi - lo], func=AF.Exp),
                      r=[r_wtmp[d]], w=[r_W])

        def junction():
            fw.op("dve", lambda e: e.tensor_tensor(out=Wj[:, 1:2], in0=Gref[:, 1, 0:1], in1=Gref[:, 1, NCH - 1:NCH], op=ALU.subtract),
                  r=[r_Gref], w=[r_Wj])
            fw.op("act", lambda e: e.activation(out=Wj[:, 0:1], in_=Wj[:, 1:2], func=AF.Exp, bias=Gl[:, 1:2]),
                  r=[r_Wj, r_Gl], w=[r_Wj])
            fw.op("dve", lambda e: e.memset(WtQ[:, 0, 0:1], 1.0), w=[r_WtQ])
            fw.op("dve", lambda e: e.tensor_copy(out=WtQ[:, 0, 1:NCH], in_=Wt[:, 0, 0:NCH - 1]), r=[r_W], w=[r_WtQ])
            fw.op("dve", lambda e: e.tensor_copy(out=WtQ[:, 1, 0:NCH - 1], in_=Wt[:, 1, 0:NCH - 1]), r=[r_W], w=[r_WtQ])
            fw.op("dve", lambda e: e.tensor_copy(out=WtQ[:, 1, NCH - 1:NCH], in_=Wj[:, 0:1]), r=[r_Wj], w=[r_WtQ])
            for p in range(NP):
                s0, n = PIECES[p]
                nch = n // CH
                c0 = s0 // CH
                for d in range(2):
                    fw.op("pool", lambda e, d=d, s0=s0, n=n, nch=nch, c0=c0: e.tensor_tensor(
                        out=Qw[d][:, s0:s0 + n].rearrange("p (c i) -> p c i", i=CH),
                        in0=Q[d][:, s0:s0 + n].rearrange("p (c i) -> p c i", i=CH),
                        in1=WtQ[:, d, c0:c0 + nch].unsqueeze(2).to_broadcast([128, nch, CH]), op=ALU.mult),
                        r=[r_Q[d][p], r_WtQ], w=[r_Qw[d][p]])

        ncc = LC // CH
        orders = [list(range(NCH)), list(range(ncc - 1, -1, -1)) + list(range(NCH - 1, ncc - 1, -1))]

        def blocks_of(order):
            out = []
            for gc in order:
                if not out or out[-1] != gc // CPB:
                    out.append(gc // CPB)
            return out

        def build_vx(d, step, blk):
            sl = step % 3
            p = blk // 4
            fw.op("pool", lambda e: e.tensor_tensor(out=Vx[d][sl][:], in0=V[:, blk:blk + 1, :].to_broadcast([128, CPB, 128]),
                                                    in1=cm3[:], op=ALU.mult),
                  r=[r_V[p], r_lc], w=[r_Vx[d][sl]])

        def stageA_pre(hd, d, step, blk):
            sl = step % 2
            p = blk // 4
            tb = slice(blk * 128, (blk + 1) * 128)
            SC_B = SC_Bs[d]
            scv = ps[SC_B][:, 0:128]
            ktv = ps[KT_B][:, :].bitcast(BF16)[:, 128 * d:128 * d + 128]
            fw.op("pe", lambda e: e.matmul(scv, Kt[d][:, tb], Q[d][:, tb], start=True, stop=True),
                  r=[r_K[d][p], r_Q[d][p]], w=[psr[SC_B]])
            mk = P.maskf if d == 0 else P.maskb
            fw.op("dve", lambda e: e.tensor_tensor(out=AT[d][step % 3][:], in0=scv, in1=mk[:], op=ALU.mult),
                  r=[psr[SC_B], P.r_const], w=[r_AT[d][step % 3]])
            fw.op("pe", lambda e: e.transpose(out=ktv, in_=Kt[d][:, tb], identity=P.ident[:]),
                  r=[r_K[d][p], P.r_const], w=[psr[KT_B]])
            fw.op("act", lambda e: e.copy(out=KT[d][sl][:], in_=ktv), r=[psr[KT_B]], w=[r_KT[d][sl]])

        def stageX(hd, d, step, blk):
            sl = step % 2
            vs = step % 3
            for h in range(CPB // 4):
                xb = x_banks[d][h]
                fw.op("pe", lambda e, xb=xb, h=h: e.matmul(ps[xb][:, :], KT[d][sl][:],
                                                          Vx[d][vs][:, 4 * h:4 * h + 4, :].rearrange("p c v -> p (c v)"),
                                                          start=True, stop=True),
                      r=[r_KT[d][sl], r_Vx[d][vs]], w=[psr[xb]])

        def chain_step(hd, d, pos, gc, state):
            c = gc % CPB
            xb = x_banks[d][c // 4]
            xv = ps[xb][:, (c % 4) * 128:(c % 4 + 1) * 128]
            bb, sl = (pos // CPB) % 2, pos % CPB
            out_ = Tb[d][bb][:, sl, :]
            if pos == 0:
                fw.op("dve", lambda e: e.tensor_copy(out=out_, in_=xv), r=[psr[xb]], w=[r_Tb[d][bb]])
                return
            if d == 0:
                wcol, rw = Wt[:, 0, gc - 1:gc], r_W
            elif gc == NCH - 1:
                wcol, rw = Wj[:, 0:1], r_Wj
            else:
                wcol, rw = Wt[:, 1, gc:gc + 1], r_W
            pb, psl = ((pos - 1) // CPB) % 2, (pos - 1) % CPB
            prev = Tb[d][pb][:, psl, :]
            fw.op("dve", lambda e: e.scalar_tensor_tensor(out=out_, in0=prev, scalar=wcol, in1=xv, op0=ALU.mult, op1=ALU.add),
                  r=[r_Tb[d][pb], rw, psr[xb]], w=[r_Tb[d][bb]])

        def cast_block(d, bidx):
            bb = bidx % 2
            fw.op("act", lambda e: e.copy(out=Sp[d][bb][:].rearrange("p c v -> p (c v)"),
                                          in_=Tb[d][bb][:].rearrange("p c v -> p (c v)")),
                  r=[r_Tb[d][bb]], w=[r_Sp[d][bb]])

        def stageB(hd, d, step, blk):
            sl = step % 2
            p = blk // 4
            ov = ps[O_B][:, 128 * d:128 * d + 128]
            mm = [(ov, V[:, blk, :], AT[d][step % 3][:], [r_V[p], r_AT[d][step % 3]])]
            cord = corder[d]
            for ci in range(CPB):
                pos = step * CPB + ci
                if pos == 0:
                    continue
                c = cord[ci]
                pb, psl = ((pos - 1) // CPB) % 2, (pos - 1) % CPB
                tq = slice(blk * 128 + c * CH, blk * 128 + (c + 1) * CH)
                mm.append((ov[:, c * CH:(c + 1) * CH], Sp[d][pb][:, psl, :], Qw[d][:, tq], [r_Sp[d][pb], r_Qw[d][p]]))
            for n_, (o_, l_, r_, rr) in enumerate(mm):
                fw.op("pe", lambda e, o_=o_, l_=l_, r_=r_, n_=n_: e.matmul(o_, l_, r_, start=(n_ == 0), stop=(n_ == len(mm) - 1)),
                      r=rr, w=[psr[O_B]], inc=(n_ == len(mm) - 1))
            return ov

        def readout1(hd, blk, k):
            fw.op("act", lambda e: e.activation(out=sq[k][:], in_=osum[k][:], func=AF.Square), r=[r_osum[k]], w=[r_sq[k]])
            fw.op("pe", lambda e: e.matmul(ps[SS_B][:, 256:384], P.ones_bf[:], sq[k][:], start=True, stop=True),
                  r=[r_sq[k], P.r_const], w=[psr[SS_B]])
            fw.op("act", lambda e: e.activation(out=rt[k][:], in_=ps[SS_B][:, 256:384], func=AF.Ln, scale=1.0 / 128, bias=P.epsc[:, 0:1]),
                  r=[psr[SS_B], P.r_const], w=[r_rt[k]])
            fw.op("act", lambda e: e.activation(out=rt[k][:], in_=rt[k][:], func=AF.Exp, scale=-0.5), r=[r_rt[k]], w=[r_rt[k]])

        def readout2(hd, blk, k):
            p = blk // 4
            tb = slice(blk * 128, (blk + 1) * 128)
            fw.op("dve", lambda e: e.scalar_tensor_tensor(out=osum[k][:], in0=osum[k][:], scalar=gain[:, 0:1], in1=rt[k][:],
                                                          op0=ALU.mult, op1=ALU.mult), r=[r_osum[k], r_rt[k], r_lc], w=[r_osum[k]])
            fw.op("pool", lambda e: e.tensor_tensor(out=ozp[k][:], in0=osum[k][:], in1=zs[:, tb], op=ALU.mult),
                  r=[r_osum[k], r_zs[p]], w=[r_ozp[k]])
            fw.dma("sp", P.ozT[hd, :, tb], ozp[k][:], r=[r_ozp[k]], w=[P.r_ozT[hd][p]])

        blks = [blocks_of(o) for o in orders]
        nsteps = len(blks[0])
        bpos = [{b: n for n, b in enumerate(bl)} for bl in blks]
        corder = [list(range(CPB)), list(range(CPB - 1, -1, -1))]
        _NHD = int(os.environ.get("HG_NH", NH))
        load_weights(0)
        ridx = 0
        pre_hk = [None]
        for hd in range(_NHD):
            if hd + 1 < NH:
                load_weights(hd + 1)
            hk = pre_hk[0] if pre_hk[0] is not None else load_piece(0)
            pre_hk[0] = None
            for p in range(NP):
                hk_next = load_piece(p + 1) if p + 1 < NP else None
                prep(hd, p, hk)
                hk = hk_next
            if hd + 1 < _NHD:
                pre_hk[0] = load_piece(0)
            junction()
            for d in range(2):
                build_vx(d, 0, blks[d][0])
                build_vx(d, 1, blks[d][1])
                stageA_pre(hd, d, 0, blks[d][0])
            L1, L2 = [], []
            for step in range(nsteps + 1):
                if step < nsteps:
                    for d in range(2):
                        stageX(hd, d, step, blks[d][step])
                    for ci in range(CPB):
                        for d in range(2):
                            blk = blks[d][step]
                            chain_step(hd, d, step * CPB + ci, blk * CPB + corder[d][ci], None)
                    if step + 2 < nsteps:
                        for d in range(2):
                            build_vx(d, step + 2, blks[d][step + 2])
                    if step + 1 < nsteps:
                        for d in range(2):
                            stageA_pre(hd, d, step + 1, blks[d][step + 1])
                run2, L2 = L2, []
                for fn in run2:
                    fn()
                run1, L1 = L1, []
                for (hd_, blk_, k_) in run1:
                    readout1(hd_, blk_, k_)
                    L2.append(lambda hd_=hd_, blk_=blk_, k_=k_: readout2(hd_, blk_, k_))
                if step >= 1:
                    pstep = step - 1
                    for d in range(2):
                        blk = blks[d][pstep]
                        ov = stageB(hd, d, pstep, blk)
                        tb = slice(blk * 128, (blk + 1) * 128)
                        if bpos[d][blk] < bpos[1 - d][blk]:
                            fw.op("act", lambda e, ov=ov, tb=tb: e.copy(out=oacc[:, tb], in_=ov), r=[psr[O_B]], w=[r_oacc[blk]])
                        else:
                            k = ridx % NRO
                            ridx += 1
                            fw.op("dve", lambda e, ov=ov, tb=tb, k=k: e.tensor_tensor(
                                out=osum[k][:], in0=ov, in1=oacc[:, tb], op=ALU.add),
                                r=[psr[O_B], r_oacc[blk]], w=[r_osum[k]])
                            L1.append((hd, blk, k))
                if step < nsteps:
                    for d in range(2):
                        cast_block(d, step)
            for (hd_, blk_, k_) in L1:
                readout1(hd_, blk_, k_)
                L2.append(lambda hd_=hd_, blk_=blk_, k_=k_: readout2(hd_, blk_, k_))
            for fn in L2:
                fn()
        fw.barrier()


def from_mla(P, j, hT, r_hT, need_ctx):
    nc, fw = P.nc, P.fw
    NP = len(PIECES)
    ps, psr = P.ps, P.psr
    W = P.mla_w_in[j]
    QR, KVR, RD = 256, 128, 64
    with ExitStack() as es:
        sb = lambda n, s, d=F32: es.enter_context(_sbt(nc, n, s, d))
        qn = sb("ml_qn", [128, 2, T], BF16)
        kvn = sb("ml_kvn", [128, T], BF16)
        kpeT = sb("ml_kpe", [128, T], BF16)
        r_qn = [Res() for p in range(NP)]
        r_kvn = [Res() for p in range(NP)]
        r_kpe = [Res() for p in range(NP)]
        fw.op("pool", lambda e: e.memset(kpeT[64:128, :], 0.0), w=r_kpe)
        wqb = sb("ml_wqb", [128, 2, NH * 192], BF16)
        wqs = sb("ml_wqs", [128, 2, NH, 64], BF16)
        wkvb = sb("ml_wkvb", [128, NH * 256], BF16)
        qanT = sb("ml_qan", [128, 2])
        kvanT = sb("ml_kvan", [128, 1])
        r_w2 = Res("ml_w2")
        stg = [sb("ml_stg%d" % k, [128, 2048]) for k in range(2)]
        r_stg = [Res(), Res()]
        stg_n = [0]

        def load_cast(dst, src, shape):
            k = stg_n[0] % 2
            stg_n[0] += 1
            n = int(np.prod(shape[1:]))
            sv = stg[k][:, 0:n]
            if len(shape) == 3:
                sv = sv.rearrange("p (a b) -> p a b", b=shape[2])
            fw.dma("sp", sv, src, w=[r_stg[k]])
            return sv, k

        with ExitStack() as es1:
            sb1 = lambda n, s, d=F32: es1.enter_context(_sbt(nc, n, s, d))
            wqa = sb1("ml_wqa", [128, 8, QR], BF16)
            wkva = sb1("ml_wkva", [128, 8, KVR], BF16)
            wkpe = sb1("ml_wkpe", [128, 8, RD], BF16)
            wkpes = sb1("ml_wkpes", [128, 8, RD], BF16)
            wz = sb1("ml_wz", [128, 8, DI], BF16)
            r_w1 = Res("ml_w1")
            for (dst, c0, n) in ((wqa, 0, QR), (wkva, QR, KVR), (wkpe, QR + KVR, RD)):
                sv, k = load_cast(dst, W[:, c0:c0 + n].rearrange("(kc p) n -> p kc n", p=128), [128, 8, n])
                fw.op("pool", lambda e, dst=dst, sv=sv: e.tensor_copy(out=dst[:], in_=sv), r=[r_stg[k]], w=[r_w1])
            fw.op("pool", lambda e: e.tensor_copy(out=wkpes[:, :, 0:32], in_=wkpe[:, :, 32:64]), r=[r_w1], w=[r_w1])
            fw.op("pool", lambda e: e.tensor_copy(out=wkpes[:, :, 32:64], in_=wkpe[:, :, 0:32]), r=[r_w1], w=[r_w1])
            zc0 = QR + KVR + RD
            for cb in range(8):
                sv, k = load_cast(wz, W[:, zc0 + cb * 256:zc0 + (cb + 1) * 256].rearrange("(kc p) n -> p kc n", p=128), [128, 8, 256])
                _cast(fw, cb, wz[:, :, cb * 256:(cb + 1) * 256], sv, [r_stg[k]], [r_w1])
            for cb in range(3):
                sv, k = load_cast(wqb, P.mla_w_qb[j, :, cb * 1024:(cb + 1) * 1024].rearrange("(g p) n -> p g n", p=128), [128, 2, 1024])
                _cast(fw, cb + 1, wqb[:, :, cb * 1024:(cb + 1) * 1024], sv, [r_stg[k]], [r_w2])
            wqbv = wqb[:].rearrange("p g (h c) -> p g h c", c=192)
            fw.op("pool", lambda e: e.tensor_copy(out=wqs[:, :, :, 0:32], in_=wqbv[:, :, :, 160:192]), r=[r_w2], w=[r_w2])
            fw.op("pool", lambda e: e.tensor_copy(out=wqs[:, :, :, 32:64], in_=wqbv[:, :, :, 128:160]), r=[r_w2], w=[r_w2])
            for cb in range(2):
                sv, k = load_cast(wkvb, P.mla_w_kvb[j, :, cb * 2048:(cb + 1) * 2048], [128, 2048])
                _cast(fw, cb + 2, wkvb[:, cb * 2048:(cb + 1) * 2048], sv, [r_stg[k]], [r_w2])
            fw.dma("sp", qanT[:], P.mla_qa_norm[j].rearrange("(g p) -> p g", p=128), w=[r_w2], allow_slow_non_contiguous=True)
            fw.dma("sp", kvanT[:], P.mla_kva_norm[j].rearrange("(p o) -> p o", o=1), w=[r_w2])
            hp = [sb1("ml_hp%d" % k, [128, 8, 512], BF16) for k in range(2)]
            r_hp = [Res(), Res()]
            cosp = [sb1("ml_cos%d" % k, [64, 512]) for k in range(2)]
            sinp = [sb1("ml_sin%d" % k, [64, 512]) for k in range(2)]
            r_cs = [Res(), Res()]
            sqq = sb1("ml_sqq", [128, 2, 512], BF16)
            sqk = sb1("ml_sqk", [128, 512], BF16)
            rq = sb1("ml_rq", [128, 512])
            rk = sb1("ml_rk", [128, 512])
            t1 = sb1("ml_t1", [64, 512])
            t2 = sb1("ml_t2", [64, 512])
            zst = [sb1("ml_zst%d" % k, [128, DI], BF16) for k in range(2)]
            r_sqq, r_sqk, r_rq, r_rk, r_t1, r_t2 = [Res() for _ in range(6)]
            r_zst = [Res(), Res()]
            r_zs_d = [Res() for t in range(NT)]
            P.r_zs_d = r_zs_d
            zcount = 0
            def m1_load(p):
                s0, n = PIECES[p]
                k = p % 2
                fw.dma("sp", hp[k][:, :, 0:n], hT[:, :, s0:s0 + n], r=[r_hT[t] for t in range(s0 // 128, (s0 + n) // 128)], w=[r_hp[k]])
                fw.dma("sp", cosp[k][:, 0:n], P.k_cos[:, s0:s0 + n], w=[r_cs[k]])
                fw.dma("sp", sinp[k][:, 0:n], P.k_sin[:, s0:s0 + n], w=[r_cs[k]])
            m1_load(0)
            for p in range(NP):
                s0, n = PIECES[p]
                k = p % 2
                tsl = slice(s0, s0 + n)
                if p + 1 < NP:
                    m1_load(p + 1)
                hpk = hp[k]
                for g in range(2):
                    for kc in range(8):
                        fw.op("pe", lambda e, g=g, kc=kc: e.matmul(ps[g][:, 0:n], wqa[:, kc, g * 128:(g + 1) * 128], hpk[:, kc, 0:n],
                                                                  start=(kc == 0), stop=(kc == 7)),
                              r=[r_w1, r_hp[k]], w=[psr[g]], inc=(kc == 7))
                for kc in range(8):
                    fw.op("pe", lambda e, kc=kc: e.matmul(ps[2][:, 0:n], wkva[:, kc, :], hpk[:, kc, 0:n], start=(kc == 0), stop=(kc == 7)),
                          r=[r_w1, r_hp[k]], w=[psr[2]], inc=(kc == 7))
                for (wt_, bank) in ((wkpe, 3), (wkpes, 4)):
                    for kc in range(8):
                        fw.op("pe", lambda e, kc=kc, wt_=wt_, bank=bank: e.matmul(ps[bank][0:64, 0:n], wt_[:, kc, :], hpk[:, kc, 0:n],
                                                                                start=(kc == 0), stop=(kc == 7)),
                              r=[r_w1, r_hp[k]], w=[psr[bank]], inc=(kc == 7))
                for g in range(2):
                    fw.op("act", lambda e, g=g: e.activation(out=sqq[:, g, 0:n], in_=ps[g][:, 0:n], func=AF.Square), r=[psr[g]], w=[r_sqq])
                fw.op("act", lambda e: e.activation(out=sqk[:, 0:n], in_=ps[2][:, 0:n], func=AF.Square), r=[psr[2]], w=[r_sqk])
                for g in range(2):
                    fw.op("pe", lambda e, g=g: e.matmul(ps[5][:, 0:n], P.ones_bf[:], sqq[:, g, 0:n], start=(g == 0), stop=(g == 1)),
                          r=[r_sqq, P.r_const], w=[psr[5]], inc=(g == 1))
                fw.op("pe", lambda e: e.matmul(ps[6][:, 0:n], P.ones_bf[:], sqk[:, 0:n], start=True, stop=True),
                      r=[r_sqk, P.r_const], w=[psr[6]])
                fw.op("act", lambda e: e.activation(out=rq[:, 0:n], in_=ps[5][:, 0:n], func=AF.Sqrt, scale=1.0 / QR, bias=P.epsc[:, 0:1]),
                      r=[psr[5], P.r_const], w=[r_rq])
                fw.op("act", lambda e: e.activation(out=rk[:, 0:n], in_=ps[6][:, 0:n], func=AF.Sqrt, scale=1.0 / KVR, bias=P.epsc[:, 0:1]),
                      r=[psr[6], P.r_const], w=[r_rk])
                fw.op("dve", lambda e: e.reciprocal(out=rq[:, 0:n], in_=rq[:, 0:n]), r=[r_rq], w=[r_rq])
                fw.op("dve", lambda e: e.reciprocal(out=rk[:, 0:n], in_=rk[:, 0:n]), r=[r_rk], w=[r_rk])
                for g in range(2):
                    fw.op("dve", lambda e, g=g: e.scalar_tensor_tensor(out=qn[:, g, tsl], in0=ps[g][:, 0:n], scalar=qanT[:, g:g + 1],
                                                                      in1=rq[:, 0:n], op0=ALU.mult, op1=ALU.mult),
                          r=[psr[g], r_rq, r_w2], w=[r_qn[p]])
                fw.op("dve", lambda e: e.scalar_tensor_tensor(out=kvn[:, tsl], in0=ps[2][:, 0:n], scalar=kvanT[:, 0:1],
                                                              in1=rk[:, 0:n], op0=ALU.mult, op1=ALU.mult),
                      r=[psr[2], r_rk, r_w2], w=[r_kvn[p]])
                fw.op("dve", lambda e: e.tensor_tensor(out=t1[:, 0:n], in0=ps[3][0:64, 0:n], in1=cosp[k][:, 0:n], op=ALU.mult),
                      r=[psr[3], r_cs[k]], w=[r_t1])
                fw.op("dve", lambda e: e.tensor_tensor(out=t2[:, 0:n], in0=ps[4][0:64, 0:n], in1=sinp[k][:, 0:n], op=ALU.mult),
                      r=[psr[4], r_cs[k]], w=[r_t2])
                fw.op("pool", lambda e: e.tensor_tensor(out=kpeT[0:64, tsl], in0=t1[:, 0:n], in1=t2[:, 0:n], op=ALU.add),
                      r=[r_t1, r_t2], w=[r_kpe[p]])
                for ti in range(n // 128):
                    tt = s0 // 128 + ti
                    if tt < 2 and not need_ctx:
                        continue
                    zk = zcount % 2
                    zcount += 1
                    for cb in range(4):
                        bank = 4 + cb
                        for kc in range(8):
                            fw.op("pe", lambda e, kc=kc, cb=cb, bank=bank, ti=ti: e.matmul(
                                ps[bank][:, :], hpk[:, kc, ti * 128:(ti + 1) * 128], wz[:, kc, cb * 512:(cb + 1) * 512],
                                start=(kc == 0), stop=(kc == 7)), r=[r_w1, r_hp[k]], w=[psr[bank]], inc=(kc == 7))
                        fw.op("act", lambda e, cb=cb, bank=bank, zk=zk: e.activation(
                            out=zst[zk][:, cb * 512:(cb + 1) * 512], in_=ps[bank][:, :], func=AF.Silu), r=[psr[bank]], w=[r_zst[zk]])
                    fw.dma("pool", P.zs_d[tt * 128:(tt + 1) * 128, :], zst[zk][:], r=[r_zst[zk]], w=[r_zs_d[tt]])
            fw.barrier()

        knT = [sb("ml_knT%d" % k, [128, T], BF16) for k in range(2)]
        V1 = [sb("ml_V1%d" % k, [128, NT, 132], BF16) for k in range(2)]
        qnT = [sb("ml_qnT%d" % k, [128, T], BF16) for k in range(2)]
        qpT = [sb("ml_qpT%d" % k, [128, T], BF16) for k in range(2)]
        r_knT, r_V1, r_qnT, r_qpT = [[Res(), Res()] for _ in range(4)]
        cosq = [sb("ml_cq%d" % k, [64, 512]) for k in range(2)]
        sinq = [sb("ml_sq%d" % k, [64, 512]) for k in range(2)]
        r_csq = [Res(), Res()]
        u1 = sb("ml_u1", [64, 512])
        u2 = sb("ml_u2", [64, 512])
        r_u1, r_u2 = Res(), Res()
        NPT = 3
        PT = [sb("ml_PT%d" % k, [128, 512], BF16) for k in range(NPT)]
        r_PT = [Res() for k in range(NPT)]
        zt = [sb("ml_zt%d" % k, [128, 4, 128], BF16) for k in range(3)]
        r_zt = [Res(), Res(), Res()]
        rinv = [sb("ml_ri%d" % k, [128, 4]) for k in range(2)]
        r_rinv = [Res(), Res()]
        ozt = [[sb("ml_ozt%d%d" % (a, k), [128, 128], BF16) for k in range(2)] for a in range(2)]
        r_ozt = [[Res(), Res()], [Res(), Res()]]
        ozs = [sb("ml_ozs%d" % k, [128, 512], BF16) for k in range(2)]
        r_ozs = [Res(), Res()]
        for k in range(2):
            fw.op("dve", lambda e, k=k: e.memset(V1[k][:, :, 128:132], 1.0), w=[r_V1[k]])
            fw.op("pool", lambda e, k=k: e.memset(qpT[k][64:128, :], 0.0), w=[r_qpT[k]])
        SCB = [0, 1, 2]
        OAB = [3, 4, 5, 6]
        PJ = 7
        qtiles = [(LC + i * 512, 512, list(range(NT))) for i in range(L // 512)]
        if need_ctx:
            qtiles = [(0, LC, [0, 1])] + qtiles
        tok_lo = 0 if need_ctx else LC
        csn = [0]
        n_oz = [0]
        n_it = [0]
        nheads = int(os.environ.get("ML_NH", NH))

        def project_steps(hd):
            hb = hd % 2
            b = PJ
            steps = []
            for p in range(NP):
                s0, n = PIECES[p]
                tsl = slice(s0, s0 + n)

                def st_k(s0=s0, n=n, tsl=tsl, p=p):
                    fw.op("pe", lambda e: e.matmul(ps[b][:, 0:n], wkvb[:, hd * 256:hd * 256 + 128], kvn[:, tsl], start=True, stop=True),
                          r=[r_w2, r_kvn[p]], w=[psr[b]])
                    fw.op("dve", lambda e: e.tensor_copy(out=knT[hb][:, tsl], in_=ps[b][:, 0:n]), r=[psr[b]], w=[r_knT[hb]])

                def st_v(s0=s0, n=n, tsl=tsl, p=p):
                    for ti in range(n // 128):
                        fw.op("pe", lambda e, ti=ti: e.matmul(ps[b][:, ti * 128:(ti + 1) * 128], kvn[:, s0 + ti * 128:s0 + (ti + 1) * 128],
                                                             wkvb[:, hd * 256 + 128:hd * 256 + 256], start=True, stop=True),
                              r=[r_w2, r_kvn[p]], w=[psr[b]], inc=(ti == n // 128 - 1))
                    fw.op("dve", lambda e: e.tensor_copy(out=V1[hb][:, s0 // 128:(s0 + n) // 128, 0:128],
                                                         in_=ps[b][:, 0:n].rearrange("p (a v) -> p a v", v=128)), r=[psr[b]], w=[r_V1[hb]])
                steps += [st_k, st_v]
                if s0 + n <= tok_lo:
                    continue

                def st_qn(s0=s0, n=n, tsl=tsl, p=p):
                    for g in range(2):
                        fw.op("pe", lambda e, g=g: e.matmul(ps[b][:, 0:n], wqb[:, g, hd * 192:hd * 192 + 128], qn[:, g, tsl],
                                                           start=(g == 0), stop=(g == 1)),
                              r=[r_w2, r_qn[p]], w=[psr[b]], inc=(g == 1))
                    fw.op("dve", lambda e: e.tensor_copy(out=qnT[hb][:, tsl], in_=ps[b][:, 0:n]), r=[psr[b]], w=[r_qnT[hb]])

                def st_qp1(s0=s0, n=n, tsl=tsl, p=p):
                    ck = csn[0] % 2
                    fw.dma("sp", cosq[ck][:, 0:n], P.k_cos[:, tsl], w=[r_csq[ck]])
                    fw.dma("sp", sinq[ck][:, 0:n], P.k_sin[:, tsl], w=[r_csq[ck]])
                    for g in range(2):
                        fw.op("pe", lambda e, g=g: e.matmul(ps[b][0:64, 0:n], wqb[:, g, hd * 192 + 128:hd * 192 + 192], qn[:, g, tsl],
                                                           start=(g == 0), stop=(g == 1)),
                              r=[r_w2, r_qn[p]], w=[psr[b]], inc=(g == 1))
                    fw.op("dve", lambda e, ck=ck: e.tensor_tensor(out=u1[:, 0:n], in0=ps[b][0:64, 0:n], in1=cosq[ck][:, 0:n], op=ALU.mult),
                          r=[psr[b], r_csq[ck]], w=[r_u1])

                def st_qp2(s0=s0, n=n, tsl=tsl, p=p):
                    ck = csn[0] % 2
                    csn[0] += 1
                    for g in range(2):
                        fw.op("pe", lambda e, g=g: e.matmul(ps[b][0:64, 0:n], wqs[:, g, hd, :], qn[:, g, tsl],
                                                           start=(g == 0), stop=(g == 1)),
                              r=[r_w2, r_qn[p]], w=[psr[b]], inc=(g == 1))
                    fw.op("dve", lambda e, ck=ck: e.tensor_tensor(out=u2[:, 0:n], in0=ps[b][0:64, 0:n], in1=sinq[ck][:, 0:n], op=ALU.mult),
                          r=[psr[b], r_csq[ck]], w=[r_u2])
                    fw.op("pool", lambda e: e.tensor_tensor(out=qpT[hb][0:64, tsl], in0=u1[:, 0:n], in1=u2[:, 0:n], op=ALU.add),
                          r=[r_u1, r_u2], w=[r_qpT[hb]])
                steps += [st_qn, st_qp1, st_qp2]
            return steps

        zk_of = {}

        def scores(it):
            hd, q0, nq, kt, ki, nk, idx = it
            hb = hd % 2
            if ki == 0:
                zk, zq = n_oz[0] % 2, n_oz[0] % 3
                n_oz[0] += 1
                zk_of[(hd, q0)] = (zk, zq)
                fw.dma("sp", zt[zq][:, 0:nq // 128, :],
                       P.zs_d[q0:q0 + nq, hd * 128:(hd + 1) * 128].rearrange("(a p) v -> p a v", p=128),
                       r=[P.r_zs_d[t] for t in range(q0 // 128, (q0 + nq) // 128)], w=[r_zt[zq]])
            ks = slice(kt * 128, (kt + 1) * 128)
            sb_ = SCB[idx % 3]
            pk = idx % NPT
            fw.op("pe", lambda e: e.matmul(ps[sb_][:, 0:nq], knT[hb][:, ks], qnT[hb][:, q0:q0 + nq], start=True, stop=False),
                  r=[r_knT[hb], r_qnT[hb]], w=[psr[sb_]], inc=False)
            fw.op("pe", lambda e: e.matmul(ps[sb_][:, 0:nq], kpeT[:, ks], qpT[hb][:, q0:q0 + nq], start=False, stop=True),
                  r=[r_kpe[kt // 4], r_qpT[hb]], w=[psr[sb_]])
            fw.op("act", lambda e: e.activation(out=PT[pk][:, 0:nq], in_=ps[sb_][:, 0:nq], func=AF.Exp, scale=MLA_SCALE),
                  r=[psr[sb_]], w=[r_PT[pk]])

        def pv(it, i):
            hd, q0, nq, kt, ki, nk, idx = it
            hb = hd % 2
            pk = idx % NPT
            nqs = nq // 128
            for qi in range(nqs):
                ob = OAB[qi]
                fw.op("pe", lambda e, ob=ob, qi=qi: e.matmul(
                    ps[ob][:, 0:129], PT[pk][:, qi * 128:(qi + 1) * 128], V1[hb][:, kt, 0:129],
                    start=(ki == 0), stop=(ki == nk - 1)),
                    r=[r_PT[pk], r_V1[hb]], w=[psr[ob]], inc=(ki == nk - 1 or qi == nqs - 1))
            if ki == nk - 1:
                finalize(hd, q0, nq, i)

        oev = [sb("ml_oev%d" % k, [128, 4, 132]) for k in range(2)]
        r_oev = [Res(), Res()]
        later = {}

        def finalize(hd, q0, nq, i):
            nqs = nq // 128
            zk, zq = zk_of[(hd, q0)]
            for qi in range(nqs):
                ob = OAB[qi]
                fw.op("dve", lambda e, ob=ob, qi=qi: e.tensor_copy(out=oev[zk][:, qi, 0:129], in_=ps[ob][:, 0:129]),
                      r=[psr[ob]], w=[r_oev[zk]])

            def f2():
                fw.op("dve", lambda e: e.reciprocal(out=rinv[zk][:, 0:nqs], in_=oev[zk][:, 0:nqs, 128]),
                      r=[r_oev[zk]], w=[r_rinv[zk]])

            def f3(qi):
                def run():
                    tk = qi % 2
                    fw.op("dve", lambda e: e.scalar_tensor_tensor(
                        out=ozt[zk][tk][:], in0=oev[zk][:, qi, 0:128], scalar=rinv[zk][:, qi:qi + 1], in1=zt[zq][:, qi, :],
                        op0=ALU.mult, op1=ALU.mult), r=[r_oev[zk], r_rinv[zk], r_zt[zq]], w=[r_ozt[zk][tk]])
                return run

            def f4(qi):
                def run():
                    tk = qi % 2
                    tv = ps[PJ][:, :].bitcast(BF16)[:, 0:128]
                    fw.op("pe", lambda e: e.transpose(out=tv, in_=ozt[zk][tk][:], identity=P.ident[:]),
                          r=[r_ozt[zk][tk], P.r_const], w=[psr[PJ]])
                    fw.op("dve", lambda e: e.tensor_copy(out=ozs[zk][:, qi * 128:(qi + 1) * 128], in_=tv),
                          r=[psr[PJ]], w=[r_ozs[zk]])
                    if qi == nqs - 1:
                        fw.dma("pool", P.ozT[hd, :, q0:q0 + nq], ozs[zk][:, 0:nq], r=[r_ozs[zk]],
                               w=[P.r_ozT[hd][pp] for pp in range(q0 // 512, (q0 + nq - 1) // 512 + 1)])
                return run
            later.setdefault(i + 1, []).append(f2)
            for qi in range(nqs):
                later.setdefault(i + 2 + 2 * qi, []).append(f3(qi))
                later.setdefault(i + 4 + 2 * qi, []).append(f4(qi))

        its = []
        for hd in range(nheads):
            for (q0, nq, ktiles) in qtiles:
                for ki, kt in enumerate(ktiles):
                    its.append((hd, q0, nq, kt, ki, len(ktiles), len(its)))
        DEPTHP = 2
        for st_ in project_steps(0):
            st_()
        nxt_steps = []
        per_head = len(its) // nheads
        total = len(its) + DEPTHP + 16
        for i in range(total):
            if i < len(its):
                hd_i = its[i][0]
                if i % per_head == 0 and hd_i + 1 < nheads:
                    nxt_steps = project_steps(hd_i + 1)
                scores(its[i])
            if 0 <= i - DEPTHP < len(its):
                pv(its[i - DEPTHP], i)
            for fn in later.pop(i, []):
                fn()
            if nxt_steps and i % per_head >= 8 and i % 2 == 0:
                nxt_steps.pop(0)()
        assert not later and not nxt_steps
        fw.barrier()


def _consts():
    ident = np.eye(128, dtype=np.float32)
    s = np.arange(128)[:, None]
    t = np.arange(128)[None, :]
    same = (s // CH) == (t // CH)
    maskf = (same & (s <= t)).astype(np.float32)
    maskb = (same & (s >= t)).astype(np.float32)
    GRID_W = 64
    rows = L // GRID_W
    row = np.repeat(np.arange(rows, dtype=np.float32), GRID_W)
    col = np.tile(np.arange(GRID_W, dtype=np.float32), rows)
    inv = (10000.0 ** (-np.arange(0, 32, 2, dtype=np.float32) / 32)).astype(np.float32)
    ang = np.concatenate([row[:, None] * inv, col[:, None] * inv], axis=-1)
    cos = np.cos(ang).astype(np.float32).T
    sin = np.sin(ang).astype(np.float32).T
    kcos = np.ones((64, T), np.float32)
    ksin = np.zeros((64, T), np.float32)
    kcos[0:32, LC:] = cos
    kcos[32:64, LC:] = cos
    ksin[0:32, LC:] = -sin
    ksin[32:64, LC:] = sin
    cm3 = np.zeros((128, CPB, 128), np.float32)
    for c in range(CPB):
        cm3[c * CH:(c + 1) * CH, c, :] = 1.0
    return {"k_cm3": cm3.reshape(128, CPB * 128), "k_ident": ident, "k_maskf": maskf, "k_maskb": maskb, "k_cos": kcos, "k_sin": ksin}


_PROG_CACHE = {}


def _get_prog(n_layers=DEPTH, dbg=False):
    key = (n_layers, dbg)
    if key not in _PROG_CACHE:
        _PROG_CACHE[key] = Prog(n_layers, dbg)
    return _PROG_CACHE[key]


def _in_maps(inputs):
    f = lambda a: np.ascontiguousarray(np.asarray(a, dtype=np.float32))
    shared = {k: f(inputs[k]) for k in ("c_ctx", "ada_w", "ada_b", "norm_pre", "norm_post", "hg_w_in",
                                         "hg_lb_logits", "hg_o_norm", "hg_w_out", "mla_w_in", "mla_qa_norm",
                                         "mla_w_qb", "mla_kva_norm", "mla_w_kvb", "mla_w_out")}
    shared.update(_consts())
    x = f(inputs["x"])
    c = f(inputs["c"])
    ctx = f(inputs["ctx"])
    maps = []
    for b in range(8):
        m = dict(shared)
        m["x"] = x[b]
        m["c"] = c[b]
        m["ctx"] = ctx[b]
        maps.append(m)
    return maps


def kernel(**inputs):
    prog = _get_prog()
    res = run_bass_kernel_spmd(prog.nc, _in_maps(inputs), core_ids=list(range(8)))
    return np.stack([np.asarray(r["out"], dtype=np.float32) for r in res.results], axis=0)
```

```python
import os
import numpy as np
from contextlib import ExitStack
import concourse.bass as bass
import concourse.mybir as mybir
from concourse.bass_utils import run_bass_kernel_spmd

F32 = mybir.dt.float32
BF16 = mybir.dt.bfloat16
AF = mybir.ActivationFunctionType
ALU = mybir.AluOpType

D = 1024
L = 4096
LC = 256
T = L + LC
NT = T // 128
DI = 2048
NH = 16
DEPTH = 4
EPS = 1e-6
CH = 16
NCH = T // CH
CPB = 128 // CH
PIECES = [(s, min(512, T - s)) for s in range(0, T, 512)]
MLA_SCALE = (128 + 64) ** -0.5
STRICT = False


_UID = [0]


def _sbt(nc, name, shape, dtype):
    _UID[0] += 1
    return nc.sbuf_tensor("%s_u%d" % (name, _UID[0]), shape, dtype)


class Res:
    __slots__ = ("name", "w", "r")

    def __init__(self, name=""):
        self.name = name
        self.w = None
        self.r = {}


class _Eng:
    def __init__(self, key, h, sem, self_sync):
        self.key = key
        self.h = h
        self.sem = sem
        self.self_sync = self_sync
        self.count = 0
        self.waited = {}


class FW:
    def __init__(self, nc, es, ndma=8):
        self.nc = nc
        self.sem = {}
        self.E = {}
        for key, h, ss in (("pe", nc.tensor, False), ("act", nc.scalar, True),
                           ("dve", nc.vector, True), ("pool", nc.gpsimd, True),
                           ("sp", nc.sync, False)):
            s = es.enter_context(nc.semaphore("sem_" + key))
            self.sem[key] = s
            self.E[key] = _Eng(key, h, s, ss)
        self.dq = {}
        for q in ("sp", "pool", "act"):
            keys = []
            for k in range(ndma):
                kk = "dq_%s%d" % (q, k)
                self.sem[kk] = es.enter_context(nc.semaphore(kk))
                keys.append(kk)
            self.dq[q] = [keys, 0]
        self.n_inst = 0

    @staticmethod
    def _deps(eng, r, w, dma=False):
        deps = {}

        def add(kv):
            k, v = kv
            if deps.get(k, 0) < v:
                deps[k] = v
        for t in r:
            if t.w is not None:
                add(t.w)
        strict = dma or (STRICT and eng.self_sync)
        for t in w:
            if t.w is not None and (strict or t.w[0] != eng.key):
                add(t.w)
            for k, v in t.r.items():
                if strict or k != eng.key:
                    add((k, v))
        return deps

    def _wait(self, eng, deps, dma=False):
        for k, v in deps.items():
            if k == eng.key and not (eng.self_sync or dma):
                continue
            if eng.waited.get(k, 0) >= v:
                continue
            eng.h.wait_ge(self.sem[k], v)
            eng.waited[k] = v

    @staticmethod
    def _mark(me, r, w):
        for t in r:
            if t.r.get(me[0], 0) < me[1]:
                t.r[me[0]] = me[1]
        for t in w:
            t.w = me
            t.r = {}

    def op(self, e, fn, r=(), w=(), inc=True):
        eng = self.E[e]
        self._wait(eng, self._deps(eng, r, w))
        inst = fn(eng.h)
        self.n_inst += 1
        if inc:
            eng.count += 1
            inst.then_inc(eng.sem, 1)
            me = (eng.key, eng.count)
        else:
            me = (eng.key, eng.count + 1)
        self._mark(me, r, w)
        return inst

    def dma(self, q, out, in_, r=(), w=(), **kw):
        eng = self.E[q]
        keys, i = self.dq[q]
        P = len(keys)
        key = keys[i % P]
        val = 16 * (i // P + 1)
        deps = self._deps(eng, r, w, dma=True)
        if i >= P and deps.get(key, 0) < val - 16:
            deps[key] = val - 16
        self._wait(eng, deps, dma=True)
        eng.h.dma_start(out=out, in_=in_, **kw).then_inc(self.sem[key], 16)
        self.n_inst += 1
        self.dq[q][1] = i + 1
        self._mark((key, val), r, w)

    def barrier(self):
        deps = {}
        for k, e in self.E.items():
            if e.count > 0:
                deps[k] = e.count
        for q, (keys, i) in self.dq.items():
            P = len(keys)
            for j in range(max(0, i - P), i):
                deps[keys[j % P]] = 16 * (j // P + 1)
        for k, e in self.E.items():
            d = {kk: v for kk, v in deps.items() if kk != k}
            self._wait(e, d, dma=True)


class Prog:
    def __init__(self, n_layers=DEPTH, dbg=False):
        self.n_layers = n_layers
        self.dbg = dbg
        self.nc = nc = bass.Bass("TRN2", target_bir_lowering=False)
        dt = lambda name, shape, dtype=F32, kind="ExternalInput": nc.dram_tensor(name, list(shape), dtype, kind=kind).ap()
        self.x_in = dt("x", [L, D])
        self.ctx_in = dt("ctx", [LC, D])
        self.c_in = dt("c", [D])
        self.cctx_in = dt("c_ctx", [D])
        self.ada_w = dt("ada_w", [DEPTH, D, 3 * D])
        self.ada_b = dt("ada_b", [DEPTH, 3 * D])
        self.norm_pre = dt("norm_pre", [DEPTH, D])
        self.norm_post = dt("norm_post", [DEPTH, D])
        self.hg_w_in = dt("hg_w_in", [2, D, 5 * DI])
        self.hg_lb = dt("hg_lb_logits", [2, 2, DI])
        self.hg_o_norm = dt("hg_o_norm", [2, 128])
        self.hg_w_out = dt("hg_w_out", [2, DI, D])
        self.mla_w_in = dt("mla_w_in", [2, D, 2496])
        self.mla_qa_norm = dt("mla_qa_norm", [2, 256])
        self.mla_w_qb = dt("mla_w_qb", [2, 256, 3072])
        self.mla_kva_norm = dt("mla_kva_norm", [2, 128])
        self.mla_w_kvb = dt("mla_w_kvb", [2, 128, 4096])
        self.mla_w_out = dt("mla_w_out", [2, DI, D])
        self.k_ident = dt("k_ident", [128, 128])
        self.k_maskf = dt("k_maskf", [128, 128])
        self.k_maskb = dt("k_maskb", [128, 128])
        self.k_cm3 = dt("k_cm3", [128, CPB * 128])
        self.k_cos = dt("k_cos", [64, T])
        self.k_sin = dt("k_sin", [64, T])
        self.out = dt("out", [L, D], F32, "ExternalOutput")
        okind = "ExternalOutput" if dbg else "Internal"
        self.xres = dt("xres", [T, D], F32, okind)
        self.ozT = dt("ozT", [NH, 128, T], BF16, "Internal")
        self.zs_d = dt("zs_d", [T, DI], BF16, "Internal")
        self.hT_d = dt("hT_d", [128, 8, T], BF16, "Internal")
        self.build()

    def build(self):
        nc = self.nc
        with ExitStack() as es:
            self.fw = fw = FW(nc, es)
            self.ps = [es.enter_context(nc.psum_tensor("ps%d" % i, [128, 512], F32)) for i in range(8)]
            self.psr = [Res("ps%d" % i) for i in range(8)]
            self.ident = es.enter_context(_sbt(nc, "ident", [128, 128], BF16))
            self.ones_bf = es.enter_context(_sbt(nc, "ones_bf", [128, 128], BF16))
            self.maskf = es.enter_context(_sbt(nc, "maskf", [128, 128], F32))
            self.maskb = es.enter_context(_sbt(nc, "maskb", [128, 128], F32))
            self.epsc = es.enter_context(_sbt(nc, "epsc", [128, 1], F32))
            self.sc2 = es.enter_context(_sbt(nc, "sc2", [128, 8, 2], F32))
            self.AB = es.enter_context(_sbt(nc, "AB", [128, 2, 2, 8], F32))
            self.Gbc = es.enter_context(_sbt(nc, "Gbc", [128, 2, D], F32))
            self.r_const = Res("const")
            self.r_sc2 = Res("sc2")
            self.r_AB = Res("AB")
            self.r_Gbc = Res("Gbc")
            self.r_xres = [Res("xres%d" % i) for i in range(NT)]
            self.r_ozT = [[Res("oz%d_%d" % (h, p)) for p in range(len(PIECES))] for h in range(NH)]
            self.setup_consts()
            for i in range(self.n_layers):
                self.layer(i)
            if self.n_layers < DEPTH:
                self.copy_out()
            fw.barrier()

    def setup_consts(self):
        nc, fw = self.nc, self.fw
        with ExitStack() as es:
            st = es.enter_context(_sbt(nc, "cst_stage", [128, 128], F32))
            cst = es.enter_context(_sbt(nc, "cst_c", [128, 8, 2], F32))
            r_st = Res("st")
            r_c = Res("c")
            fw.dma("sp", st[:], self.k_ident[:, :], w=[r_st])
            fw.op("dve", lambda e: e.tensor_copy(out=self.ident[:], in_=st[:]), r=[r_st], w=[self.r_const])
            fw.op("dve", lambda e: e.memset(self.ones_bf[:], 1.0), w=[self.r_const])
            fw.op("dve", lambda e: e.memset(self.epsc[:], EPS), w=[self.r_const])
            fw.dma("sp", self.maskf[:], self.k_maskf[:, :], w=[self.r_const])
            fw.dma("sp", self.maskb[:], self.k_maskb[:, :], w=[self.r_const])
            fw.dma("sp", cst[:, :, 0], self.c_in.rearrange("(kc p) -> p kc", p=128), w=[r_c],
                   allow_slow_non_contiguous=True)
            fw.dma("sp", cst[:, :, 1], self.cctx_in.rearrange("(kc p) -> p kc", p=128), w=[r_c],
                   allow_slow_non_contiguous=True)
            fw.op("act", lambda e: e.activation(out=self.sc2[:], in_=cst[:], func=AF.Silu), r=[r_c], w=[self.r_sc2])
            for tt in range(NT):
                src = self.ctx_in[tt * 128:(tt + 1) * 128, :] if tt < 2 else self.x_in[(tt - 2) * 128:(tt - 1) * 128, :]
                fw.dma("sp", self.xres[tt * 128:(tt + 1) * 128, :], src, w=[self.r_xres[tt]])
            if os.environ.get("DBG_ZERO_OZ"):
                zt_ = es.enter_context(_sbt(nc, "dbgz", [128, T], BF16))
                rz = Res()
                fw.op("dve", lambda e: e.memset(zt_[:], 0.0), w=[rz])
                for h in range(NH):
                    fw.dma("sp", self.ozT[h], zt_[:], r=[rz], w=self.r_ozT[h])
            fw.barrier()

    def copy_out(self):
        fw = self.fw
        for tt in range(2, NT):
            fw.dma("sp", self.out[(tt - 2) * 128:(tt - 1) * 128, :], self.xres[tt * 128:(tt + 1) * 128, :],
                   r=[self.r_xres[tt]])

    def layer(self, i):
        fw = self.fw
        last = (i == DEPTH - 1)
        self.phase0_mod(i)
        fw.barrier()
        hT = self.hT_d
        r_hT = [Res("hT%d" % t) for t in range(NT)]
        self.phase1_prenorm(i, hT, r_hT)
        if i % 2 == 0:
            self.hgrn_mixer(i // 2, hT, r_hT)
        else:
            self.mla_mixer(i // 2, hT, r_hT, need_ctx=not last)
        fw.barrier()
        w_out = self.hg_w_out[i // 2] if i % 2 == 0 else self.mla_w_out[i // 2]
        self.phase3_out(i, w_out, last)
        fw.barrier()

    def phase0_mod(self, i):
        nc, fw = self.nc, self.fw
        with ExitStack() as es:
            sb = lambda n, s, d=F32: es.enter_context(_sbt(nc, n, s, d))
            wblk = [sb("adaw%d" % k, [128, 8, 512]) for k in range(2)]
            r_wblk = [Res("adaw%d" % k) for k in range(2)]
            screp = sb("screp", [128, 2, 8, 128])
            r_screp = Res("screp")
            adabT = sb("adabT", [128, 24])
            npreT = sb("npreT", [128, 8])
            adab_bc = sb("adab_bc", [128, D])
            npost_bc = sb("npost_bc", [128, D])
            tmpg = sb("tmpg", [128, 512])
            r_small = Res("small")
            r_bc = Res("bc")
            r_tmpg = Res("tmpg")
            fw.dma("sp", adabT[:], self.ada_b[i].rearrange("(g p) -> p g", p=128), w=[r_small],
                   allow_slow_non_contiguous=True)
            fw.dma("sp", npreT[:], self.norm_pre[i].rearrange("(g p) -> p g", p=128), w=[r_small],
                   allow_slow_non_contiguous=True)
            fw.dma("sp", adab_bc[:], self.ada_b[i, 2 * D:3 * D].partition_broadcast(128), w=[r_bc])
            fw.dma("sp", npost_bc[:], self.norm_post[i].partition_broadcast(128), w=[r_bc])
            for wh in range(2):
                for kc in range(8):
                    fw.op("dve", lambda e, wh=wh, kc=kc: e.tensor_copy(
                        out=screp[:, wh, kc, :], in_=self.sc2[:, kc, wh:wh + 1].to_broadcast([128, 128])),
                        r=[self.r_sc2], w=[r_screp])
            psM, r_psM = self.ps[0], self.psr[0]
            for jb in range(6):
                wb, r_wb = wblk[jb % 2], r_wblk[jb % 2]
                fw.dma("sp", wb[:], self.ada_w[i, :, jb * 512:(jb + 1) * 512].rearrange("(kc p) n -> p kc n", p=128),
                       w=[r_wb])
                if jb < 4:
                    for g in range(4):
                        gi = jb * 4 + g
                        for kc in range(8):
                            fw.op("pe", lambda e, gi=gi, g=g, kc=kc, wb=wb: e.matmul(
                                psM[:, 2 * gi:2 * gi + 2], wb[:, kc, g * 128:(g + 1) * 128], self.sc2[:, kc, :],
                                start=(kc == 0), stop=(kc == 7)),
                                r=[r_wb, self.r_sc2], w=[r_psM], inc=(kc == 7))
                else:
                    for wh in range(2):
                        pg, r_pg = self.ps[1 + wh], self.psr[1 + wh]
                        for kc in range(8):
                            fw.op("pe", lambda e, wh=wh, kc=kc, wb=wb, pg=pg: e.matmul(
                                pg[:, :], screp[:, wh, kc, :], wb[:, kc, :], start=(kc == 0), stop=(kc == 7)),
                                r=[r_wb, r_screp], w=[r_pg], inc=(kc == 7))
                        cs = slice((jb - 4) * 512, (jb - 3) * 512)
                        fw.op("dve", lambda e, pg=pg, cs=cs: e.tensor_tensor(
                            out=tmpg[:], in0=pg[:, :], in1=adab_bc[:, cs], op=ALU.add),
                            r=[r_pg, r_bc], w=[r_tmpg])
                        fw.op("dve", lambda e, wh=wh, cs=cs: e.tensor_tensor(
                            out=self.Gbc[:, wh, cs], in0=tmpg[:], in1=npost_bc[:, cs], op=ALU.mult),
                            r=[r_tmpg, r_bc], w=[self.r_Gbc])
            pv = psM[:, 0:32].rearrange("p (g w) -> p g w", w=2)
            for wh in range(2):
                fw.op("dve", lambda e, wh=wh: e.tensor_tensor(
                    out=self.AB[:, wh, 1, :], in0=pv[:, 0:8, wh], in1=adabT[:, 0:8], op=ALU.add),
                    r=[r_psM, r_small], w=[self.r_AB])
                fw.op("dve", lambda e, wh=wh: e.scalar_tensor_tensor(
                    out=self.AB[:, wh, 0, :], in0=pv[:, 8:16, wh], scalar=1.0, in1=adabT[:, 8:16],
                    op0=ALU.add, op1=ALU.add),
                    r=[r_psM, r_small], w=[self.r_AB])
                fw.op("dve", lambda e, wh=wh: e.tensor_tensor(
                    out=self.AB[:, wh, 0, :], in0=self.AB[:, wh, 0, :], in1=npreT[:], op=ALU.mult),
                    r=[self.r_AB, r_small], w=[self.r_AB])
            fw.barrier()

    def phase1_prenorm(self, i, hT, r_hT):
        nc, fw = self.nc, self.fw
        with ExitStack() as es:
            sb = lambda n, s, d=F32: es.enter_context(_sbt(nc, n, s, d))
            NB = 3
            xt = [sb("p1x%d" % k, [128, D]) for k in range(NB)]
            r_xt = [Res() for k in range(NB)]
            xn = [sb("p1xn%d" % k, [128, D], BF16) for k in range(2)]
            r_xn = [Res() for k in range(2)]
            junk = sb("p1junk", [128, D], BF16)
            r_junk = Res()
            st = [sb("p1st%d" % k, [128, 4]) for k in range(2)]
            r_st = [Res() for k in range(2)]
            hst = [sb("p1hs%d" % k, [128, 8, 128], BF16) for k in range(2)]
            r_hst = [Res() for k in range(2)]
            def load_x(tt):
                fw.dma("sp", xt[tt % NB][:], self.xres[tt * 128:(tt + 1) * 128, :], r=[self.r_xres[tt]], w=[r_xt[tt % NB]])
            load_x(0)
            load_x(1)
            for tt in range(NT):
                wh = 1 if tt < 2 else 0
                x_, rx = xt[tt % NB], r_xt[tt % NB]
                xn_, rxn = xn[tt % 2], r_xn[tt % 2]
                s_, rs = st[tt % 2], r_st[tt % 2]
                pT, rpT = self.ps[tt % 2], self.psr[tt % 2]
                pTb = pT[:, :].bitcast(BF16)
                fw.op("act", lambda e, x_=x_, s_=s_: e.activation(
                    out=junk[:], in_=x_[:], func=AF.Square, accum_out=s_[:, 0:1]), r=[rx], w=[r_junk, rs])
                fw.op("act", lambda e, s_=s_: e.activation(
                    out=s_[:, 1:2], in_=s_[:, 0:1], func=AF.Sqrt, scale=1.0 / D, bias=self.epsc[:, 0:1]),
                    r=[rs, self.r_const], w=[rs])
                fw.op("dve", lambda e, s_=s_: e.reciprocal(out=s_[:, 2:3], in_=s_[:, 1:2]), r=[rs], w=[rs])
                fw.op("dve", lambda e, x_=x_, xn_=xn_, s_=s_: e.tensor_scalar(
                    out=xn_[:], in0=x_[:], scalar1=s_[:, 2:3], scalar2=None, op0=ALU.mult), r=[rx, rs], w=[rxn])
                for kc in range(8):
                    fw.op("pe", lambda e, kc=kc, xn_=xn_, pTb=pTb: e.transpose(
                        out=pTb[:, kc * 128:(kc + 1) * 128], in_=xn_[:, kc * 128:(kc + 1) * 128], identity=self.ident[:]),
                        r=[rxn, self.r_const], w=[rpT], inc=(kc == 7))
                rh = r_hT[tt]
                hs, rhs_ = hst[tt % 2], r_hst[tt % 2]
                for kc in range(8):
                    dst = hs[:, kc, :]
                    src = pTb[:, kc * 128:(kc + 1) * 128]
                    if kc % 2 == 0:
                        fw.op("dve", lambda e, dst=dst, src=src, kc=kc, wh=wh: e.tensor_scalar(
                            out=dst, in0=src, scalar1=self.AB[:, wh, 0, kc:kc + 1], scalar2=self.AB[:, wh, 1, kc:kc + 1],
                            op0=ALU.mult, op1=ALU.add), r=[rpT, self.r_AB], w=[rhs_])
                    else:
                        fw.op("act", lambda e, dst=dst, src=src, kc=kc, wh=wh: e.activation(
                            out=dst, in_=src, func=AF.Identity, scale=self.AB[:, wh, 0, kc:kc + 1],
                            bias=self.AB[:, wh, 1, kc:kc + 1]), r=[rpT, self.r_AB], w=[rhs_])
                fw.dma("pool", hT[:, :, tt * 128:(tt + 1) * 128], hs[:], r=[rhs_], w=[rh])
                if tt + 2 < NT:
                    load_x(tt + 2)
            fw.barrier()

    def phase3_out(self, i, w_out, last):
        nc, fw = self.nc, self.fw
        with ExitStack() as es:
            sb = lambda n, s, d=F32: es.enter_context(_sbt(nc, n, s, d))
            wo = sb("p3wo", [128, NH, D], BF16)
            r_wo = Res()
            wst = [sb("p3wst%d" % k, [128, 2, D]) for k in range(2)]
            r_wst = [Res() for k in range(2)]
            for g in range(8):
                s_, rs = wst[g % 2], r_wst[g % 2]
                fw.dma("sp", s_[:], w_out[g * 256:(g + 1) * 256, :].rearrange("(h p) n -> p h n", p=128), w=[rs])
                fw.op("pool", lambda e, s_=s_, g=g: e.tensor_copy(out=wo[:, 2 * g:2 * g + 2, :], in_=s_[:]),
                      r=[rs], w=[r_wo])
            NB = 3
            oz = [sb("p3oz%d" % k, [128, NH, 128], BF16) for k in range(NB)]
            r_oz = [Res() for k in range(NB)]
            xt = [sb("p3x%d" % k, [128, D]) for k in range(NB)]
            r_xt = [Res() for k in range(NB)]
            tmp = [sb("p3t%d" % k, [128, D]) for k in range(NB)]
            r_tmp = [Res() for k in range(NB)]
            junk = sb("p3junk", [128, 512], BF16)
            r_junk = Res()
            st = [sb("p3st%d" % k, [128, 8]) for k in range(NB)]
            r_st = [Res() for k in range(NB)]
            tiles = list(range(2, NT)) if last else list(range(NT))
            NB = 3

            def load_t(n):
                tt = tiles[n]
                k = n % NB
                pc = (tt * 128) // 512
                fw.dma("sp", oz[k][:], self.ozT[:, :, tt * 128:(tt + 1) * 128].rearrange("h v t -> v h t"),
                       r=[self.r_ozT[h][pc] for h in range(NH)], w=[r_oz[k]])
                fw.dma("sp", xt[k][:], self.xres[tt * 128:(tt + 1) * 128, :], r=[self.r_xres[tt]], w=[r_xt[k]])
            load_t(0)
            load_t(1)
            for n, tt in enumerate(tiles):
                wh = 1 if tt < 2 else 0
                k = n % NB
                pb = [(self.ps[2 * (n % 2)], self.psr[2 * (n % 2)]), (self.ps[2 * (n % 2) + 1], self.psr[2 * (n % 2) + 1])]
                for nb in range(2):
                    py, rpy = pb[nb]
                    for h in range(NH):
                        fw.op("pe", lambda e, py=py, h=h, nb=nb, k=k: e.matmul(
                            py[:, :], oz[k][:, h, :], wo[:, h, nb * 512:(nb + 1) * 512], start=(h == 0), stop=(h == NH - 1)),
                            r=[r_oz[k], r_wo], w=[rpy], inc=(h == NH - 1))
                    fw.op("act", lambda e, py=py, nb=nb, k=k: e.activation(
                        out=junk[:], in_=py[:, :], func=AF.Square, accum_out=st[k][:, nb:nb + 1]),
                        r=[rpy], w=[r_junk, r_st[k]])
                fw.op("dve", lambda e, k=k: e.tensor_tensor(
                    out=st[k][:, 2:3], in0=st[k][:, 0:1], in1=st[k][:, 1:2], op=ALU.add), r=[r_st[k]], w=[r_st[k]])
                fw.op("act", lambda e, k=k: e.activation(
                    out=st[k][:, 3:4], in_=st[k][:, 2:3], func=AF.Sqrt, scale=1.0 / D, bias=self.epsc[:, 0:1]),
                    r=[r_st[k], self.r_const], w=[r_st[k]])
                fw.op("dve", lambda e, k=k: e.reciprocal(out=st[k][:, 4:5], in_=st[k][:, 3:4]), r=[r_st[k]], w=[r_st[k]])
                for nb in range(2):
                    py, rpy = pb[nb]
                    cs = slice(nb * 512, (nb + 1) * 512)
                    fw.op("dve", lambda e, py=py, cs=cs, k=k, wh=wh: e.scalar_tensor_tensor(
                        out=tmp[k][:, cs], in0=py[:, :], scalar=st[k][:, 4:5], in1=self.Gbc[:, wh, cs],
                        op0=ALU.mult, op1=ALU.mult), r=[rpy, r_st[k], self.r_Gbc], w=[r_tmp[k]])
                fw.op("pool", lambda e, k=k: e.tensor_tensor(
                    out=tmp[k][:], in0=tmp[k][:], in1=xt[k][:], op=ALU.add), r=[r_tmp[k], r_xt[k]], w=[r_tmp[k]])
                if last:
                    fw.dma("pool", self.out[(tt - 2) * 128:(tt - 1) * 128, :], tmp[k][:], r=[r_tmp[k]])
                else:
                    fw.dma("pool", self.xres[tt * 128:(tt + 1) * 128, :], tmp[k][:], r=[r_tmp[k]], w=[self.r_xres[tt]])
                if n + 2 < len(tiles):
                    load_t(n + 2)

    def hgrn_mixer(self, j, hT, r_hT):
        from_hgrn(self, j, hT, r_hT)

    def mla_mixer(self, j, hT, r_hT, need_ctx):
        from_mla(self, j, hT, r_hT, need_ctx)


HG_COL = {"q": 0, "v": DI, "ff": 2 * DI, "fb": 3 * DI, "z": 4 * DI}
DCLAMP = 40.0


def from_hgrn(P, j, hT, r_hT):
    nc, fw = P.nc, P.fw
    NP = len(PIECES)
    REFI = CH // 2 - 1
    with ExitStack() as es:
        sb = lambda n, s, d=F32: es.enter_context(_sbt(nc, n, s, d))
        cm3 = sb("hg_cm3", [128, CPB, 128], BF16)
        gain = sb("hg_gain", [128, 1])
        lbT = sb("hg_lb", [128, 3, 2, NH])
        ones32 = sb("hg_ones", [128, 512])
        r_lc = Res("hg_lc")
        with ExitStack() as es2:
            st = es2.enter_context(_sbt(nc, "hg_cst", [128, CPB * 128], F32))
            lg = es2.enter_context(_sbt(nc, "hg_lg", [128, 2, 2, NH], F32))
            r_st = Res()
            fw.dma("sp", st[:], P.k_cm3[:, :], w=[r_st])
            fw.op("dve", lambda e: e.tensor_copy(out=cm3[:].rearrange("p c v -> p (c v)"), in_=st[:]), r=[r_st], w=[r_lc])
            fw.op("dve", lambda e: e.memset(ones32[:], 1.0), w=[r_lc])
            fw.dma("sp", gain[:], P.hg_o_norm[j].rearrange("(p o) -> p o", o=1), w=[r_lc])
            if j > 0:
                for jj in range(2):
                    for d in range(2):
                        fw.dma("sp", lg[:, jj, d, :], P.hg_lb[jj, d].rearrange("(h p) -> p h", p=128), w=[r_st],
                               allow_slow_non_contiguous=True)
                fw.op("dve", lambda e: e.tensor_tensor(out=lg[:, 0], in0=lg[:, 1], in1=lg[:, 0], op=ALU.subtract), r=[r_st], w=[r_st])
                fw.op("act", lambda e: e.activation(out=lbT[:, 0], in_=lg[:, 0], func=AF.Sigmoid), r=[r_st], w=[r_lc])
                fw.op("dve", lambda e: e.tensor_scalar(out=lbT[:, 1], in0=lbT[:, 0], scalar1=-1.0, scalar2=1.0,
                                                       op0=ALU.mult, op1=ALU.add), r=[r_lc], w=[r_lc])
                fw.op("dve", lambda e: e.tensor_scalar(out=lbT[:, 2], in0=lbT[:, 0], scalar1=-1.0, scalar2=None,
                                                       op0=ALU.add), r=[r_lc], w=[r_lc])
            fw.barrier()
        Q = [sb("hg_Q%d" % d, [128, T], BF16) for d in range(2)]
        Kt = [sb("hg_K%d" % d, [128, T], BF16) for d in range(2)]
        V = sb("hg_V", [128, NT, 128], BF16)
        zs = sb("hg_zs", [128, T], BF16)
        oacc = sb("hg_oacc", [128, T], BF16)
        Gref = sb("hg_Gref", [128, 2, NCH])
        Wt = sb("hg_W", [128, 2, NCH])
        Gl = sb("hg_Gl", [128, 2])
        Wj = sb("hg_Wj", [128, 2])
        r_Q = [[Res() for p in range(NP)] for d in range(2)]
        r_K = [[Res() for p in range(NP)] for d in range(2)]
        r_V = [Res() for p in range(NP)]
        r_zs = [Res() for p in range(NP)]
        r_oacc = [Res() for p in range(NT)]
        r_Gref, r_W, r_Gl, r_Wj = Res(), Res(), Res(), Res()
        wbf = [sb("hg_wbf%d" % k, [128, 5, 8, 128], BF16) for k in range(2)]
        r_wbf = [Res() for k in range(2)]
        wst = [sb("hg_wst%d" % k, [128, 8, 128]) for k in range(2)]
        r_wst = [Res() for k in range(2)]
        hp = [sb("hg_hp%d" % k, [128, 8, 512], BF16) for k in range(2)]
        r_hp = [Res() for k in range(2)]
        eq = sb("hg_eq", [128, 512])
        ez = sb("hg_ez", [128, 512])
        Ld = [sb("hg_L%d" % d, [128, 512]) for d in range(2)]
        Fd = [sb("hg_F%d" % d, [128, 512]) for d in range(2)]
        E1 = [sb("hg_E1%d" % d, [128, 512], BF16) for d in range(2)]
        E1b = [sb("hg_E1b%d" % d, [128, 512], BF16) for d in range(2)]
        k1 = [sb("hg_k1%d" % d, [128, 512], BF16) for d in range(2)]
        wtmp = [sb("hg_wtmp%d" % d, [128, 512 // CH + 1]) for d in range(2)]
        r_eq, r_ez = Res(), Res()
        r_L, r_F, r_E1, r_E2, r_k1, r_wtmp = [[Res(), Res()] for _ in range(6)]
        AT = [[sb("hg_AT%d%d" % (d, k), [128, 128], BF16) for k in range(3)] for d in range(2)]
        KT = [[sb("hg_KT%d%d" % (d, k), [128, 128], BF16) for k in range(2)] for d in range(2)]
        Vx = [[sb("hg_Vx%d%d" % (d, k), [128, CPB, 128], BF16) for k in range(3)] for d in range(2)]
        Tb = [[sb("hg_Tb%d%d" % (d, k), [128, CPB, 128]) for k in range(2)] for d in range(2)]
        Sp = [[sb("hg_Sp%d%d" % (d, k), [128, CPB, 128], BF16) for k in range(2)] for d in range(2)]
        Qw = [sb("hg_Qw%d" % d, [128, T], BF16) for d in range(2)]
        WtQ = sb("hg_WtQ", [128, 2, NCH + 1])
        r_AT = [[Res() for k in range(3)] for d in range(2)]
        r_KT = [[Res() for k in range(2)] for d in range(2)]
        r_Vx = [[Res() for k in range(3)] for d in range(2)]
        r_Tb = [[Res() for k in range(2)] for d in range(2)]
        r_Sp = [[Res() for k in range(2)] for d in range(2)]
        r_Qw = [[Res() for p in range(NP)] for d in range(2)]
        r_WtQ = Res()
        NRO = 6
        osum = [sb("hg_osum%d" % k, [128, 128]) for k in range(NRO)]
        sq = [sb("hg_sq%d" % k, [128, 128], BF16) for k in range(NRO)]
        rt = [sb("hg_rt%d" % k, [128, 128]) for k in range(NRO)]
        ozp = [sb("hg_ozp%d" % k, [128, 128], BF16) for k in range(NRO)]
        r_osum = [Res() for k in range(NRO)]
        r_sq = [Res() for k in range(NRO)]
        r_rt = [Res() for k in range(NRO)]
        r_ozp = [Res() for k in range(NRO)]
        ps, psr = P.ps, P.psr
        SC_Bs, KT_B, O_B, SS_B = [5, 0], 4, 3, 4
        x_banks = [[6, 7], [1, 2]]

        def load_weights(hd):
            k = hd % 2
            for m, nm in enumerate(("q", "v", "ff", "fb", "z")):
                c0 = HG_COL[nm] + hd * 128
                s_, rs = wst[m % 2], r_wst[m % 2]
                fw.dma("sp", s_[:], P.hg_w_in[j, :, c0:c0 + 128].rearrange("(kc p) n -> p kc n", p=128), w=[rs])
                fw.op("pool", lambda e, s_=s_, m=m, k=k: e.tensor_copy(out=wbf[k][:, m], in_=s_[:]), r=[rs], w=[r_wbf[k]])

        ld_count = [0]

        def load_piece(p):
            s0, n = PIECES[p]
            k = ld_count[0] % 2
            ld_count[0] += 1
            fw.dma("sp", hp[k][:, :, 0:n], hT[:, :, s0:s0 + n], r=[r_hT[t] for t in range(s0 // 128, (s0 + n) // 128)], w=[r_hp[k]])
            return k

        def prep(hd, p, hk):
            k = hd % 2
            s0, n = PIECES[p]
            ntile = n // 128
            nch = n // CH
            c0 = s0 // CH
            wb, rwb = wbf[k], r_wbf[k]
            hpk, rhp = hp[hk], r_hp[hk]
            tsl = slice(s0, s0 + n)
            one = ones32[:, 0:1]
            for m, bank in ((0, 0), (2, 1), (3, 2), (4, 3)):
                for kc in range(8):
                    fw.op("pe", lambda e, m=m, bank=bank, kc=kc: e.matmul(
                        ps[bank][:, 0:n], wb[:, m, kc, :], hpk[:, kc, 0:n], start=(kc == 0), stop=(kc == 7)),
                        r=[rwb, rhp], w=[psr[bank]], inc=(kc == 7))
            for ti in range(ntile):
                for kc in range(8):
                    fw.op("pe", lambda e, ti=ti, kc=kc: e.matmul(
                        ps[4][:, ti * 128:(ti + 1) * 128], hpk[:, kc, ti * 128:(ti + 1) * 128], wb[:, 1, kc, :],
                        start=(kc == 0), stop=(kc == 7)),
                        r=[rwb, rhp], w=[psr[4]], inc=(kc == 7))
            fw.op("act", lambda e: e.activation(out=eq[:, 0:n], in_=ps[0][:, 0:n], func=AF.Exp, scale=-1.0), r=[psr[0]], w=[r_eq])
            for d in range(2):
                fw.op("act", lambda e, d=d: e.activation(out=Ld[d][:, 0:n], in_=ps[1 + d][:, 0:n], func=AF.Exp, scale=-1.0),
                      r=[psr[1 + d]], w=[r_L[d]])
            fw.op("act", lambda e: e.activation(out=ez[:, 0:n], in_=ps[3][:, 0:n], func=AF.Exp, scale=-1.0), r=[psr[3]], w=[r_ez])
            fw.op("dve", lambda e: e.tensor_copy(out=V[:, s0 // 128:s0 // 128 + ntile, :].rearrange("p a v -> p (a v)"),
                                                 in_=ps[4][:, 0:n]), r=[psr[4]], w=[r_V[p]])
            fw.op("act", lambda e: e.activation(out=eq[:, 0:n], in_=eq[:, 0:n], func=AF.Ln, bias=one), r=[r_eq, r_lc], w=[r_eq])
            for d in range(2):
                fw.op("act", lambda e, d=d: e.activation(out=Ld[d][:, 0:n], in_=Ld[d][:, 0:n], func=AF.Ln, bias=one),
                      r=[r_L[d], r_lc], w=[r_L[d]])
            fw.op("act", lambda e: e.activation(out=ez[:, 0:n], in_=ez[:, 0:n], func=AF.Ln, bias=one), r=[r_ez, r_lc], w=[r_ez])
            fw.op("act", lambda e: e.activation(out=eq[:, 0:n], in_=eq[:, 0:n], func=AF.Exp, scale=-1.0), r=[r_eq], w=[r_eq])
            for d in range(2):
                fw.op("act", lambda e, d=d: e.activation(out=Fd[d][:, 0:n], in_=Ld[d][:, 0:n], func=AF.Exp, scale=-1.0),
                      r=[r_L[d]], w=[r_F[d]])
            fw.op("act", lambda e: e.activation(out=ez[:, 0:n], in_=ez[:, 0:n], func=AF.Exp, scale=-1.0), r=[r_ez], w=[r_ez])
            fw.op("dve", lambda e: e.tensor_tensor(out=eq[:, 0:n], in0=ps[0][:, 0:n], in1=eq[:, 0:n], op=ALU.mult),
                  r=[psr[0], r_eq], w=[r_eq])
            fw.op("dve", lambda e: e.tensor_tensor(out=zs[:, tsl], in0=ps[3][:, 0:n], in1=ez[:, 0:n], op=ALU.mult),
                  r=[psr[3], r_ez], w=[r_zs[p]])
            sops = [None, None]
            for d in range(2):
                L_, F_ = Ld[d], Fd[d]
                if j == 0:
                    fw.op("dve", lambda e, d=d, F_=F_: e.tensor_scalar(out=k1[d][:, 0:n], in0=F_[:, 0:n], scalar1=-1.0, scalar2=1.0,
                                                                      op0=ALU.mult, op1=ALU.add), r=[r_F[d]], w=[r_k1[d]])
                    sops[d] = ALU.subtract
                else:
                    fw.op("dve", lambda e, d=d, F_=F_: e.tensor_scalar(out=k1[d][:, 0:n], in0=F_[:, 0:n], scalar1=-1.0,
                                                                      scalar2=lbT[:, 2, d, hd:hd + 1],
                                                                      op0=ALU.add, op1=ALU.mult), r=[r_F[d], r_lc], w=[r_k1[d]])
                    fw.op("dve", lambda e, d=d, F_=F_: e.tensor_scalar(out=F_[:, 0:n], in0=F_[:, 0:n],
                                                                      scalar1=lbT[:, 1, d, hd:hd + 1], scalar2=lbT[:, 0, d, hd:hd + 1],
                                                                      op0=ALU.mult, op1=ALU.add), r=[r_F[d], r_lc], w=[r_F[d]])
                    fw.op("act", lambda e, d=d, F_=F_, L_=L_: e.activation(out=L_[:, 0:n], in_=F_[:, 0:n], func=AF.Ln),
                          r=[r_F[d]], w=[r_L[d]])
                    sops[d] = ALU.add
            for d in range(2):
                L_, F_ = Ld[d], Fd[d]
                init = 0.0 if p == 0 else Gl[:, d:d + 1]
                fw.op("dve", lambda e, init=init, sop=sops[d], L_=L_, F_=F_: e.tensor_tensor_scan(
                    out=F_[:, 0:n], data0=ones32[:, 0:n], data1=L_[:, 0:n], initial=init, op0=ALU.mult, op1=sop),
                    r=[r_L[d], r_lc, r_Gl, r_k1[d]], w=[r_F[d]])
            for d in range(2):
                L_, F_ = Ld[d], Fd[d]
                gv = F_[:, 0:n].rearrange("p (c i) -> p c i", i=CH)
                fw.op("pool", lambda e, d=d, F_=F_: e.tensor_copy(out=Gl[:, d:d + 1], in_=F_[:, n - 1:n]), r=[r_F[d]], w=[r_Gl])
                fw.op("pool", lambda e, d=d, gv=gv: e.tensor_copy(out=Gref[:, d, c0:c0 + nch], in_=gv[:, :, REFI]), r=[r_F[d]], w=[r_Gref])
                if d == 1:
                    fw.op("dve", lambda e, sop=sops[d], L_=L_, F_=F_: e.tensor_tensor(
                        out=F_[:, 0:n], in0=F_[:, 0:n], in1=L_[:, 0:n], op=(ALU.add if sop == ALU.subtract else ALU.subtract)),
                        r=[r_F[d], r_L[d]], w=[r_F[d]])
                fw.op("pool", lambda e, d=d, gv=gv: e.tensor_tensor(
                    out=gv, in0=gv, in1=Gref[:, d, c0:c0 + nch].unsqueeze(2).to_broadcast([128, nch, CH]), op=ALU.subtract),
                    r=[r_F[d], r_Gref], w=[r_F[d]])
            for d in range(2):
                F_ = Fd[d]
                sq_ = 1.0 if d == 0 else -1.0
                fw.op("act", lambda e, d=d, sq_=sq_, F_=F_: e.activation(out=E1[d][:, 0:n], in_=F_[:, 0:n], func=AF.Exp, scale=sq_),
                      r=[r_F[d]], w=[r_E1[d]])
                fw.op("act", lambda e, d=d, sq_=sq_, F_=F_: e.activation(out=E1b[d][:, 0:n], in_=F_[:, 0:n], func=AF.Exp, scale=-sq_),
                      r=[r_F[d]], w=[r_E2[d]])
            for d in range(2):
                fw.op("dve", lambda e, d=d: e.tensor_tensor(out=Q[d][:, tsl], in0=eq[:, 0:n], in1=E1[d][:, 0:n], op=ALU.mult),
                      r=[r_eq, r_E1[d]], w=[r_Q[d][p]])
                fw.op("pool", lambda e, d=d: e.tensor_tensor(out=Kt[d][:, tsl], in0=k1[d][:, 0:n], in1=E1b[d][:, 0:n], op=ALU.mult),
                      r=[r_k1[d], r_E2[d]], w=[r_K[d][p]])
                lo = max(c0 - 1, 0)
                hi = c0 + nch - 1
                fw.op("pool", lambda e, d=d, lo=lo, hi=hi: e.tensor_tensor(
                    out=wtmp[d][:, 0:hi - lo], in0=Gref[:, d, lo + 1:hi + 1], in1=Gref[:, d, lo:hi], op=ALU.subtract),
                    r=[r_Gref], w=[r_wtmp[d]])
                fw.op("act", lambda e, d=d, lo=lo, hi=hi: e.activation(out=Wt[:, d, lo:hi], in_=wtmp[d][:, 0:hi - lo], func=AF.Exp),
                      r=[r_wtmp[d]], w=[r_W])

        def junction():
            fw.op("dve", lambda e: e.tensor_tensor(out=Wj[:, 1:2], in0=Gref[:, 1, 0:1], in1=Gref[:, 1, NCH - 1:NCH], op=ALU.subtract),
                  r=[r_Gref], w=[r_Wj])
            fw.op("act", lambda e: e.activation(out=Wj[:, 0:1], in_=Wj[:, 1:2], func=AF.Exp, bias=Gl[:, 1:2]),
                  r=[r_Wj, r_Gl], w=[r_Wj])
            fw.op("dve", lambda e: e.memset(WtQ[:, 0, 0:1], 1.0), w=[r_WtQ])
            fw.op("dve", lambda e: e.tensor_copy(out=WtQ[:, 0, 1:NCH], in_=Wt[:, 0, 0:NCH - 1]), r=[r_W], w=[r_WtQ])
            fw.op("dve", lambda e: e.tensor_copy(out=WtQ[:, 1, 0:NCH - 1], in_=Wt[:, 1, 0:NCH - 1]), r=[r_W], w=[r_WtQ])
            fw.op("dve", lambda e: e.tensor_copy(out=WtQ[:, 1, NCH - 1:NCH], in_=Wj[:, 0:1]), r=[r_Wj], w=[r_WtQ])
            for p in range(NP):
                s0, n = PIECES[p]
                nch = n // CH
                c0 = s0 // CH
                for d in range(2):
                    fw.op("pool", lambda e, d=d, s0=s0, n=n, nch=nch, c0=c0: e.tensor_tensor(
                        out=Qw[d][:, s0:s0 + n].rearrange("p (c i) -> p c i", i=CH),
                        in0=Q[d][:, s0:s0 + n].rearrange("p (c i) -> p c i", i=CH),
                        in1=WtQ[:, d, c0:c0 + nch].unsqueeze(2).to_broadcast([128, nch, CH]), op=ALU.mult),
                        r=[r_Q[d][p], r_WtQ], w=[r_Qw[d][p]])

        ncc = LC // CH
        orders = [list(range(NCH)), list(range(ncc - 1, -1, -1)) + list(range(NCH - 1, ncc - 1, -1))]

        def blocks_of(order):
            out = []
            for gc in order:
                if not out or out[-1] != gc // CPB:
                    out.append(gc // CPB)
            return out

        def build_vx(d, step, blk):
            sl = step % 3
            p = blk // 4
            fw.op("pool", lambda e: e.tensor_tensor(out=Vx[d][sl][:], in0=V[:, blk:blk + 1, :].to_broadcast([128, CPB, 128]),
                                                    in1=cm3[:], op=ALU.mult),
                  r=[r_V[p], r_lc], w=[r_Vx[d][sl]])

        def stageA_pre(hd, d, step, blk):
            sl = step % 2
            p = blk // 4
            tb = slice(blk * 128, (blk + 1) * 128)
            SC_B = SC_Bs[d]
            scv = ps[SC_B][:, 0:128]
            ktv = ps[KT_B][:, :].bitcast(BF16)[:, 128 * d:128 * d + 128]
            fw.op("pe", lambda e: e.matmul(scv, Kt[d][:, tb], Q[d][:, tb], start=True, stop=True),
                  r=[r_K[d][p], r_Q[d][p]], w=[psr[SC_B]])
            mk = P.maskf if d == 0 else P.maskb
            fw.op("dve", lambda e: e.tensor_tensor(out=AT[d][step % 3][:], in0=scv, in1=mk[:], op=ALU.mult),
                  r=[psr[SC_B], P.r_const], w=[r_AT[d][step % 3]])
            fw.op("pe", lambda e: e.transpose(out=ktv, in_=Kt[d][:, tb], identity=P.ident[:]),
                  r=[r_K[d][p], P.r_const], w=[psr[KT_B]])
            fw.op("act", lambda e: e.copy(out=KT[d][sl][:], in_=ktv), r=[psr[KT_B]], w=[r_KT[d][sl]])

        def stageX(hd, d, step, blk):
            sl = step % 2
            vs = step % 3
            for h in range(CPB // 4):
                xb = x_banks[d][h]
                fw.op("pe", lambda e, xb=xb, h=h: e.matmul(ps[xb][:, :], KT[d][sl][:],
                                                          Vx[d][vs][:, 4 * h:4 * h + 4, :].rearrange("p c v -> p (c v)"),
                                                          start=True, stop=True),
                      r=[r_KT[d][sl], r_Vx[d][vs]], w=[psr[xb]])

        def chain_step(hd, d, pos, gc, state):
            c = gc % CPB
            xb = x_banks[d][c // 4]
            xv = ps[xb][:, (c % 4) * 128:(c % 4 + 1) * 128]
            bb, sl = (pos // CPB) % 2, pos % CPB
            out_ = Tb[d][bb][:, sl, :]
            if pos == 0:
                fw.op("dve", lambda e: e.tensor_copy(out=out_, in_=xv), r=[psr[xb]], w=[r_Tb[d][bb]])
                return
            if d == 0:
                wcol, rw = Wt[:, 0, gc - 1:gc], r_W
            elif gc == NCH - 1:
                wcol, rw = Wj[:, 0:1], r_Wj
            else:
                wcol, rw = Wt[:, 1, gc:gc + 1], r_W
            pb, psl = ((pos - 1) // CPB) % 2, (pos - 1) % CPB
            prev = Tb[d][pb][:, psl, :]
            fw.op("dve", lambda e: e.scalar_tensor_tensor(out=out_, in0=prev, scalar=wcol, in1=xv, op0=ALU.mult, op1=ALU.add),
                  r=[r_Tb[d][pb], rw, psr[xb]], w=[r_Tb[d][bb]])

        def cast_block(d, bidx):
            bb = bidx % 2
            fw.op("act", lambda e: e.copy(out=Sp[d][bb][:].rearrange("p c v -> p (c v)"),
                                          in_=Tb[d][bb][:].rearrange("p c v -> p (c v)")),
                  r=[r_Tb[d][bb]], w=[r_Sp[d][bb]])

        def stageB(hd, d, step, blk):
            sl = step % 2
            p = blk // 4
            ov = ps[O_B][:, 128 * d:128 * d + 128]
            mm = [(ov, V[:, blk, :], AT[d][step % 3][:], [r_V[p], r_AT[d][step % 3]])]
            cord = corder[d]
            for ci in range(CPB):
                pos = step * CPB + ci
                if pos == 0:
                    continue
                c = cord[ci]
                pb, psl = ((pos - 1) // CPB) % 2, (pos - 1) % CPB
                tq = slice(blk * 128 + c * CH, blk * 128 + (c + 1) * CH)
                mm.append((ov[:, c * CH:(c + 1) * CH], Sp[d][pb][:, psl, :], Qw[d][:, tq], [r_Sp[d][pb], r_Qw[d][p]]))
            for n_, (o_, l_, r_, rr) in enumerate(mm):
                fw.op("pe", lambda e, o_=o_, l_=l_, r_=r_, n_=n_: e.matmul(o_, l_, r_, start=(n_ == 0), stop=(n_ == len(mm) - 1)),
                      r=rr, w=[psr[O_B]], inc=(n_ == len(mm) - 1))
            return ov

        def readout1(hd, blk, k):
            fw.op("act", lambda e: e.activation(out=sq[k][:], in_=osum[k][:], func=AF.Square), r=[r_osum[k]], w=[r_sq[k]])
            fw.op("pe", lambda e: e.matmul(ps[SS_B][:, 256:384], P.ones_bf[:], sq[k][:], start=True, stop=True),
                  r=[r_sq[k], P.r_const], w=[psr[SS_B]])
            fw.op("act", lambda e: e.activation(out=rt[k][:], in_=ps[SS_B][:, 256:384], func=AF.Ln, scale=1.0 / 128, bias=P.epsc[:, 0:1]),
                  r=[psr[SS_B], P.r_const], w=[r_rt[k]])
            fw.op("act", lambda e: e.activation(out=rt[k][:], in_=rt[k][:], func=AF.Exp, scale=-0.5), r=[r_rt[k]], w=[r_rt[k]])

        def readout2(hd, blk, k):
            p = blk // 4
            tb = slice(blk * 128, (blk + 1) * 128)
            fw.op("dve", lambda e: e.scalar_tensor_tensor(out=osum[k][:], in0=osum[k][:], scalar=gain[:, 0:1], in1=rt[k][:],
                                                          op0=ALU.mult, op1=ALU.mult), r=[r_osum[k], r_rt[k], r_lc], w=[r_osum[k]])
            fw.op("pool", lambda e: e.tensor_tensor(out=ozp[k][:], in0=osum[k][:], in1=zs[:, tb], op=ALU.mult),
                  r=[r_osum[k], r_zs[p]], w=[r_ozp[k]])
            fw.dma("sp", P.ozT[hd, :, tb], ozp[k][:], r=[r_ozp[k]], w=[P.r_ozT[hd][p]])

        blks = [blocks_of(o) for o in orders]
        nsteps = len(blks[0])
        bpos = [{b: n for n, b in enumerate(bl)} for bl in blks]
        corder = [list(range(CPB)), list(range(CPB - 1, -1, -1))]
        _NHD = int(os.environ.get("HG_NH", NH))
        load_weights(0)
        ridx = 0
        pre_hk = [None]
        for hd in range(_NHD):
            if hd + 1 < NH:
                load_weights(hd + 1)
            hk = pre_hk[0] if pre_hk[0] is not None else load_piece(0)
            pre_hk[0] = None
            for p in range(NP):
                hk_next = load_piece(p + 1) if p + 1 < NP else None
                prep(hd, p, hk)
                hk = hk_next
            if hd + 1 < _NHD:
                pre_hk[0] = load_piece(0)
            junction()
            for d in range(2):
                build_vx(d, 0, blks[d][0])
                build_vx(d, 1, blks[d][1])
                stageA_pre(hd, d, 0, blks[d][0])
            L1, L2 = [], []
            for step in range(nsteps + 1):
                if step < nsteps:
                    for d in range(2):
                        stageX(hd, d, step, blks[d][step])
                    for ci in range(CPB):
                        for d in range(2):
                            blk = blks[d][step]
                            chain_step(hd, d, step * CPB + ci, blk * CPB + corder[d][ci], None)
                    if step + 2 < nsteps:
                        for d in range(2):
                            build_vx(d, step + 2, blks[d][step + 2])
                    if step + 1 < nsteps:
                        for d in range(2):
                            stageA_pre(hd, d, step + 1, blks[d][step + 1])
                run2, L2 = L2, []
                for fn in run2:
                    fn()
                run1, L1 = L1, []
                for (hd_, blk_, k_) in run1:
                    readout1(hd_, blk_, k_)
                    L2.append(lambda hd_=hd_, blk_=blk_, k_=k_: readout2(hd_, blk_, k_))
                if step >= 1:
                    pstep = step - 1
                    for d in range(2):
                        blk = blks[d][pstep]
                        ov = stageB(hd, d, pstep, blk)
                        tb = slice(blk * 128, (blk + 1) * 128)
                        if bpos[d][blk] < bpos[1 - d][blk]:
                            fw.op("act", lambda e, ov=ov, tb=tb: e.copy(out=oacc[:, tb], in_=ov), r=[psr[O_B]], w=[r_oacc[blk]])
                        else:
                            k = ridx % NRO
                            ridx += 1
                            fw.op("dve", lambda e, ov=ov, tb=tb, k=k: e.tensor_tensor(
                                out=osum[k][:], in0=ov, in1=oacc[:, tb], op=ALU.add),
                                r=[psr[O_B], r_oacc[blk]], w=[r_osum[k]])
                            L1.append((hd, blk, k))
                if step < nsteps:
                    for d in range(2):
                        cast_block(d, step)
            for (hd_, blk_, k_) in L1:
                readout1(hd_, blk_, k_)
                L2.append(lambda hd_=hd_, blk_=blk_, k_=k_: readout2(hd_, blk_, k_))
            for fn in L2:
                fn()
        fw.barrier()


def from_mla(P, j, hT, r_hT, need_ctx):
    nc, fw = P.nc, P.fw
    NP = len(PIECES)
    ps, psr = P.ps, P.psr
    W = P.mla_w_in[j]
    QR, KVR, RD = 256, 128, 64
    with ExitStack() as es:
        sb = lambda n, s, d=F32: es.enter_context(_sbt(nc, n, s, d))
        qn = sb("ml_qn", [128, 2, T], BF16)
        kvn = sb("ml_kvn", [128, T], BF16)
        kpeT = sb("ml_kpe", [128, T], BF16)
        r_qn = [Res() for p in range(NP)]
        r_kvn = [Res() for p in range(NP)]
        r_kpe = [Res() for p in range(NP)]
        fw.op("pool", lambda e: e.memset(kpeT[64:128, :], 0.0), w=r_kpe)
        wqb = sb("ml_wqb", [128, 2, NH * 192], BF16)
        wqs = sb("ml_wqs", [128, 2, NH, 64], BF16)
        wkvb = sb("ml_wkvb", [128, NH * 256], BF16)
        qanT = sb("ml_qan", [128, 2])
        kvanT = sb("ml_kvan", [128, 1])
        r_w2 = Res("ml_w2")
        stg = [sb("ml_stg%d" % k, [128, 2048]) for k in range(2)]
        r_stg = [Res(), Res()]
        stg_n = [0]

        def load_cast(dst, src, shape):
            k = stg_n[0] % 2
            stg_n[0] += 1
            n = int(np.prod(shape[1:]))
            sv = stg[k][:, 0:n]
            if len(shape) == 3:
                sv = sv.rearrange("p (a b) -> p a b", b=shape[2])
            fw.dma("sp", sv, src, w=[r_stg[k]])
            return sv, k

        with ExitStack() as es1:
            sb1 = lambda n, s, d=F32: es1.enter_context(_sbt(nc, n, s, d))
            wqa = sb1("ml_wqa", [128, 8, QR], BF16)
            wkva = sb1("ml_wkva", [128, 8, KVR], BF16)
            wkpe = sb1("ml_wkpe", [128, 8, RD], BF16)
            wkpes = sb1("ml_wkpes", [128, 8, RD], BF16)
            wz = sb1("ml_wz", [128, 8, DI], BF16)
            r_w1 = Res("ml_w1")
            for (dst, c0, n) in ((wqa, 0, QR), (wkva, QR, KVR), (wkpe, QR + KVR, RD)):
                sv, k = load_cast(dst, W[:, c0:c0 + n].rearrange("(kc p) n -> p kc n", p=128), [128, 8, n])
                fw.op("pool", lambda e, dst=dst, sv=sv: e.tensor_copy(out=dst[:], in_=sv), r=[r_stg[k]], w=[r_w1])
            fw.op("pool", lambda e: e.tensor_copy(out=wkpes[:, :, 0:32], in_=wkpe[:, :, 32:64]), r=[r_w1], w=[r_w1])
            fw.op("pool", lambda e: e.tensor_copy(out=wkpes[:, :, 32:64], in_=wkpe[:, :, 0:32]), r=[r_w1], w=[r_w1])
            zc0 = QR + KVR + RD
            for cb in range(8):
                sv, k = load_cast(wz, W[:, zc0 + cb * 256:zc0 + (cb + 1) * 256].rearrange("(kc p) n -> p kc n", p=128), [128, 8, 256])
                fw.op("pool", lambda e, sv=sv, cb=cb: e.tensor_copy(out=wz[:, :, cb * 256:(cb + 1) * 256], in_=sv), r=[r_stg[k]], w=[r_w1])
            for cb in range(3):
                sv, k = load_cast(wqb, P.mla_w_qb[j, :, cb * 1024:(cb + 1) * 1024].rearrange("(g p) n -> p g n", p=128), [128, 2, 1024])
                fw.op("pool", lambda e, sv=sv, cb=cb: e.tensor_copy(out=wqb[:, :, cb * 1024:(cb + 1) * 1024], in_=sv), r=[r_stg[k]], w=[r_w2])
            wqbv = wqb[:].rearrange("p g (h c) -> p g h c", c=192)
            fw.op("pool", lambda e: e.tensor_copy(out=wqs[:, :, :, 0:32], in_=wqbv[:, :, :, 160:192]), r=[r_w2], w=[r_w2])
            fw.op("pool", lambda e: e.tensor_copy(out=wqs[:, :, :, 32:64], in_=wqbv[:, :, :, 128:160]), r=[r_w2], w=[r_w2])
            for cb in range(2):
                sv, k = load_cast(wkvb, P.mla_w_kvb[j, :, cb * 2048:(cb + 1) * 2048], [128, 2048])
                fw.op("pool", lambda e, sv=sv, cb=cb: e.tensor_copy(out=wkvb[:, cb * 2048:(cb + 1) * 2048], in_=sv), r=[r_stg[k]], w=[r_w2])
            fw.dma("sp", qanT[:], P.mla_qa_norm[j].rearrange("(g p) -> p g", p=128), w=[r_w2], allow_slow_non_contiguous=True)
            fw.dma("sp", kvanT[:], P.mla_kva_norm[j].rearrange("(p o) -> p o", o=1), w=[r_w2])
            hp = [sb1("ml_hp%d" % k, [128, 8, 512], BF16) for k in range(2)]
            r_hp = [Res(), Res()]
            cosp = [sb1("ml_cos%d" % k, [64, 512]) for k in range(2)]
            sinp = [sb1("ml_sin%d" % k, [64, 512]) for k in range(2)]
            r_cs = [Res(), Res()]
            sqq = sb1("ml_sqq", [128, 2, 512], BF16)
            sqk = sb1("ml_sqk", [128, 512], BF16)
            rq = sb1("ml_rq", [128, 512])
            rk = sb1("ml_rk", [128, 512])
            t1 = sb1("ml_t1", [64, 512])
            t2 = sb1("ml_t2", [64, 512])
            zst = [sb1("ml_zst%d" % k, [128, DI], BF16) for k in range(2)]
            r_sqq, r_sqk, r_rq, r_rk, r_t1, r_t2 = [Res() for _ in range(6)]
            r_zst = [Res(), Res()]
            r_zs_d = [Res() for t in range(NT)]
            P.r_zs_d = r_zs_d
            zcount = 0
            def m1_load(p):
                s0, n = PIECES[p]
                k = p % 2
                fw.dma("sp", hp[k][:, :, 0:n], hT[:, :, s0:s0 + n], r=[r_hT[t] for t in range(s0 // 128, (s0 + n) // 128)], w=[r_hp[k]])
                fw.dma("sp", cosp[k][:, 0:n], P.k_cos[:, s0:s0 + n], w=[r_cs[k]])
                fw.dma("sp", sinp[k][:, 0:n], P.k_sin[:, s0:s0 + n], w=[r_cs[k]])
            m1_load(0)
            for p in range(NP):
                s0, n = PIECES[p]
                k = p % 2
                tsl = slice(s0, s0 + n)
                if p + 1 < NP:
                    m1_load(p + 1)
                hpk = hp[k]
                for g in range(2):
                    for kc in range(8):
                        fw.op("pe", lambda e, g=g, kc=kc: e.matmul(ps[g][:, 0:n], wqa[:, kc, g * 128:(g + 1) * 128], hpk[:, kc, 0:n],
                                                                  start=(kc == 0), stop=(kc == 7)),
                              r=[r_w1, r_hp[k]], w=[psr[g]], inc=(kc == 7))
                for kc in range(8):
                    fw.op("pe", lambda e, kc=kc: e.matmul(ps[2][:, 0:n], wkva[:, kc, :], hpk[:, kc, 0:n], start=(kc == 0), stop=(kc == 7)),
                          r=[r_w1, r_hp[k]], w=[psr[2]], inc=(kc == 7))
                for (wt_, bank) in ((wkpe, 3), (wkpes, 4)):
                    for kc in range(8):
                        fw.op("pe", lambda e, kc=kc, wt_=wt_, bank=bank: e.matmul(ps[bank][0:64, 0:n], wt_[:, kc, :], hpk[:, kc, 0:n],
                                                                                start=(kc == 0), stop=(kc == 7)),
                              r=[r_w1, r_hp[k]], w=[psr[bank]], inc=(kc == 7))
                for g in range(2):
                    fw.op("act", lambda e, g=g: e.activation(out=sqq[:, g, 0:n], in_=ps[g][:, 0:n], func=AF.Square), r=[psr[g]], w=[r_sqq])
                fw.op("act", lambda e: e.activation(out=sqk[:, 0:n], in_=ps[2][:, 0:n], func=AF.Square), r=[psr[2]], w=[r_sqk])
                for g in range(2):
                    fw.op("pe", lambda e, g=g: e.matmul(ps[5][:, 0:n], P.ones_bf[:], sqq[:, g, 0:n], start=(g == 0), stop=(g == 1)),
                          r=[r_sqq, P.r_const], w=[psr[5]], inc=(g == 1))
                fw.op("pe", lambda e: e.matmul(ps[6][:, 0:n], P.ones_bf[:], sqk[:, 0:n], start=True, stop=True),
                      r=[r_sqk, P.r_const], w=[psr[6]])
                fw.op("act", lambda e: e.activation(out=rq[:, 0:n], in_=ps[5][:, 0:n], func=AF.Sqrt, scale=1.0 / QR, bias=P.epsc[:, 0:1]),
                      r=[psr[5], P.r_const], w=[r_rq])
                fw.op("act", lambda e: e.activation(out=rk[:, 0:n], in_=ps[6][:, 0:n], func=AF.Sqrt, scale=1.0 / KVR, bias=P.epsc[:, 0:1]),
                      r=[psr[6], P.r_const], w=[r_rk])
                fw.op("dve", lambda e: e.reciprocal(out=rq[:, 0:n], in_=rq[:, 0:n]), r=[r_rq], w=[r_rq])
                fw.op("dve", lambda e: e.reciprocal(out=rk[:, 0:n], in_=rk[:, 0:n]), r=[r_rk], w=[r_rk])
                for g in range(2):
                    fw.op("dve", lambda e, g=g: e.scalar_tensor_tensor(out=qn[:, g, tsl], in0=ps[g][:, 0:n], scalar=qanT[:, g:g + 1],
                                                                      in1=rq[:, 0:n], op0=ALU.mult, op1=ALU.mult),
                          r=[psr[g], r_rq, r_w2], w=[r_qn[p]])
                fw.op("dve", lambda e: e.scalar_tensor_tensor(out=kvn[:, tsl], in0=ps[2][:, 0:n], scalar=kvanT[:, 0:1],
                                                              in1=rk[:, 0:n], op0=ALU.mult, op1=ALU.mult),
                      r=[psr[2], r_rk, r_w2], w=[r_kvn[p]])
                fw.op("dve", lambda e: e.tensor_tensor(out=t1[:, 0:n], in0=ps[3][0:64, 0:n], in1=cosp[k][:, 0:n], op=ALU.mult),
                      r=[psr[3], r_cs[k]], w=[r_t1])
                fw.op("dve", lambda e: e.tensor_tensor(out=t2[:, 0:n], in0=ps[4][0:64, 0:n], in1=sinp[k][:, 0:n], op=ALU.mult),
                      r=[psr[4], r_cs[k]], w=[r_t2])
                fw.op("pool", lambda e: e.tensor_tensor(out=kpeT[0:64, tsl], in0=t1[:, 0:n], in1=t2[:, 0:n], op=ALU.add),
                      r=[r_t1, r_t2], w=[r_kpe[p]])
                for ti in range(n // 128):
                    tt = s0 // 128 + ti
                    if tt < 2 and not need_ctx:
                        continue
                    zk = zcount % 2
                    zcount += 1
                    for cb in range(4):
                        bank = 4 + cb
                        for kc in range(8):
                            fw.op("pe", lambda e, kc=kc, cb=cb, bank=bank, ti=ti: e.matmul(
                                ps[bank][:, :], hpk[:, kc, ti * 128:(ti + 1) * 128], wz[:, kc, cb * 512:(cb + 1) * 512],
                                start=(kc == 0), stop=(kc == 7)), r=[r_w1, r_hp[k]], w=[psr[bank]], inc=(kc == 7))
                        fw.op("act", lambda e, cb=cb, bank=bank, zk=zk: e.activation(
                            out=zst[zk][:, cb * 512:(cb + 1) * 512], in_=ps[bank][:, :], func=AF.Silu), r=[psr[bank]], w=[r_zst[zk]])
                    fw.dma("pool", P.zs_d[tt * 128:(tt + 1) * 128, :], zst[zk][:], r=[r_zst[zk]], w=[r_zs_d[tt]])
            fw.barrier()

        knT = [sb("ml_knT%d" % k, [128, T], BF16) for k in range(2)]
        V1 = [sb("ml_V1%d" % k, [128, NT, 132], BF16) for k in range(2)]
        qnT = [sb("ml_qnT%d" % k, [128, T], BF16) for k in range(2)]
        qpT = [sb("ml_qpT%d" % k, [128, T], BF16) for k in range(2)]
        r_knT, r_V1, r_qnT, r_qpT = [[Res(), Res()] for _ in range(4)]
        cosq = [sb("ml_cq%d" % k, [64, 512]) for k in range(2)]
        sinq = [sb("ml_sq%d" % k, [64, 512]) for k in range(2)]
        r_csq = [Res(), Res()]
        u1 = sb("ml_u1", [64, 512])
        u2 = sb("ml_u2", [64, 512])
        r_u1, r_u2 = Res(), Res()
        NPT = 3
        PT = [sb("ml_PT%d" % k, [128, 512], BF16) for k in range(NPT)]
        r_PT = [Res() for k in range(NPT)]
        zt = [sb("ml_zt%d" % k, [128, 4, 128], BF16) for k in range(3)]
        r_zt = [Res(), Res(), Res()]
        rinv = [sb("ml_ri%d" % k, [128, 4]) for k in range(2)]
        r_rinv = [Res(), Res()]
        ozt = [[sb("ml_ozt%d%d" % (a, k), [128, 128], BF16) for k in range(2)] for a in range(2)]
        r_ozt = [[Res(), Res()], [Res(), Res()]]
        ozs = [sb("ml_ozs%d" % k, [128, 512], BF16) for k in range(2)]
        r_ozs = [Res(), Res()]
        for k in range(2):
            fw.op("dve", lambda e, k=k: e.memset(V1[k][:, :, 128:132], 1.0), w=[r_V1[k]])
            fw.op("pool", lambda e, k=k: e.memset(qpT[k][64:128, :], 0.0), w=[r_qpT[k]])
        SCB = [0, 1, 2]
        OAB = [3, 4, 5, 6]
        PJ = 7
        qtiles = [(LC + i * 512, 512, list(range(NT))) for i in range(L // 512)]
        if need_ctx:
            qtiles = [(0, LC, [0, 1])] + qtiles
        tok_lo = 0 if need_ctx else LC
        csn = [0]
        n_oz = [0]
        n_it = [0]
        nheads = int(os.environ.get("ML_NH", NH))

        def project_steps(hd):
            hb = hd % 2
            b = PJ
            steps = []
            for p in range(NP):
                s0, n = PIECES[p]
                tsl = slice(s0, s0 + n)

                def st_k(s0=s0, n=n, tsl=tsl, p=p):
                    fw.op("pe", lambda e: e.matmul(ps[b][:, 0:n], wkvb[:, hd * 256:hd * 256 + 128], kvn[:, tsl], start=True, stop=True),
                          r=[r_w2, r_kvn[p]], w=[psr[b]])
                    fw.op("dve", lambda e: e.tensor_copy(out=knT[hb][:, tsl], in_=ps[b][:, 0:n]), r=[psr[b]], w=[r_knT[hb]])

                def st_v(s0=s0, n=n, tsl=tsl, p=p):
                    for ti in range(n // 128):
                        fw.op("pe", lambda e, ti=ti: e.matmul(ps[b][:, ti * 128:(ti + 1) * 128], kvn[:, s0 + ti * 128:s0 + (ti + 1) * 128],
                                                             wkvb[:, hd * 256 + 128:hd * 256 + 256], start=True, stop=True),
                              r=[r_w2, r_kvn[p]], w=[psr[b]], inc=(ti == n // 128 - 1))
                    fw.op("dve", lambda e: e.tensor_copy(out=V1[hb][:, s0 // 128:(s0 + n) // 128, 0:128],
                                                         in_=ps[b][:, 0:n].rearrange("p (a v) -> p a v", v=128)), r=[psr[b]], w=[r_V1[hb]])
                steps += [st_k, st_v]
                if s0 + n <= tok_lo:
                    continue

                def st_qn(s0=s0, n=n, tsl=tsl, p=p):
                    for g in range(2):
                        fw.op("pe", lambda e, g=g: e.matmul(ps[b][:, 0:n], wqb[:, g, hd * 192:hd * 192 + 128], qn[:, g, tsl],
                                                           start=(g == 0), stop=(g == 1)),
                              r=[r_w2, r_qn[p]], w=[psr[b]], inc=(g == 1))
                    fw.op("dve", lambda e: e.tensor_copy(out=qnT[hb][:, tsl], in_=ps[b][:, 0:n]), r=[psr[b]], w=[r_qnT[hb]])

                def st_qp1(s0=s0, n=n, tsl=tsl, p=p):
                    ck = csn[0] % 2
                    fw.dma("sp", cosq[ck][:, 0:n], P.k_cos[:, tsl], w=[r_csq[ck]])
                    fw.dma("sp", sinq[ck][:, 0:n], P.k_sin[:, tsl], w=[r_csq[ck]])
                    for g in range(2):
                        fw.op("pe", lambda e, g=g: e.matmul(ps[b][0:64, 0:n], wqb[:, g, hd * 192 + 128:hd * 192 + 192], qn[:, g, tsl],
                                                           start=(g == 0), stop=(g == 1)),
                              r=[r_w2, r_qn[p]], w=[psr[b]], inc=(g == 1))
                    fw.op("dve", lambda e, ck=ck: e.tensor_tensor(out=u1[:, 0:n], in0=ps[b][0:64, 0:n], in1=cosq[ck][:, 0:n], op=ALU.mult),
                          r=[psr[b], r_csq[ck]], w=[r_u1])

                def st_qp2(s0=s0, n=n, tsl=tsl, p=p):
                    ck = csn[0] % 2
                    csn[0] += 1
                    for g in range(2):
                        fw.op("pe", lambda e, g=g: e.matmul(ps[b][0:64, 0:n], wqs[:, g, hd, :], qn[:, g, tsl],
                                                           start=(g == 0), stop=(g == 1)),
                              r=[r_w2, r_qn[p]], w=[psr[b]], inc=(g == 1))
                    fw.op("dve", lambda e, ck=ck: e.tensor_tensor(out=u2[:, 0:n], in0=ps[b][0:64, 0:n], in1=sinq[ck][:, 0:n], op=ALU.mult),
                          r=[psr[b], r_csq[ck]], w=[r_u2])
                    fw.op("pool", lambda e: e.tensor_tensor(out=qpT[hb][0:64, tsl], in0=u1[:, 0:n], in1=u2[:, 0:n], op=ALU.add),
                          r=[r_u1, r_u2], w=[r_qpT[hb]])
                steps += [st_qn, st_qp1, st_qp2]
            return steps

        zk_of = {}

        def scores(it):
            hd, q0, nq, kt, ki, nk, idx = it
            hb = hd % 2
            if ki == 0:
                zk, zq = n_oz[0] % 2, n_oz[0] % 3
                n_oz[0] += 1
                zk_of[(hd, q0)] = (zk, zq)
                fw.dma("sp", zt[zq][:, 0:nq // 128, :],
                       P.zs_d[q0:q0 + nq, hd * 128:(hd + 1) * 128].rearrange("(a p) v -> p a v", p=128),
                       r=[P.r_zs_d[t] for t in range(q0 // 128, (q0 + nq) // 128)], w=[r_zt[zq]])
            ks = slice(kt * 128, (kt + 1) * 128)
            sb_ = SCB[idx % 3]
            pk = idx % NPT
            fw.op("pe", lambda e: e.matmul(ps[sb_][:, 0:nq], knT[hb][:, ks], qnT[hb][:, q0:q0 + nq], start=True, stop=False),
                  r=[r_knT[hb], r_qnT[hb]], w=[psr[sb_]], inc=False)
            fw.op("pe", lambda e: e.matmul(ps[sb_][:, 0:nq], kpeT[:, ks], qpT[hb][:, q0:q0 + nq], start=False, stop=True),
                  r=[r_kpe[kt // 4], r_qpT[hb]], w=[psr[sb_]])
            fw.op("act", lambda e: e.activation(out=PT[pk][:, 0:nq], in_=ps[sb_][:, 0:nq], func=AF.Exp, scale=MLA_SCALE),
                  r=[psr[sb_]], w=[r_PT[pk]])

        def pv(it, i):
            hd, q0, nq, kt, ki, nk, idx = it
            hb = hd % 2
            pk = idx % NPT
            nqs = nq // 128
            for qi in range(nqs):
                ob = OAB[qi]
                fw.op("pe", lambda e, ob=ob, qi=qi: e.matmul(
                    ps[ob][:, 0:129], PT[pk][:, qi * 128:(qi + 1) * 128], V1[hb][:, kt, 0:129],
                    start=(ki == 0), stop=(ki == nk - 1)),
                    r=[r_PT[pk], r_V1[hb]], w=[psr[ob]], inc=(ki == nk - 1 or qi == nqs - 1))
            if ki == nk - 1:
                finalize(hd, q0, nq, i)

        oev = [sb("ml_oev%d" % k, [128, 4, 132]) for k in range(2)]
        r_oev = [Res(), Res()]
        later = {}

        def finalize(hd, q0, nq, i):
            nqs = nq // 128
            zk, zq = zk_of[(hd, q0)]
            for qi in range(nqs):
                ob = OAB[qi]
                fw.op("dve", lambda e, ob=ob, qi=qi: e.tensor_copy(out=oev[zk][:, qi, 0:129], in_=ps[ob][:, 0:129]),
                      r=[psr[ob]], w=[r_oev[zk]])

            def f2():
                fw.op("dve", lambda e: e.reciprocal(out=rinv[zk][:, 0:nqs], in_=oev[zk][:, 0:nqs, 128]),
                      r=[r_oev[zk]], w=[r_rinv[zk]])

            def f3(qi):
                def run():
                    tk = qi % 2
                    fw.op("dve", lambda e: e.scalar_tensor_tensor(
                        out=ozt[zk][tk][:], in0=oev[zk][:, qi, 0:128], scalar=rinv[zk][:, qi:qi + 1], in1=zt[zq][:, qi, :],
                        op0=ALU.mult, op1=ALU.mult), r=[r_oev[zk], r_rinv[zk], r_zt[zq]], w=[r_ozt[zk][tk]])
                return run

            def f4(qi):
                def run():
                    tk = qi % 2
                    tv = ps[PJ][:, :].bitcast(BF16)[:, 0:128]
                    fw.op("pe", lambda e: e.transpose(out=tv, in_=ozt[zk][tk][:], identity=P.ident[:]),
                          r=[r_ozt[zk][tk], P.r_const], w=[psr[PJ]])
                    fw.op("dve", lambda e: e.tensor_copy(out=ozs[zk][:, qi * 128:(qi + 1) * 128], in_=tv),
                          r=[psr[PJ]], w=[r_ozs[zk]])
                    if qi == nqs - 1:
                        fw.dma("pool", P.ozT[hd, :, q0:q0 + nq], ozs[zk][:, 0:nq], r=[r_ozs[zk]],
                               w=[P.r_ozT[hd][pp] for pp in range(q0 // 512, (q0 + nq - 1) // 512 + 1)])
                return run
            later.setdefault(i + 1, []).append(f2)
            for qi in range(nqs):
                later.setdefault(i + 2 + 2 * qi, []).append(f3(qi))
                later.setdefault(i + 4 + 2 * qi, []).append(f4(qi))

        its = []
        for hd in range(nheads):
            for (q0, nq, ktiles) in qtiles:
                for ki, kt in enumerate(ktiles):
                    its.append((hd, q0, nq, kt, ki, len(ktiles), len(its)))
        DEPTHP = 2
        for st_ in project_steps(0):
            st_()
        nxt_steps = []
        per_head = len(its) // nheads
        total = len(its) + DEPTHP + 16
        for i in range(total):
            if i < len(its):
                hd_i = its[i][0]
                if i % per_head == 0 and hd_i + 1 < nheads:
                    nxt_steps = project_steps(hd_i + 1)
                scores(its[i])
            if 0 <= i - DEPTHP < len(its):
                pv(its[i - DEPTHP], i)
            for fn in later.pop(i, []):
                fn()
            if nxt_steps and i % per_head >= 8 and i % 2 == 0:
                nxt_steps.pop(0)()
        assert not later and not nxt_steps
        fw.barrier()


def _consts():
    ident = np.eye(128, dtype=np.float32)
    s = np.arange(128)[:, None]
    t = np.arange(128)[None, :]
    same = (s // CH) == (t // CH)
    maskf = (same & (s <= t)).astype(np.float32)
    maskb = (same & (s >= t)).astype(np.float32)
    GRID_W = 64
    rows = L // GRID_W
    row = np.repeat(np.arange(rows, dtype=np.float32), GRID_W)
    col = np.tile(np.arange(GRID_W, dtype=np.float32), rows)
    inv = (10000.0 ** (-np.arange(0, 32, 2, dtype=np.float32) / 32)).astype(np.float32)
    ang = np.concatenate([row[:, None] * inv, col[:, None] * inv], axis=-1)
    cos = np.cos(ang).astype(np.float32).T
    sin = np.sin(ang).astype(np.float32).T
    kcos = np.ones((64, T), np.float32)
    ksin = np.zeros((64, T), np.float32)
    kcos[0:32, LC:] = cos
    kcos[32:64, LC:] = cos
    ksin[0:32, LC:] = -sin
    ksin[32:64, LC:] = sin
    cm3 = np.zeros((128, CPB, 128), np.float32)
    for c in range(CPB):
        cm3[c * CH:(c + 1) * CH, c, :] = 1.0
    return {"k_cm3": cm3.reshape(128, CPB * 128), "k_ident": ident, "k_maskf": maskf, "k_maskb": maskb, "k_cos": kcos, "k_sin": ksin}


_PROG_CACHE = {}


def _get_prog(n_layers=DEPTH, dbg=False):
    key = (n_layers, dbg)
    if key not in _PROG_CACHE:
        _PROG_CACHE[key] = Prog(n_layers, dbg)
    return _PROG_CACHE[key]


def _in_maps(inputs):
    f = lambda a: np.ascontiguousarray(np.asarray(a, dtype=np.float32))
    shared = {k: f(inputs[k]) for k in ("c_ctx", "ada_w", "ada_b", "norm_pre", "norm_post", "hg_w_in",
                                         "hg_lb_logits", "hg_o_norm", "hg_w_out", "mla_w_in", "mla_qa_norm",
                                         "mla_w_qb", "mla_kva_norm", "mla_w_kvb", "mla_w_out")}
    shared.update(_consts())
    x = f(inputs["x"])
    c = f(inputs["c"])
    ctx = f(inputs["ctx"])
    maps = []
    for b in range(8):
        m = dict(shared)
        m["x"] = x[b]
        m["c"] = c[b]
        m["ctx"] = ctx[b]
        maps.append(m)
    return maps


def kernel(**inputs):
    prog = _get_prog()
    res = run_bass_kernel_spmd(prog.nc, _in_maps(inputs), core_ids=list(range(8)))
    return np.stack([np.asarray(r["out"], dtype=np.float32) for r in res.results], axis=0)
```
